# Optimizing a Trainium2 kernel written in Bass

```python
import math
import jax
import jax.numpy as jnp
from jax import lax
import numpy as np

D_MODEL = 2048
BATCH = 8
SEQ = 4096
DEPTH = 2

CTX_LEN = 256
GRID_W = 64
Q_BLOCK = 128
ROPE_THETA = 10000.0
NORM_EPS = 1e-6
N_MOD = 9
D_FF = 5632
BRANCH_W = 1024
N_BRANCH = 3

MLA_HEADS = 8
MLA_Q_LORA = 1536
MLA_KV_LORA = 512
MLA_NOPE = 128
MLA_ROPE = 64
MLA_V = 128
MLA_SCALE = (MLA_NOPE + MLA_ROPE) ** -0.5

S5_W = 1024
S5_H = 16
S5_G = S5_W // S5_H
S5_P = 64
S5_DT_MIN = 1e-3
S5_DT_MAX = 1e-1
S5_LAMBDA_RE_MAX = -1e-4

GQA_Q_HEADS = 8
GQA_KV_HEADS = 2
GQA_HEAD_DIM = 128
GQA_SCALE = GQA_HEAD_DIM ** -0.5

IN_WIDTHS = (MLA_Q_LORA, MLA_KV_LORA, MLA_ROPE, S5_W,
             GQA_Q_HEADS * GQA_HEAD_DIM, GQA_KV_HEADS * GQA_HEAD_DIM, GQA_KV_HEADS * GQA_HEAD_DIM)
D_IN = sum(IN_WIDTHS)
IN_OFFSETS = tuple(int(v) for v in np.cumsum(IN_WIDTHS)[:-1])

kernel_name = 'hybrid_mla_s5_gqa_macaron_dit'


def rms_norm(x, g):
    xf = x.astype(jnp.float32)
    y = xf * lax.rsqrt(jnp.mean(xf * xf, axis=-1, keepdims=True) + NORM_EPS)
    return (y * g.astype(jnp.float32)).astype(x.dtype)


def modulate(h, shift, scale):
    return h * (1 + scale) + shift


def swiglu(h, w_up, w_down):
    gate, up = jnp.split(h @ w_up, 2, axis=-1)
    return (jax.nn.silu(gate) * up) @ w_down


def grid_positions(n_tokens):
    rows = n_tokens // GRID_W
    r, col = jnp.meshgrid(jnp.arange(rows, dtype=jnp.int32), jnp.arange(GRID_W, dtype=jnp.int32), indexing='ij')
    return r.reshape(-1), col.reshape(-1)


def axial_rope_tables(rows, cols, d_rot):
    half = d_rot // 2
    freqs = ROPE_THETA ** (-jnp.arange(0, half, 2, dtype=jnp.float32) / half)

    def table(pos):
        ang = pos.astype(jnp.float32)[:, None] * freqs[None, :]
        ang = jnp.concatenate([ang, ang], axis=-1)
        return jnp.cos(ang), jnp.sin(ang)

    cr, sr = table(rows)
    cc, sc = table(cols)
    return jnp.concatenate([cr, cc], axis=-1), jnp.concatenate([sr, sc], axis=-1)


def rotate_half(v):
    v1, v2 = jnp.split(v, 2, axis=-1)
    return jnp.concatenate([-v2, v1], axis=-1)


def apply_axial_rope(x, rope):
    cos, sin = rope
    half = x.shape[-1] // 2
    xf = x.astype(jnp.float32)
    rotated = jnp.concatenate([rotate_half(xf[..., :half]), rotate_half(xf[..., half:])], axis=-1)
    return (xf * cos[:, None, :] + rotated * sin[:, None, :]).astype(x.dtype)


def blocked_attention(q, k, v, scale):
    b, lq, hq, dk = q.shape
    hkv, dv = k.shape[2], v.shape[-1]
    grp = hq // hkv
    nb = lq // Q_BLOCK
    qb = jnp.moveaxis(q.reshape(b, nb, Q_BLOCK, hkv, grp, dk), 1, 0)

    def one_block(q_blk):
        s = jnp.einsum('bqkgd,bskd->bkgqs', q_blk, k).astype(jnp.float32) * scale
        pr = jax.nn.softmax(s, axis=-1).astype(v.dtype)
        return jnp.einsum('bkgqs,bskd->bqkgd', pr, v)

    o = lax.map(one_block, qb)
    return jnp.moveaxis(o, 0, 1).reshape(b, lq, hq * dv)


def mla_q(cq, p, rope):
    b, l = cq.shape[:2]
    q = (rms_norm(cq, p['g_cq']) @ p['w_uq']).reshape(b, l, MLA_HEADS, MLA_NOPE + MLA_ROPE)
    q_nope, q_rope = q[..., :MLA_NOPE], q[..., MLA_NOPE:]
    if rope is not None:
        q_rope = apply_axial_rope(q_rope, rope)
    return jnp.concatenate([q_nope, q_rope], axis=-1)


def mla_kv(ckv, kr, p, rope):
    b, l = ckv.shape[:2]
    kv = (rms_norm(ckv, p['g_ckv']) @ p['w_ukv']).reshape(b, l, MLA_HEADS, MLA_NOPE + MLA_V)
    k_nope, v = kv[..., :MLA_NOPE], kv[..., MLA_NOPE:]
    k_rope = kr[:, :, None, :]
    if rope is not None:
        k_rope = apply_axial_rope(k_rope, rope)
    k_rope = jnp.broadcast_to(k_rope, (b, l, MLA_HEADS, MLA_ROPE))
    return jnp.concatenate([k_nope, k_rope], axis=-1), v


def gqa_q(gq, p, rope):
    b, l = gq.shape[:2]
    q = rms_norm(gq.reshape(b, l, GQA_Q_HEADS, GQA_HEAD_DIM), p['g_q'])
    return apply_axial_rope(q, rope) if rope is not None else q


def gqa_kv(gk, gv, p, rope):
    b, l = gk.shape[:2]
    k = rms_norm(gk.reshape(b, l, GQA_KV_HEADS, GQA_HEAD_DIM), p['g_k'])
    if rope is not None:
        k = apply_axial_rope(k, rope)
    return k, gv.reshape(b, l, GQA_KV_HEADS, GQA_HEAD_DIM)


def s5_discretize(lam_re, lam_im, log_dt, b_re, b_im):
    lr = jnp.minimum(lam_re.astype(jnp.float32), S5_LAMBDA_RE_MAX)
    li = lam_im.astype(jnp.float32)
    dt = jnp.exp(log_dt.astype(jnp.float32))[:, None]
    mag = jnp.exp(lr * dt)
    a_r, a_i = mag * jnp.cos(li * dt), mag * jnp.sin(li * dt)
    den = lr * lr + li * li
    n_r, n_i = a_r - 1.0, a_i
    f_r = (n_r * lr + n_i * li) / den
    f_i = (n_i * lr - n_r * li) / den
    br, bi = b_re.astype(jnp.float32), b_im.astype(jnp.float32)
    bb_r = f_r[..., None] * br - f_i[..., None] * bi
    bb_i = f_r[..., None] * bi + f_i[..., None] * br
    return a_r, a_i, bb_r, bb_i


def complex_affine_combine(e1, e2):
    a1r, a1i, b1r, b1i = e1
    a2r, a2i, b2r, b2i = e2
    return (a2r * a1r - a2i * a1i,
            a2r * a1i + a2i * a1r,
            a2r * b1r - a2i * b1i + b2r,
            a2r * b1i + a2i * b1r + b2i)


def s5_scan(u, disc, h0, reverse):
    a_r, a_i, bb_r, bb_i = disc
    bu_r = jnp.einsum('blgh,gph->blgp', u, bb_r)
    bu_i = jnp.einsum('blgh,gph->blgp', u, bb_i)
    shape = (1,) + bu_r.shape[1:]
    cum_r, cum_i, s_r, s_i = lax.associative_scan(
        complex_affine_combine,
        (jnp.broadcast_to(a_r, shape), jnp.broadcast_to(a_i, shape), bu_r, bu_i),
        reverse=reverse, axis=1)
    if h0 is not None:
        h_r, h_i = h0[0][:, None], h0[1][:, None]
        s_r, s_i = s_r + cum_r * h_r - cum_i * h_i, s_i + cum_r * h_i + cum_i * h_r
    return s_r, s_i


def s5_readout(s_r, s_i, c_re, c_im):
    return jnp.einsum('blgp,ghp->blgh', s_r, c_re) - jnp.einsum('blgp,ghp->blgh', s_i, c_im)


def s5_glu(y, w, bias):
    return y * jax.nn.sigmoid(jax.nn.gelu(y) @ w.astype(jnp.float32) + bias.astype(jnp.float32))


def s5_mixer(u_lat, u_ctx, p, ctx_out):
    b, l = u_lat.shape[:2]
    lc = u_ctx.shape[1]
    ul = u_lat.astype(jnp.float32).reshape(b, l, S5_G, S5_H)
    uc = u_ctx.astype(jnp.float32).reshape(b, lc, S5_G, S5_H)
    d_skip = p['s5_d'].astype(jnp.float32)
    y_lat = ul * d_skip
    y_ctx = uc * d_skip if ctx_out else None
    for direction in range(2):
        reverse = direction == 1
        disc = s5_discretize(p['lam_re'][direction], p['lam_im'][direction], p['log_dt'][direction],
                             p['b_re'][direction], p['b_im'][direction])
        c_re = p['c_re'][direction].astype(jnp.float32)
        c_im = p['c_im'][direction].astype(jnp.float32)
        sc_r, sc_i = s5_scan(uc, disc, None, reverse)
        edge = 0 if reverse else -1
        sl_r, sl_i = s5_scan(ul, disc, (sc_r[:, edge], sc_i[:, edge]), reverse)
        y_lat = y_lat + s5_readout(sl_r, sl_i, c_re, c_im)
        if ctx_out:
            y_ctx = y_ctx + s5_readout(sc_r, sc_i, c_re, c_im)
    out_lat = s5_glu(y_lat.reshape(b, l, S5_W), p['w_glu'], p['b_glu']).astype(u_lat.dtype)
    if not ctx_out:
        return out_lat, None
    out_ctx = s5_glu(y_ctx.reshape(b, lc, S5_W), p['w_glu'], p['b_glu']).astype(u_ctx.dtype)
    return out_lat, out_ctx


def gated_merge(h, branches, p):
    y = None
    for n, br in enumerate(branches):
        gate = jax.nn.sigmoid(h @ p['w_gate'][n] + p['b_gate'][n])
        term = gate * (br @ p['w_branch'][n])
        y = term if y is None else y + term
    return y @ p['w_o']


def token_mixer(h_lat, h_ctx, p, rope_mla, rope_gqa, ctx_out):
    z_lat = h_lat @ p['w_in']
    z_ctx = h_ctx @ p['w_in']
    cq_l, ckv_l, kr_l, u_l, gq_l, gk_l, gv_l = jnp.split(z_lat, IN_OFFSETS, axis=-1)
    cq_c, ckv_c, kr_c, u_c, gq_c, gk_c, gv_c = jnp.split(z_ctx, IN_OFFSETS, axis=-1)

    mk_c, mv_c = mla_kv(ckv_c, kr_c, p, None)
    mk_l, mv_l = mla_kv(ckv_l, kr_l, p, rope_mla)
    mla_l = blocked_attention(mla_q(cq_l, p, rope_mla), jnp.concatenate([mk_c, mk_l], axis=1),
                              jnp.concatenate([mv_c, mv_l], axis=1), MLA_SCALE)

    s5_l, s5_c = s5_mixer(u_l, u_c, p, ctx_out)

    gk_cc, gv_cc = gqa_kv(gk_c, gv_c, p, None)
    gk_ll, gv_ll = gqa_kv(gk_l, gv_l, p, rope_gqa)
    gqa_l = blocked_attention(gqa_q(gq_l, p, rope_gqa), jnp.concatenate([gk_cc, gk_ll], axis=1),
                              jnp.concatenate([gv_cc, gv_ll], axis=1), GQA_SCALE)

    out_lat = gated_merge(h_lat, (mla_l, s5_l, gqa_l), p)
    if not ctx_out:
        return out_lat, None
    mla_c = blocked_attention(mla_q(cq_c, p, None), mk_c, mv_c, MLA_SCALE)
    gqa_c = blocked_attention(gqa_q(gq_c, p, None), gk_cc, gv_cc, GQA_SCALE)
    out_ctx = gated_merge(h_ctx, (mla_c, s5_c, gqa_c), p)
    return out_lat, out_ctx


def hybrid_layer(x_lat, x_ctx, ml, mc, p, rope_mla, rope_gqa, last):
    g = p['norm_g']
    up, down = p['ffn_up'], p['ffn_down']
    x_lat = x_lat + 0.5 * ml[2] * swiglu(modulate(rms_norm(x_lat, g[0]), ml[0], ml[1]), up[0], down[0])
    x_ctx = x_ctx + 0.5 * mc[2] * swiglu(modulate(rms_norm(x_ctx, g[0]), mc[0], mc[1]), up[0], down[0])
    h_lat = modulate(rms_norm(x_lat, g[1]), ml[3], ml[4])
    h_ctx = modulate(rms_norm(x_ctx, g[1]), mc[3], mc[4])
    out_lat, out_ctx = token_mixer(h_lat, h_ctx, p, rope_mla, rope_gqa, not last)
    x_lat = x_lat + ml[5] * out_lat
    x_lat = x_lat + 0.5 * ml[8] * swiglu(modulate(rms_norm(x_lat, g[2]), ml[6], ml[7]), up[1], down[1])
    if last:
        return x_lat, None
    x_ctx = x_ctx + mc[5] * out_ctx
    x_ctx = x_ctx + 0.5 * mc[8] * swiglu(modulate(rms_norm(x_ctx, g[2]), mc[6], mc[7]), up[1], down[1])
    return x_lat, x_ctx


def setup_inputs(seed: int = 0) -> dict:
    key = jax.random.key(seed)
    ks = iter(jax.random.split(key, 40))
    f32 = jnp.float32
    D = D_MODEL

    def normal(shape, scale):
        return jax.random.normal(next(ks), shape, f32) * scale

    def gain(shape):
        return 1.0 + normal(shape, 0.02)

    return {
        'x': normal((BATCH, SEQ, D), 1.0),
        'c': normal((BATCH, D), 1.0),
        'ctx': normal((BATCH, CTX_LEN, D), 1.0),
        'c_ctx': normal((D,), 1.0),
        'w_mod': normal((DEPTH, D, N_MOD * D), 0.5 * D ** -0.5),
        'b_mod': normal((DEPTH, N_MOD * D), 0.02),
        'norm_g': gain((DEPTH, 3, D)),
        'w_ffn_up': normal((DEPTH, 2, D, 2 * D_FF), D ** -0.5),
        'w_ffn_down': normal((DEPTH, 2, D_FF, D), D_FF ** -0.5),
        'w_in': normal((DEPTH, D, D_IN), D ** -0.5),
        'mla_g_cq': gain((DEPTH, MLA_Q_LORA)),
        'mla_g_ckv': gain((DEPTH, MLA_KV_LORA)),
        'mla_w_uq': normal((DEPTH, MLA_Q_LORA, MLA_HEADS * (MLA_NOPE + MLA_ROPE)), MLA_Q_LORA ** -0.5),
        'mla_w_ukv': normal((DEPTH, MLA_KV_LORA, MLA_HEADS * (MLA_NOPE + MLA_V)), MLA_KV_LORA ** -0.5),
        'gqa_g_q': gain((DEPTH, GQA_HEAD_DIM)),
        'gqa_g_k': gain((DEPTH, GQA_HEAD_DIM)),
        's5_lam_re': -0.5 + normal((DEPTH, 2, S5_G, S5_P), 0.01),
        's5_lam_im': math.pi * jnp.arange(S5_P, dtype=f32) + normal((DEPTH, 2, S5_G, S5_P), 0.01),
        's5_log_dt': jax.random.uniform(next(ks), (DEPTH, 2, S5_G), f32,
                                        minval=math.log(S5_DT_MIN), maxval=math.log(S5_DT_MAX)),
        's5_b_re': normal((DEPTH, 2, S5_G, S5_P, S5_H), (2 * S5_H) ** -0.5),
        's5_b_im': normal((DEPTH, 2, S5_G, S5_P, S5_H), (2 * S5_H) ** -0.5),
        's5_c_re': normal((DEPTH, 2, S5_G, S5_H, S5_P), S5_P ** -0.5),
        's5_c_im': normal((DEPTH, 2, S5_G, S5_H, S5_P), S5_P ** -0.5),
        's5_d': normal((DEPTH, S5_G, S5_H), 1.0),
        's5_w_glu': normal((DEPTH, S5_W, S5_W), S5_W ** -0.5),
        's5_b_glu': normal((DEPTH, S5_W), 0.02),
        'w_gate': normal((DEPTH, N_BRANCH, D, D), D ** -0.5),
        'b_gate': normal((DEPTH, N_BRANCH, D), 0.02),
        'w_branch': normal((DEPTH, N_BRANCH, BRANCH_W, D), BRANCH_W ** -0.5),
        'w_o': normal((DEPTH, D, D), D ** -0.5),
        'final_g': gain((D,)),
    }


def reference(x, c, ctx, c_ctx, w_mod, b_mod, norm_g, w_ffn_up, w_ffn_down, w_in,
              mla_g_cq, mla_g_ckv, mla_w_uq, mla_w_ukv, gqa_g_q, gqa_g_k,
              s5_lam_re, s5_lam_im, s5_log_dt, s5_b_re, s5_b_im, s5_c_re, s5_c_im, s5_d,
              s5_w_glu, s5_b_glu, w_gate, b_gate, w_branch, w_o, final_g):
    b, l, d = x.shape
    rows, cols = grid_positions(l)
    rope_mla = axial_rope_tables(rows, cols, MLA_ROPE)
    rope_gqa = axial_rope_tables(rows, cols, GQA_HEAD_DIM)
    x_lat, x_ctx = x, ctx
    for li in range(DEPTH):
        m_lat = (jax.nn.silu(c) @ w_mod[li] + b_mod[li]).reshape(b, N_MOD, d)
        m_ctx = (jax.nn.silu(c_ctx) @ w_mod[li] + b_mod[li]).reshape(N_MOD, d)
        ml = [m_lat[:, i, None, :] for i in range(N_MOD)]
        mc = [m_ctx[i][None, None, :] for i in range(N_MOD)]
        p = {
            'norm_g': norm_g[li], 'ffn_up': w_ffn_up[li], 'ffn_down': w_ffn_down[li], 'w_in': w_in[li],
            'g_cq': mla_g_cq[li], 'g_ckv': mla_g_ckv[li], 'w_uq': mla_w_uq[li], 'w_ukv': mla_w_ukv[li],
            'g_q': gqa_g_q[li], 'g_k': gqa_g_k[li],
            'lam_re': s5_lam_re[li], 'lam_im': s5_lam_im[li], 'log_dt': s5_log_dt[li],
            'b_re': s5_b_re[li], 'b_im': s5_b_im[li], 'c_re': s5_c_re[li], 'c_im': s5_c_im[li],
            's5_d': s5_d[li], 'w_glu': s5_w_glu[li], 'b_glu': s5_b_glu[li],
            'w_gate': w_gate[li], 'b_gate': b_gate[li], 'w_branch': w_branch[li], 'w_o': w_o[li],
        }
        x_lat, x_ctx = hybrid_layer(x_lat, x_ctx, ml, mc, p, rope_mla, rope_gqa, li == DEPTH - 1)
    return rms_norm(x_lat, final_g)
```

```python
import math
import contextlib
import numpy as np
import concourse.bass as bass
import concourse.mybir as mybir
from concourse.bass_utils import run_bass_kernel_spmd

F32 = mybir.dt.float32
BF16 = mybir.dt.bfloat16
AF = mybir.ActivationFunctionType
ALU = mybir.AluOpType

FULL = dict(D=2048, L=4096, LC=256, DFF=5632, HM=8, QL=1536, KVL=512, G=64, HQ=8, HKV=2, GW=64, DEPTH=2)
EPS = 1e-6
NCORES = 8


class Buf:
    __slots__ = ("w", "r", "excl")

    def __init__(self, excl=False):
        self.w = None
        self.r = {}
        self.excl = excl


def _split(reads, writes):
    ex = [b for b in reads if b.excl]
    if not ex:
        return reads, writes
    return [b for b in reads if not b.excl], list(writes) + ex


class Eng:
    def __init__(self, name, handle, sem, dsems):
        self.name = name
        self.h = handle
        self.sem = sem
        self.count = 0
        self.seen = {}
        self.dsems = dsems
        self.dvals = [0] * len(dsems)
        self.di = 0


class KB:
    def __init__(self, nc, es):
        self.nc = nc
        self.es = es
        self.engs = {}
        for name, h, nd in (("pe", nc.tensor, 0), ("act", nc.scalar, 8), ("dve", nc.vector, 0),
                            ("pool", nc.gpsimd, 24), ("sp", nc.sync, 24)):
            sem = es.enter_context(nc.semaphore("s_" + name))
            ds = [es.enter_context(nc.semaphore(f"d_{name}{i}")) for i in range(nd)]
            self.engs[name] = Eng(name, h, sem, ds)
        self.uid = 0

    def _waits(self, eng, reads, writes, same_sync):
        need = {}

        def add(tok):
            s, v = tok
            if (s is eng.sem) and not same_sync:
                return
            k = id(s)
            if k not in need or need[k][1] < v:
                need[k] = (s, v)
        for b in reads:
            if b.w is not None:
                add(b.w)
        for b in writes:
            if b.w is not None:
                add(b.w)
            for tok in b.r.values():
                add(tok)
        for k, (s, v) in need.items():
            if eng.seen.get(k, 0) < v:
                eng.h.wait_ge(s, v)
                eng.seen[k] = v

    def _mark(self, tok, reads, writes):
        for b in writes:
            b.w = tok
            b.r = {}
        k = id(tok[0])
        for b in reads:
            b.r[k] = tok

    def op(self, en, fn, reads=(), writes=()):
        reads, writes = _split(reads, writes)
        eng = self.engs[en]
        self._waits(eng, reads, writes, same_sync=(en != "pe"))
        ins = fn(eng.h)
        eng.count += 1
        ins.then_inc(eng.sem, 1)
        self._mark((eng.sem, eng.count), reads, writes)

    def mmg(self, out_ap, pairs, reads, writes, start=True, stop=True):
        reads, writes = _split(reads, writes)
        eng = self.engs["pe"]
        self._waits(eng, reads, writes, same_sync=False)
        n = len(pairs)
        ins = None
        for i, (l, r) in enumerate(pairs):
            ins = eng.h.matmul(out_ap, l, r, start=(start and i == 0), stop=(stop and i == n - 1))
        eng.count += 1
        ins.then_inc(eng.sem, 1)
        self._mark((eng.sem, eng.count), reads, writes)

    def dma(self, qn, out_ap, in_ap, reads=(), writes=(), slow=False):
        eng = self.engs[qn]
        i = eng.di
        eng.di = (i + 1) % len(eng.dsems)
        s = eng.dsems[i]
        pv = eng.dvals[i]
        if pv > 0 and eng.seen.get(id(s), 0) < pv:
            eng.h.wait_ge(s, pv)
            eng.seen[id(s)] = pv
        self._waits(eng, reads, writes, same_sync=True)
        if slow:
            ins = eng.h.dma_start(out=out_ap, in_=in_ap, allow_slow_non_contiguous=True)
        else:
            ins = eng.h.dma_start(out=out_ap, in_=in_ap)
        ins.then_inc(s, 16)
        eng.dvals[i] = pv + 16
        self._mark((s, pv + 16), reads, writes)

    def barrier(self):
        toks = []
        for e in self.engs.values():
            if e.count:
                toks.append((e.sem, e.count))
            for s, v in zip(e.dsems, e.dvals):
                if v:
                    toks.append((s, v))
        for e in self.engs.values():
            for s, v in toks:
                if e.seen.get(id(s), 0) < v:
                    e.h.wait_ge(s, v)
                    e.seen[id(s)] = v

    def tile(self, st, shape, dt, name=None):
        self.uid += 1
        t = st.enter_context(self.nc.sbuf_tensor(f"{name or 't'}_{self.uid}", list(shape), dt))
        return t, Buf()


def ceil_div(a, b):
    return (a + b - 1) // b


def build_program(cfg):
    D, L, LC, DFF = cfg["D"], cfg["L"], cfg["LC"], cfg["DFF"]
    HM, QL, KVL, G, HQ, HKV, GW, DEPTH = (cfg[k] for k in ("HM", "QL", "KVL", "G", "HQ", "HKV", "GW", "DEPTH"))
    T = L + LC
    KC = D // 128
    FC = DFF // 128
    SW = G * 16
    BW = SW
    assert HM * 128 == BW and HQ * 128 == BW
    DIN = QL + KVL + 64 + SW + HQ * 128 + 2 * HKV * 128
    OFF_CQ, OFF_CKV, OFF_KR = 0, QL, QL + KVL
    OFF_U = OFF_KR + 64
    OFF_GQ = OFF_U + SW
    OFF_GK = OFF_GQ + HQ * 128
    OFF_GV = OFF_GK + HKV * 128
    NC8 = T // 8
    NMODC = 9 * KC

    nc = bass.Bass("TRN2", target_bir_lowering=False)

    def din(name, shape, dt=F32):
        return nc.dram_tensor(name, list(shape), dt, kind="ExternalInput").ap()

    DBG = cfg.get("dbg", ())
    STOP = cfg.get("stop", 10 ** 9)
    unit = [0]

    def skip():
        unit[0] += 1
        return unit[0] > STOP or unit[0] < cfg.get('start', 0)

    def dscr(name, shape, dt):
        if name in DBG:
            return nc.dram_tensor(name, list(shape), dt, kind="ExternalOutput").ap()
        return nc.dram_tensor(name, list(shape), dt).ap()

    x_in = din("x", [L, D])
    c_in = din("c", [1, D])
    ctx_in = din("ctx", [LC, D])
    cctx_in = din("c_ctx", [1, D])
    w_mod = din("w_mod", [DEPTH, D, 9 * D])
    b_mod = din("b_mod", [DEPTH, 9 * D])
    norm_g = din("norm_g", [DEPTH, 3, D])
    w_up = din("w_ffn_up", [DEPTH, 2, D, 2 * DFF])
    w_down = din("w_ffn_down", [DEPTH, 2, DFF, D])
    w_in = din("w_in", [DEPTH, D, DIN])
    g_cq = din("mla_g_cq", [DEPTH, QL])
    g_ckv = din("mla_g_ckv", [DEPTH, KVL])
    w_uq = din("mla_w_uq", [DEPTH, QL, HM * 192])
    w_ukv = din("mla_w_ukv", [DEPTH, KVL, HM * 256])
    g_q = din("gqa_g_q", [DEPTH, 128])
    g_k = din("gqa_g_k", [DEPTH, 128])
    lam_re = din("s5_lam_re", [DEPTH, 2, G, 64])
    lam_im = din("s5_lam_im", [DEPTH, 2, G, 64])
    log_dt = din("s5_log_dt", [DEPTH, 2, G])
    b_re = din("s5_b_re", [DEPTH, 2, G, 64, 16])
    b_im = din("s5_b_im", [DEPTH, 2, G, 64, 16])
    c_re = din("s5_c_re", [DEPTH, 2, G, 16, 64])
    c_im = din("s5_c_im", [DEPTH, 2, G, 16, 64])
    s5_d = din("s5_d", [DEPTH, G, 16])
    w_glu = din("s5_w_glu", [DEPTH, SW, SW])
    b_glu = din("s5_b_glu", [DEPTH, SW])
    w_gate = din("w_gate", [DEPTH, 3, D, D])
    b_gate = din("b_gate", [DEPTH, 3, D])
    w_branch = din("w_branch", [DEPTH, 3, BW, D])
    w_o = din("w_o", [DEPTH, D, D])
    final_g = din("final_g", [1, D])
    k_cos64 = din("k_cos64", [64, T])
    k_sin64 = din("k_sin64", [64, T])
    k_cos128 = din("k_cos128", [128, T])
    k_sin128 = din("k_sin128", [128, T])
    k_mats = din("k_mats", [7, 128, 128])
    k_jf = din("k_jf", [128, 16])
    y_out = nc.dram_tensor("y", [L, D], F32, kind="ExternalOutput").ap()

    xT = dscr("xT", [KC, 128, T], F32)
    hT = dscr("hT", [KC, 128, T], BF16)
    actT = dscr("actT", [FC, 128, T], BF16)
    scT = dscr("scT", [KC, 128, 2], BF16)
    cqT = dscr("cqT", [QL // 128, 128, T], F32)
    cqnT = dscr("cqnT", [QL // 128, 128, T], BF16)
    ckvT = dscr("ckvT", [KVL // 128, 128, T], F32)
    ckvnT = dscr("ckvnT", [KVL // 128, 128, T], BF16)
    qmnT = dscr("qmnT", [HM, 128, T], BF16)
    qmrT = dscr("qmrT", [HM, 64, T], BF16)
    kmT = dscr("kmT", [HM, 128, T], BF16)
    krT = dscr("krT", [1, 64, T], BF16)
    vmD = dscr("vmD", [T, HM * 128], BF16)
    u8D = dscr("u8D", [G, 128, NC8], BF16)
    uTD = dscr("uTD", [G // 8, 128, T], F32)
    gqT = dscr("gqT", [HQ, 128, T], BF16)
    gkT = dscr("gkT", [HKV, 128, T], BF16)
    gvD = dscr("gvD", [T, HKV * 128], BF16)
    y8D = dscr("y8D", [2, G, 128, NC8], F32)
    brT = [dscr(f"brT{n}", [BW // 128, 128, T], BF16) for n in range(3)]
    ygT = dscr("ygT", [KC, 128, T], BF16)
    geluT = dscr("geluT", [SW // 128, 128, T], BF16)
    S5RL = 64
    S5NR = (T // 8 + S5RL - 1) // S5RL + 2
    vvD = dscr("vvD", [2, S5NR, 128, 2 * G * S5RL], F32)
    saD = dscr("saD", [2, S5NR, 128, G * S5RL], BF16)
    wmD = dscr("wmD", [2, 2, 128, G * 128], BF16)
    ytotT = dscr("ytotT", [SW // 128, 128, T], F32)

    es = contextlib.ExitStack()
    kb = KB(nc, es)
    op, mmg, dma, barrier = kb.op, kb.mmg, kb.dma, kb.barrier

    PS = []
    for i in range(8):
        t = es.enter_context(nc.psum_tensor(f"ps{i}", [128, 512], F32))
        PS.append((t, Buf(excl=True)))
    ps_rr = [0]

    def next_ps():
        i = ps_rr[0]
        ps_rr[0] = (i + 1) % 8
        return PS[i]

    mats, mats_b = kb.tile(es, [128, 7, 128], F32, "mats")
    onesb, onesb_b = kb.tile(es, [128, 128], BF16, "onesb")
    matsb, matsb_b = kb.tile(es, [128, 2, 128], BF16, "matsb")
    modT, modT_b = kb.tile(es, [128, NMODC, 2], F32, "modT")
    bmodT, bmodT_b = kb.tile(es, [128, NMODC], F32, "bmodT")
    ngT, ngT_b = kb.tile(es, [128, 3, KC], F32, "ngT")
    gains, gains_b = kb.tile(es, [128, 3, 2, KC], F32, "gains")
    shifts, shifts_b = kb.tile(es, [128, 3, 2, KC], F32, "shifts")
    rgate, rgate_b = kb.tile(es, [128, 3, 2, KC], F32, "rgate")
    smallv, smallv_b = kb.tile(es, [128, 8], F32, "smallv")
    epsT, epsT_b = kb.tile(es, [128, 1], F32, "epsT")
    scS, scS_b = kb.tile(es, [128, KC, 2], BF16, "scS")
    s5tab, s5tab_b = kb.tile(es, [128, 2, 2, 2, G], F32, "s5tab")
    IDENT, R128, R64, SWAPN, MASKF, MASKR, ONESF = range(7)

    dma("sp", mats[:], k_mats.rearrange("m p q -> p m q"), writes=[mats_b])
    op("dve", lambda e: e.memset(onesb[:], 1.0), writes=[onesb_b])
    op("dve", lambda e: e.memset(epsT[:], EPS), writes=[epsT_b])
    op("dve", lambda e: e.tensor_copy(out=matsb[:], in_=mats[:, 1:3, :]), reads=[mats_b], writes=[matsb_b])
    barrier()

    ldtmp = [kb.tile(es, [128, 128], F32, "ldtmp") for _ in range(2)]
    ldctr = [0]

    def load_T(dst_ap, dst_buf, rows_ap, n, dup64=False):
        tt, tb_ = ldtmp[ldctr[0] % 2]
        ldctr[0] += 1
        if dup64:
            dma("sp", tt[0:n, 0:64], rows_ap, writes=[tb_])
            dma("sp", tt[0:n, 64:128], rows_ap, writes=[tb_])
        else:
            dma("sp", tt[0:n, :], rows_ap, writes=[tb_])
        pt, pb = next_ps()
        op("pe", lambda e: e.transpose(pt[:, 0:n], tt[0:n, :], mats[0:n, IDENT, 0:n]), reads=[tb_, mats_b], writes=[pb])
        op("dve", lambda e: e.tensor_copy(out=dst_ap, in_=pt[:, 0:n]), reads=[pb], writes=[dst_buf])

    def stream_of(t0):
        return 1 if t0 < LC else 0

    def tok_blocks(maxlen):
        blks = [(0, LC)] if LC <= maxlen else [(i, min(maxlen, LC - i)) for i in range(0, LC, maxlen)]
        for i in range(LC, T, maxlen):
            blks.append((i, min(maxlen, T - i)))
        return blks

    def sub_tiles(tl):
        return [(s, min(512, tl - s)) for s in range(0, tl, 512)]

    def linear(srcs, rounds, chunks, epilogue, tb=1024, wgw=512, setup=None, nbufw=2, tblocks=None, sb_srcs=None):
        if skip():
            return
        with contextlib.ExitStack() as st:
            kcs = [s.shape[0] for s in srcs]
            src_bytes = sum(k * tb * 2 for k in kcs)
            nsb = 2 if src_bytes * 2 <= 72 * 1024 else 1
            sb = [[kb.tile(st, [128, k, tb], BF16, f"src{i}") for _ in range(nsb)] for i, k in enumerate(kcs)]
            groups = []
            cur = []
            for ci, (off, m) in enumerate(chunks):
                if cur and (off + m - chunks[cur[0]][0] > wgw or off != chunks[cur[-1]][0] + chunks[cur[-1]][1]):
                    groups.append(cur)
                    cur = []
                cur.append(ci)
            if cur:
                groups.append(cur)
            terms = [t for r in rounds for t in r]
            wt = {}
            for ti, (si, W, cb) in enumerate(terms):
                wt[ti] = [kb.tile(st, [128, kcs[si], wgw], BF16, f"w{ti}") for _ in range(nbufw)]
            env = setup(st) if setup else None
            blocks = tblocks if tblocks is not None else tok_blocks(tb)
            wctr = 0
            for bi, (t0, tl) in enumerate(blocks):
                cur_sb = []
                for i, s in enumerate(srcs):
                    if sb_srcs and i in sb_srcs:
                        cur_sb.append(sb_srcs[i])
                        continue
                    tl_, b_ = sb[i][bi % nsb]
                    dma("sp", tl_[:, :, 0:tl], s[:, :, t0:t0 + tl].rearrange("k p t -> p k t"), writes=[b_])
                    cur_sb.append((tl_, b_))
                for grp in groups:
                    g0 = chunks[grp[0]][0]
                    g1 = chunks[grp[-1]][0] + chunks[grp[-1]][1]
                    wcur = {}
                    ti = 0
                    for r in rounds:
                        for (si, W, cb) in r:
                            wtile, wb = wt[ti][wctr % nbufw]
                            dma("pool", wtile[:, :, 0:g1 - g0],
                                W[:, cb + g0:cb + g1].rearrange("(k p) n -> p k n", p=128), writes=[wb])
                            wcur[ti] = (wtile, wb)
                            ti += 1
                    wctr += 1
                    for ci in grp:
                        off, m = chunks[ci]
                        for (s0, sl) in sub_tiles(tl):
                            ti = 0
                            for ri, r in enumerate(rounds):
                                pss = []
                                for (si, W, cb) in r:
                                    wtile, wb = wcur[ti]
                                    stile, sbf = cur_sb[si]
                                    pt, pb = next_ps()
                                    mmg(pt[0:m, 0:sl],
                                        [(wtile[:, k, off - g0:off - g0 + m], stile[:, k, s0:s0 + sl])
                                         for k in range(kcs[si])],
                                        reads=[wb, sbf], writes=[pb])
                                    pss.append((pt, pb))
                                    ti += 1
                                epilogue(ri, ci, pss, t0 + s0, sl, env)
            barrier()

    def linear_tm(src, W, col_base, ncols, epilogue, setup=None):
        if skip():
            return
        with contextlib.ExitStack() as st:
            kc = src.shape[0]
            tb = 512
            sb = [kb.tile(st, [128, kc, tb], BF16, "srctm") for _ in range(2)]
            wtile, wb = kb.tile(st, [128, kc, ncols], BF16, "wtm")
            env = setup(st) if setup else None
            dma("pool", wtile[:], W[:, col_base:col_base + ncols].rearrange("(k p) n -> p k n", p=128), writes=[wb])
            for bi, (t0, tl) in enumerate(tok_blocks(tb)):
                stile, sbf = sb[bi % 2]
                dma("sp", stile[:, :, 0:tl], src[:, :, t0:t0 + tl].rearrange("k p t -> p k t"), writes=[sbf])
                for q in range(tl // 128):
                    for c0 in range(0, ncols, 512):
                        w = min(512, ncols - c0)
                        pt, pb = next_ps()
                        mmg(pt[:, 0:w], [(stile[:, k, q * 128:(q + 1) * 128], wtile[:, k, c0:c0 + w]) for k in range(kc)],
                            reads=[wb, sbf], writes=[pb])
                        epilogue(pt, pb, t0 + q * 128, c0, w, env)
            barrier()

    def ep_store(dst, dt, rowmap=None, func=AF.Copy):
        def setup(st):
            return [kb.tile(st, [128, 512], dt, "stg") for _ in range(3)], [0]

        def ep(ri, ci, pss, t0, sl, env):
            stg, ctr = env
            (pt, pb), = pss
            tl_, b_ = stg[ctr[0] % 3]
            ctr[0] += 1
            ch, m = rowmap(ci) if rowmap else (ci, 128)
            op("act", lambda e: e.activation(out=tl_[0:m, 0:sl], in_=pt[0:m, 0:sl], func=func), reads=[pb], writes=[b_])
            dma("act", dst[ch, 0:m, t0:t0 + sl], tl_[0:m, 0:sl], reads=[b_])
        return setup, ep

    def norm_stage(src, dst, nfeat, gain_ap_fn, shift_ap_fn, blocks=None):
        if skip():
            return
        kc = src.shape[0]
        with contextlib.ExitStack() as st:
            xs = [kb.tile(st, [128, kc, 512], F32, "nx") for _ in range(2)]
            sq = [kb.tile(st, [128, 512], BF16, "nsq") for _ in range(4)]
            rs = [kb.tile(st, [128, 512], F32, "nrs") for _ in range(2)]
            tmp = [kb.tile(st, [128, 512], F32, "ntmp") for _ in range(3)]
            ob = [kb.tile(st, [128, kc, 512], BF16, "nob") for _ in range(2)]
            for bi, (t0, tl) in enumerate(blocks or tok_blocks(512)):
                xt, xb = xs[bi % 2]
                dma("sp", xt[:, :, 0:tl], src[:, :, t0:t0 + tl].rearrange("k p t -> p k t"), writes=[xb])
                pt, pb = next_ps()
                for k in range(kc):
                    sqt, sqb = sq[k % 4]
                    if k % 3 == 2:
                        op("pool", lambda e: e.tensor_tensor(out=sqt[:, 0:tl], in0=xt[:, k, 0:tl], in1=xt[:, k, 0:tl], op=ALU.mult),
                           reads=[xb], writes=[sqb])
                    else:
                        op("act", lambda e: e.activation(out=sqt[:, 0:tl], in_=xt[:, k, 0:tl], func=AF.Square),
                           reads=[xb], writes=[sqb])
                    mmg(pt[:, 0:tl], [(onesb[:, :], sqt[:, 0:tl])], reads=[sqb, onesb_b], writes=[pb],
                        start=(k == 0), stop=(k == kc - 1))
                rt, rb = rs[bi % 2]
                op("act", lambda e: e.activation(out=rt[:, 0:tl], in_=pt[:, 0:tl], func=AF.Sqrt,
                                                 bias=epsT[:, 0:1], scale=1.0 / nfeat), reads=[pb, epsT_b], writes=[rb])
                op("dve", lambda e: e.reciprocal(out=rt[:, 0:tl], in_=rt[:, 0:tl]), reads=[rb], writes=[rb])
                ot, obf = ob[bi % 2]
                gain = gain_ap_fn(t0)
                shift = shift_ap_fn(t0) if shift_ap_fn else None
                for k in range(kc):
                    tt, tbf = tmp[k % 3]
                    op("dve", lambda e: e.scalar_tensor_tensor(out=tt[:, 0:tl], in0=xt[:, k, 0:tl], scalar=gain[0][:, k:k + 1],
                                                               in1=rt[:, 0:tl], op0=ALU.mult, op1=ALU.mult),
                       reads=[xb, rb, gain[1]], writes=[tbf])
                    if shift is not None:
                        op("act", lambda e: e.activation(out=ot[:, k, 0:tl], in_=tt[:, 0:tl], func=AF.Identity,
                                                         bias=shift[0][:, k:k + 1]), reads=[tbf, shift[1]], writes=[obf])
                    else:
                        op("act", lambda e: e.activation(out=ot[:, k, 0:tl], in_=tt[:, 0:tl], func=AF.Copy),
                           reads=[tbf], writes=[obf])
                dma("act", dst[:, :, t0:t0 + tl].rearrange("k p t -> p k t"), ot[:, :, 0:tl], reads=[obf])
            barrier()

    def input_stage():
        if skip():
            return
        with contextlib.ExitStack() as st:
            xin = [kb.tile(st, [128, D], F32, "xin") for _ in range(2)]
            xo = [kb.tile(st, [128, KC, 128], F32, "xo") for _ in range(2)]
            for bi in range(T // 128):
                t0 = bi * 128
                it, ib = xin[bi % 2]
                srcap = ctx_in[t0:t0 + 128, :] if t0 < LC else x_in[t0 - LC:t0 - LC + 128, :]
                dma("sp", it[:], srcap, writes=[ib])
                ot, obf = xo[bi % 2]
                for k in range(KC):
                    pt, pb = next_ps()
                    op("pe", lambda e: e.transpose(pt[:, 0:128], it[:, k * 128:(k + 1) * 128], mats[:, IDENT, :]),
                       reads=[ib, mats_b], writes=[pb])
                    if k % 2 == 0:
                        op("dve", lambda e: e.tensor_copy(out=ot[:, k, :], in_=pt[:, 0:128]), reads=[pb], writes=[obf])
                    else:
                        op("act", lambda e: e.activation(out=ot[:, k, :], in_=pt[:, 0:128], func=AF.Copy), reads=[pb], writes=[obf])
                dma("act", xT[:, :, t0:t0 + 128].rearrange("k p t -> p k t"), ot[:], reads=[obf])
            if cfg.get('iv') == 1:
                barrier()
                return
            ct, cb_ = kb.tile(st, [128, KC, 2], F32, "ct")
            load_T(ct[:, :, 0], cb_, c_in.rearrange("o (k p) -> (o k) p", p=128), KC)
            load_T(ct[:, :, 1], cb_, cctx_in.rearrange("o (k p) -> (o k) p", p=128), KC)
            op("act", lambda e: e.activation(out=scS[:], in_=ct[:], func=AF.Silu), reads=[cb_], writes=[scS_b])
            barrier()

    def mod_stage(li):
        if skip():
            return
        bm_rows = b_mod[li:li + 1, :].rearrange("o (n p) -> (o n) p", p=128)
        for r0 in range(0, NMODC, 72):
            r1 = min(NMODC, r0 + 72)
            load_T(bmodT[:, r0:r1], bmodT_b, bm_rows[r0:r1, :], r1 - r0)
        load_T(ngT[:].rearrange("p j k -> p (j k)"), ngT_b, norm_g[li].rearrange("j (k p) -> (j k) p", p=128), 3 * KC)
        load_T(smallv[:, 0:1], smallv_b, g_q[li:li + 1, :], 1)
        load_T(smallv[:, 1:2], smallv_b, g_k[li:li + 1, :], 1)

        def ep(ri, ci, pss, t0, sl, env):
            (pt, pb), = pss
            op("dve", lambda e: e.tensor_scalar(out=modT[:, ci, :], in0=pt[:, 0:2], scalar1=bmodT[:, ci:ci + 1], scalar2=None,
                                                op0=ALU.add), reads=[pb, bmodT_b], writes=[modT_b])
        linear([scT], [[(0, w_mod[li], 0)]], [(i * 128, 128) for i in range(NMODC)], ep, tb=2, tblocks=[(0, 2)], nbufw=3, sb_srcs={0: (scS, scS_b)})
        for j in range(3):
            for s in range(2):
                sc = modT[:, (3 * j + 1) * KC:(3 * j + 2) * KC, s]
                sh = modT[:, (3 * j) * KC:(3 * j + 1) * KC, s]
                gt = modT[:, (3 * j + 2) * KC:(3 * j + 3) * KC, s]
                op("dve", lambda e: e.scalar_tensor_tensor(out=gains[:, j, s, :], in0=sc, scalar=1.0, in1=ngT[:, j, :],
                                                           op0=ALU.add, op1=ALU.mult), reads=[modT_b, ngT_b], writes=[gains_b])
                op("dve", lambda e: e.tensor_copy(out=shifts[:, j, s, :], in_=sh), reads=[modT_b], writes=[shifts_b])
                op("dve", lambda e: e.tensor_scalar(out=rgate[:, j, s, :], in0=gt, scalar1=(1.0 if j == 1 else 0.5), scalar2=None,
                                                    op0=ALU.mult), reads=[modT_b], writes=[rgate_b])
        barrier()

    def gain_fn(j):
        return lambda t0: (gains[:, j, stream_of(t0), :], gains_b)

    def shift_fn(j):
        return lambda t0: (shifts[:, j, stream_of(t0), :], shifts_b)

    def ep_residual(j):
        def setup(st):
            return ([kb.tile(st, [128, 512], F32, "rx") for _ in range(3)], [0])

        def ep(ri, ci, pss, t0, sl, env):
            xs, ctr = env
            (pt, pb), = pss
            xt, xb = xs[ctr[0] % 3]
            ctr[0] += 1
            dma("sp", xt[:, 0:sl], xT[ci, :, t0:t0 + sl], writes=[xb])
            s = stream_of(t0)
            op("dve", lambda e: e.scalar_tensor_tensor(out=xt[:, 0:sl], in0=pt[:, 0:sl], scalar=rgate[:, j, s, ci:ci + 1],
                                                       in1=xt[:, 0:sl], op0=ALU.mult, op1=ALU.add),
               reads=[pb, xb, rgate_b], writes=[xb])
            dma("act", xT[ci, :, t0:t0 + sl], xt[:, 0:sl], reads=[xb])
        return setup, ep

    def ffn_stage(li, which, j):
        norm_stage(xT, hT, D, gain_fn(j), shift_fn(j))
        Wu = w_up[li, which]
        Wd = w_down[li, which]

        def setup(st):
            return ([kb.tile(st, [128, 512], F32, "sg") for _ in range(3)],
                    [kb.tile(st, [128, 512], BF16, "ao") for _ in range(3)], [0])

        def ep(ri, ci, pss, t0, sl, env):
            sgs, aos, ctr = env
            (pg, pgb), (pu, pub) = pss
            sg, sgb = sgs[ctr[0] % 3]
            ao, aob = aos[ctr[0] % 3]
            ctr[0] += 1
            op("act", lambda e: e.activation(out=sg[:, 0:sl], in_=pg[:, 0:sl], func=AF.Silu), reads=[pgb], writes=[sgb])
            op("dve", lambda e: e.tensor_tensor(out=ao[:, 0:sl], in0=sg[:, 0:sl], in1=pu[:, 0:sl], op=ALU.mult),
               reads=[sgb, pub], writes=[aob])
            dma("pool", actT[ci, :, t0:t0 + sl], ao[:, 0:sl], reads=[aob])
        linear([hT], [[(0, Wu, 0), (0, Wu, DFF)]], [(i * 128, 128) for i in range(FC)], ep, tb=1024, wgw=512, setup=setup)
        su, epr = ep_residual(j)
        linear([actT], [[(0, Wd, 0)]], [(i * 128, 128) for i in range(KC)], epr, tb=1024, wgw=256, setup=su)

    def ep_rope(dst, rows, norm_col, rowmap=None):
        cosD, sinD = (k_cos128, k_sin128) if rows == 128 else (k_cos64, k_sin64)
        RM = R128 if rows == 128 else R64

        def setup(st):
            return dict(q=[kb.tile(st, [128, 512], F32, "rq") for _ in range(4)],
                        cs=[kb.tile(st, [128, 2, 512], F32, "rcs") for _ in range(4)],
                        t=[kb.tile(st, [128, 512], F32, "rt") for _ in range(4)],
                        qh=[kb.tile(st, [128, 512], BF16, "rqh") for _ in range(4)],
                        sqh=[kb.tile(st, [128, 512], BF16, "rsqh") for _ in range(4)],
                        r=[kb.tile(st, [128, 512], F32, "rr") for _ in range(4)],
                        o=[kb.tile(st, [128, 512], BF16, "ro") for _ in range(4)], ctr=[0])

        def ep(ri, ci, pss, t0, sl, env):
            i = env["ctr"][0] % 4
            env["ctr"][0] += 1
            (pt, pb), = pss
            q, qb = env["q"][i]
            cs, csb = env["cs"][i]
            tt, tb_ = env["t"][i]
            rr, rrb = env["r"][i]
            o, ob_ = env["o"][i]
            qh, qhb = env["qh"][i]
            sqh, sqhb = env["sqh"][i]
            ch = rowmap(ci) if rowmap else ci
            dma("sp", cs[0:rows, 0, 0:sl], cosD[:, t0:t0 + sl], writes=[csb])
            dma("sp", cs[0:rows, 1, 0:sl], sinD[:, t0:t0 + sl], writes=[csb])
            if norm_col is not None:
                op("act", lambda e: e.activation(out=sqh[0:rows, 0:sl], in_=pt[0:rows, 0:sl], func=AF.Square), reads=[pb], writes=[sqhb])
                p2, p2b = next_ps()
                mmg(p2[0:rows, 0:sl], [(onesb[0:rows, 0:rows], sqh[0:rows, 0:sl])], reads=[sqhb, onesb_b], writes=[p2b])
                op("act", lambda e: e.activation(out=rr[0:rows, 0:sl], in_=p2[0:rows, 0:sl], func=AF.Sqrt, bias=epsT[0:rows, 0:1],
                                                 scale=1.0 / rows), reads=[p2b, epsT_b], writes=[rrb])
                op("dve", lambda e: e.reciprocal(out=rr[0:rows, 0:sl], in_=rr[0:rows, 0:sl]), reads=[rrb], writes=[rrb])
                op("dve", lambda e: e.scalar_tensor_tensor(out=qh[0:rows, 0:sl], in0=pt[0:rows, 0:sl],
                                                           scalar=smallv[0:rows, norm_col:norm_col + 1], in1=rr[0:rows, 0:sl],
                                                           op0=ALU.mult, op1=ALU.mult), reads=[pb, rrb, smallv_b], writes=[qhb])
            else:
                op("act", lambda e: e.activation(out=qh[0:rows, 0:sl], in_=pt[0:rows, 0:sl], func=AF.Copy), reads=[pb], writes=[qhb])
            p3, p3b = next_ps()
            mmg(p3[0:rows, 0:sl], [(matsb[0:rows, RM - 1, 0:rows], qh[0:rows, 0:sl])], reads=[qhb, matsb_b], writes=[p3b])
            op("pool", lambda e: e.tensor_tensor(out=q[0:rows, 0:sl], in0=qh[0:rows, 0:sl], in1=cs[0:rows, 0, 0:sl], op=ALU.mult),
               reads=[qhb, csb], writes=[qb])
            op("dve", lambda e: e.tensor_tensor(out=tt[0:rows, 0:sl], in0=p3[0:rows, 0:sl], in1=cs[0:rows, 1, 0:sl], op=ALU.mult),
               reads=[p3b, csb], writes=[tb_])
            op("dve", lambda e: e.tensor_tensor(out=o[0:rows, 0:sl], in0=q[0:rows, 0:sl], in1=tt[0:rows, 0:sl], op=ALU.add),
               reads=[qb, tb_], writes=[ob_])
            dma("pool", dst[ch, 0:rows, t0:t0 + sl], o[0:rows, 0:sl], reads=[ob_])
        return setup, ep

    def attention_stage(heads, Vd, scale, bg=None):
        if skip():
            return
        nkc = T // 128
        with contextlib.ExitStack() as st:
            kt = [[kb.tile(st, [128, T], BF16, "ak") for _ in range(2)] for _ in range(2)]
            vt = [kb.tile(st, [128, nkc, 128], BF16, "av") for _ in range(2)]
            qt = [[kb.tile(st, [128, 512], BF16, "aq") for _ in range(2)] for _ in range(2)]
            pts = [kb.tile(st, [128, 512], BF16, "ap") for _ in range(4)]
            rd = [kb.tile(st, [128, 512], F32, "ard") for _ in range(2)]
            ot = [kb.tile(st, [128, 512], BF16, "ao") for _ in range(2)]
            qblocks = tok_blocks(512)
            item = 0
            pctr = 0
            for hi, hd in enumerate(heads):
                nk = len(hd["k"])
                for j, (kap, rows) in enumerate(hd["k"]):
                    dma("sp", kt[j][hi % 2][0][0:rows, :], kap, writes=[kt[j][hi % 2][1]])
                vtile, vb = vt[hi % 2]
                vc = hd["vcol"]
                dma("sp", vtile[:], Vd[:, vc:vc + 128].rearrange("(c p) d -> p c d", p=128), writes=[vb])
                for (q0, ql) in qblocks:
                    nkeys = LC if q0 < LC else T
                    for j, (qap, rows) in enumerate(hd["q"]):
                        dma("sp", qt[j][item % 2][0][0:rows, 0:ql], qap[:, q0:q0 + ql], writes=[qt[j][item % 2][1]])
                    pso, psob = PS[4 + (item % 2)]
                    psd, psdb = PS[6 + (item % 2)]
                    ncks = nkeys // 128

                    def score(c, slot):
                        pt, pb = PS[slot]
                        pairs = []
                        rds = []
                        for j, (kap, rows) in enumerate(hd["k"]):
                            pairs.append((kt[j][hi % 2][0][0:rows, c * 128:(c + 1) * 128], qt[j][item % 2][0][0:rows, 0:ql]))
                            rds += [kt[j][hi % 2][1], qt[j][item % 2][1]]
                        mmg(pt[:, 0:ql], pairs, reads=rds, writes=[pb])
                    score(0, pctr % 4)
                    for c in range(ncks):
                        if c + 1 < ncks:
                            score(c + 1, (pctr + 1) % 4)
                        pt, pb = PS[pctr % 4]
                        ptile, ptb = pts[pctr % 4]
                        pctr += 1
                        op("act", lambda e: e.activation(out=ptile[:, 0:ql], in_=pt[:, 0:ql], func=AF.Exp, scale=scale),
                           reads=[pb], writes=[ptb])
                        mmg(pso[:, 0:ql], [(vtile[:, c, :], ptile[:, 0:ql])], reads=[vb, ptb], writes=[psob],
                            start=(c == 0), stop=(c == ncks - 1))
                        mmg(psd[:, 0:ql], [(onesb[:, :], ptile[:, 0:ql])], reads=[onesb_b, ptb], writes=[psdb],
                            start=(c == 0), stop=(c == ncks - 1))
                    rt, rb = rd[item % 2]
                    o, ob_ = ot[item % 2]
                    op("dve", lambda e: e.reciprocal(out=rt[:, 0:ql], in_=psd[:, 0:ql]), reads=[psdb], writes=[rb])
                    op("dve", lambda e: e.tensor_tensor(out=o[:, 0:ql], in0=pso[:, 0:ql], in1=rt[:, 0:ql], op=ALU.mult),
                       reads=[psob, rb], writes=[ob_])
                    dma("pool", hd["out"][:, q0:q0 + ql], o[:, 0:ql], reads=[ob_])
                    item += 1
                    if bg is not None:
                        bg(8)
            barrier()

    def s5_prep(li, d, st):
        Wm = {}
        for nm in ("toep", "bin", "bins", "cout"):
            Wm[nm] = kb.tile(st, [128, G, 128], BF16, "s5" + nm)
        ARR, ARb = kb.tile(st, [128, 2, G], F32, "s5ARR")
        AXX, AXb = kb.tile(st, [128, 2, G], F32, "s5AXX")
        AXsb = AXb
        AR = ARR[:, 0, :]
        AX = AXX[:, 0, :]
        AXs = AXX[:, 1, :]
        with contextlib.ExitStack() as s2:
            def tl(shape, name):
                return kb.tile(s2, shape, F32, name)
            lr, lrb = tl([128, G], "lr")
            li_, lib = tl([128, G], "li")
            dt, dtb = tl([128, G], "dt")
            jf, jfb = tl([128, 16], "jf")
            dma("sp", jf[:], k_jf, writes=[jfb])
            load_T(lr[:], lrb, lam_re[li, d], G, dup64=True)
            load_T(li_[:], lib, lam_im[li, d], G, dup64=True)
            dma("sp", dt[:], log_dt[li, d:d + 1, :].broadcast_to([128, G]), writes=[dtb])
            op("dve", lambda e: e.tensor_scalar(out=lr[:], in0=lr[:], scalar1=-1e-4, scalar2=None, op0=ALU.min), reads=[lrb], writes=[lrb])
            op("act", lambda e: e.activation(out=dt[:], in_=dt[:], func=AF.Exp), reads=[dtb], writes=[dtb])
            ld, ldb = tl([128, G], "ld")
            an, anb = tl([128, G], "an")
            op("dve", lambda e: e.tensor_tensor(out=ld[:], in0=lr[:], in1=dt[:], op=ALU.mult), reads=[lrb, dtb], writes=[ldb])
            op("dve", lambda e: e.tensor_tensor(out=an[:], in0=li_[:], in1=dt[:], op=ALU.mult), reads=[lib, dtb], writes=[anb])
            mg, mgb = tl([128, G, 16], "mg")
            ag, agb = tl([128, G, 16], "ag")
            PR, PRb = tl([128, G, 16], "PR")
            PI, PIb = tl([128, G, 16], "PI")
            kk, kkb = tl([128, G, 16], "kk")
            ki = kb.tile(s2, [128, G, 16], mybir.dt.int32, "ki")
            for g in range(G):
                op("dve", lambda e: e.tensor_scalar(out=mg[:, g, :], in0=jf[:], scalar1=ld[:, g:g + 1], scalar2=None, op0=ALU.mult),
                   reads=[jfb, ldb], writes=[mgb])
                op("pool", lambda e: e.tensor_scalar(out=ag[:, g, :], in0=jf[:], scalar1=an[:, g:g + 1], scalar2=None, op0=ALU.mult),
                   reads=[jfb, anb], writes=[agb])
            op("act", lambda e: e.activation(out=mg[:], in_=mg[:], func=AF.Exp), reads=[mgb], writes=[mgb])

            def sin_of(out_t, out_b, shift):
                TWO_PI = 2.0 * math.pi
                op("dve", lambda e: e.tensor_scalar(out=kk[:], in0=ag[:], scalar1=shift, scalar2=1.0 / TWO_PI, op0=ALU.add, op1=ALU.mult),
                   reads=[agb], writes=[kkb])
                op("dve", lambda e: e.tensor_copy(out=ki[0][:], in_=kk[:]), reads=[kkb], writes=[ki[1]])
                op("dve", lambda e: e.tensor_copy(out=kk[:], in_=ki[0][:]), reads=[ki[1]], writes=[kkb])
                op("dve", lambda e: e.scalar_tensor_tensor(out=kk[:], in0=kk[:], scalar=-TWO_PI, in1=ag[:], op0=ALU.mult, op1=ALU.add),
                   reads=[kkb, agb], writes=[kkb])
                op("dve", lambda e: e.tensor_scalar(out=kk[:], in0=kk[:], scalar1=shift, scalar2=None, op0=ALU.add), reads=[kkb], writes=[kkb])
                op("dve", lambda e: e.tensor_scalar(out=out_t[:], in0=kk[:], scalar1=math.pi, scalar2=-TWO_PI, op0=ALU.is_gt, op1=ALU.mult),
                   reads=[kkb], writes=[out_b])
                op("dve", lambda e: e.tensor_tensor(out=kk[:], in0=kk[:], in1=out_t[:], op=ALU.add), reads=[kkb, out_b], writes=[kkb])
                op("dve", lambda e: e.tensor_scalar(out=out_t[:], in0=kk[:], scalar1=-math.pi, scalar2=TWO_PI, op0=ALU.is_lt, op1=ALU.mult),
                   reads=[kkb], writes=[out_b])
                op("dve", lambda e: e.tensor_tensor(out=kk[:], in0=kk[:], in1=out_t[:], op=ALU.add), reads=[kkb, out_b], writes=[kkb])
                op("dve", lambda e: e.tensor_scalar(out=kk[:], in0=kk[:], scalar1=math.pi, scalar2=-math.pi, op0=ALU.min, op1=ALU.max),
                   reads=[kkb], writes=[kkb])
                op("act", lambda e: e.activation(out=out_t[:], in_=kk[:], func=AF.Sin), reads=[kkb], writes=[out_b])
            sin_of(PI, PIb, 0.0)
            sin_of(PR, PRb, math.pi / 2)
            op("dve", lambda e: e.tensor_tensor(out=PR[:], in0=PR[:], in1=mg[:], op=ALU.mult), reads=[PRb, mgb], writes=[PRb])
            op("dve", lambda e: e.tensor_tensor(out=PI[:], in0=PI[:], in1=mg[:], op=ALU.mult), reads=[PIb, mgb], writes=[PIb])

            def P(e_):
                return e_ + 7
            op("dve", lambda e: e.tensor_copy(out=ARR[:, 0, :], in_=PR[:, :, P(8)]), reads=[PRb], writes=[ARb])
            op("dve", lambda e: e.tensor_copy(out=ARR[:, 1, :], in_=PR[:, :, P(8)]), reads=[PRb], writes=[ARb])
            op("dve", lambda e: e.tensor_scalar(out=AXX[0:64, 0, :], in0=PI[0:64, :, P(8)], scalar1=-1.0, scalar2=None, op0=ALU.mult),
               reads=[PIb], writes=[AXb])
            op("dve", lambda e: e.tensor_copy(out=AXX[64:128, 0, :], in_=PI[64:128, :, P(8)]), reads=[PIb], writes=[AXb])
            op("dve", lambda e: e.tensor_scalar(out=AXX[:, 1, :], in0=AXX[:, 0, :], scalar1=-1.0, scalar2=None, op0=ALU.mult), reads=[AXb], writes=[AXb])
            den, denb = tl([128, G], "den")
            t1, t1b = tl([128, G], "t1")
            fr, frb = tl([128, G], "fr")
            fi, fib = tl([128, G], "fi")
            nr, nrb = tl([128, G], "nr")
            op("dve", lambda e: e.tensor_tensor(out=den[:], in0=lr[:], in1=lr[:], op=ALU.mult), reads=[lrb], writes=[denb])
            op("dve", lambda e: e.tensor_tensor(out=t1[:], in0=li_[:], in1=li_[:], op=ALU.mult), reads=[lib], writes=[t1b])
            op("dve", lambda e: e.tensor_tensor(out=den[:], in0=den[:], in1=t1[:], op=ALU.add), reads=[denb, t1b], writes=[denb])
            op("dve", lambda e: e.reciprocal(out=den[:], in_=den[:]), reads=[denb], writes=[denb])
            op("dve", lambda e: e.tensor_scalar(out=nr[:], in0=PR[:, :, P(1)], scalar1=-1.0, scalar2=None, op0=ALU.add), reads=[PRb], writes=[nrb])
            op("dve", lambda e: e.tensor_tensor(out=fr[:], in0=nr[:], in1=lr[:], op=ALU.mult), reads=[nrb, lrb], writes=[frb])
            op("dve", lambda e: e.tensor_tensor(out=t1[:], in0=PI[:, :, P(1)], in1=li_[:], op=ALU.mult), reads=[PIb, lib], writes=[t1b])
            op("dve", lambda e: e.tensor_tensor(out=fr[:], in0=fr[:], in1=t1[:], op=ALU.add), reads=[frb, t1b], writes=[frb])
            op("dve", lambda e: e.tensor_tensor(out=fr[:], in0=fr[:], in1=den[:], op=ALU.mult), reads=[frb, denb], writes=[frb])
            op("dve", lambda e: e.tensor_tensor(out=fi[:], in0=PI[:, :, P(1)], in1=lr[:], op=ALU.mult), reads=[PIb, lrb], writes=[fib])
            op("dve", lambda e: e.tensor_tensor(out=t1[:], in0=nr[:], in1=li_[:], op=ALU.mult), reads=[nrb, lib], writes=[t1b])
            op("dve", lambda e: e.tensor_tensor(out=fi[:], in0=fi[:], in1=t1[:], op=ALU.subtract), reads=[fib, t1b], writes=[fib])
            op("dve", lambda e: e.tensor_tensor(out=fi[:], in0=fi[:], in1=den[:], op=ALU.mult), reads=[fib, denb], writes=[fib])
            br_, brb = tl([128, G, 16], "br")
            bi_, bib = tl([128, G, 16], "bi")
            cr_, crb = tl([128, G, 16], "cr")
            ci_, cib = tl([128, G, 16], "ci")
            for h in range(2):
                for g0 in range(0, G, 8):
                    g1 = min(G, g0 + 8)
                    dma("sp", br_[h * 64:(h + 1) * 64, g0:g1, :], b_re[li, d, g0:g1].rearrange("g p h -> p g h"), writes=[brb])
                    dma("sp", bi_[h * 64:(h + 1) * 64, g0:g1, :], b_im[li, d, g0:g1].rearrange("g p h -> p g h"), writes=[bib])
            for g0 in range(0, G, 8):
                g1 = min(G, g0 + 8)
                nr_ = (g1 - g0) * 16
                load_T(cr_[:, g0:g1, :].rearrange("p g h -> p (g h)"), crb, c_re[li, d, g0:g1].rearrange("g h p -> (g h) p"), nr_, dup64=True)
                load_T(ci_[:, g0:g1, :].rearrange("p g h -> p (g h)"), cib, c_im[li, d, g0:g1].rearrange("g h p -> (g h) p"), nr_, dup64=True)
            bbr, bbrb = kk, kkb
            bbi, bbib = tl([128, G, 16], "bbi")
            t3, t3b = mg, mgb
            for g in range(G):
                e1 = "dve" if g % 2 == 0 else "pool"
                op(e1, lambda e: e.tensor_scalar(out=bbr[:, g, :], in0=br_[:, g, :], scalar1=fr[:, g:g + 1], scalar2=None, op0=ALU.mult),
                   reads=[brb, frb], writes=[bbrb])
                op(e1, lambda e: e.tensor_scalar(out=t3[:, g, :], in0=bi_[:, g, :], scalar1=fi[:, g:g + 1], scalar2=None, op0=ALU.mult),
                   reads=[bib, fib], writes=[t3b])
                op(e1, lambda e: e.tensor_scalar(out=bbi[:, g, :], in0=bi_[:, g, :], scalar1=fr[:, g:g + 1], scalar2=None, op0=ALU.mult),
                   reads=[bib, frb], writes=[bbib])
                op(e1, lambda e: e.tensor_scalar(out=br_[:, g, :], in0=br_[:, g, :], scalar1=fi[:, g:g + 1], scalar2=None, op0=ALU.mult),
                   reads=[brb, fib], writes=[brb])
            op("dve", lambda e: e.tensor_tensor(out=bbr[:], in0=bbr[:], in1=t3[:], op=ALU.subtract), reads=[bbrb, t3b], writes=[bbrb])
            op("dve", lambda e: e.tensor_tensor(out=bbi[:], in0=bbi[:], in1=br_[:], op=ALU.add), reads=[bbib, brb], writes=[bbib])
            if d == 0:
                eX = [-i for i in range(8)]
                eY = [j for j in range(8)]
                eB = [7 - i for i in range(8)]
                eC = [j + 1 for j in range(8)]
                MK = MASKF
            else:
                eX = [i - 7 for i in range(8)]
                eY = [7 - j for j in range(8)]
                eB = [i for i in range(8)]
                eC = [8 - j for j in range(8)]
                MK = MASKR
            GH = max(1, G // 2)
            X, Xb = tl([128, GH, 8, 16], "X")
            Y, Yb = tl([128, GH, 8, 16], "Y")
            t4, t4b = ag, agb

            def cmul_rows(out_t, out_b, i, pw, ur, ui, urb, uib, sign_im_rows, g0):
                prb = PR[:, g0:g0 + GH, pw:pw + 1].broadcast_to([128, GH, 16])
                pib = PI[:, g0:g0 + GH, pw:pw + 1].broadcast_to([128, GH, 16])
                o = out_t[:, :, i, :]
                ur_ = ur[:, g0:g0 + GH, :]
                ui_ = ui[:, g0:g0 + GH, :]
                t4_ = t4[:, 0:GH, :]
                op("dve", lambda e: e.tensor_tensor(out=o[0:64], in0=ur_[0:64], in1=prb[0:64], op=ALU.mult), reads=[urb, PRb], writes=[out_b])
                op("dve", lambda e: e.tensor_tensor(out=t4_[0:64], in0=ui_[0:64], in1=pib[0:64], op=ALU.mult), reads=[uib, PIb], writes=[t4b])
                op("dve", lambda e: e.tensor_tensor(out=o[0:64], in0=o[0:64], in1=t4_[0:64], op=ALU.subtract), reads=[out_b, t4b], writes=[out_b])
                op("dve", lambda e: e.tensor_tensor(out=o[64:128], in0=ui_[64:128], in1=prb[64:128], op=ALU.mult), reads=[uib, PRb], writes=[out_b])
                op("dve", lambda e: e.tensor_tensor(out=t4_[64:128], in0=ur_[64:128], in1=pib[64:128], op=ALU.mult), reads=[urb, PIb], writes=[t4b])
                op("dve", lambda e: e.tensor_tensor(out=o[64:128], in0=o[64:128], in1=t4_[64:128], op=ALU.add), reads=[out_b, t4b], writes=[out_b])
                if sign_im_rows < 0:
                    op("dve", lambda e: e.tensor_scalar(out=o[64:128], in0=o[64:128], scalar1=-1.0, scalar2=None, op0=ALU.mult),
                       reads=[out_b], writes=[out_b])
            for g0 in range(0, G, GH):
                for i in range(8):
                    cmul_rows(X, Xb, i, P(eX[i]), bbr, bbi, bbrb, bbib, +1, g0)
                    cmul_rows(Y, Yb, i, P(eY[i]), cr_, ci_, crb, cib, -1, g0)
                for gg in range(GH):
                    g = g0 + gg
                    xg = X[:, gg].rearrange("p i h -> p (i h)")
                    yg = Y[:, gg].rearrange("p i h -> p (i h)")
                    p1, p1b = next_ps()
                    mmg(p1[:, 0:128], [(xg, yg)], reads=[Xb, Yb], writes=[p1b])
                    op("dve", lambda e: e.tensor_tensor(out=Wm["toep"][0][:, g, :], in0=p1[:, 0:128], in1=mats[:, MK, :], op=ALU.mult),
                       reads=[p1b, mats_b], writes=[Wm["toep"][1]])
                for i in range(8):
                    cmul_rows(X, Xb, i, P(eB[i]), bbr, bbi, bbrb, bbib, +1, g0)
                    cmul_rows(Y, Yb, i, P(eC[i]), cr_, ci_, crb, cib, -1, g0)
                op("act", lambda e: e.activation(out=Wm["cout"][0][:, g0:g0 + GH, :], in_=Y[:].rearrange("p g i h -> p g (i h)"), func=AF.Copy),
                   reads=[Yb], writes=[Wm["cout"][1]])
                for gg in range(GH):
                    g = g0 + gg
                    xbg = X[:, gg].rearrange("p i h -> p (i h)")
                    p2, p2b = next_ps()
                    mmg(p2[:, 0:128], [(xbg, mats[:, IDENT, :])], reads=[Xb, mats_b], writes=[p2b])
                    op("act", lambda e: e.activation(out=Wm["bin"][0][:, g, :], in_=p2[:, 0:128], func=AF.Copy), reads=[p2b], writes=[Wm["bin"][1]])
                    p3, p3b = next_ps()
                    mmg(p3[:, 0:128], [(xbg, mats[:, SWAPN, :])], reads=[Xb, mats_b], writes=[p3b])
                    op("act", lambda e: e.activation(out=Wm["bins"][0][:, g, :], in_=p3[:, 0:128], func=AF.Copy), reads=[p3b], writes=[Wm["bins"][1]])
            barrier()
        return Wm, (ARR, ARb), (AXX, AXb), (None, None)

    def vv_dma(q, tile_ap, buf, dview, n, load):
        if n == S5RL:
            flat = tile_ap.rearrange("p s g c -> p (s g c)")
            if load:
                dma(q, flat, dview, writes=[buf])
            else:
                dma(q, dview, flat, reads=[buf])
            return
        dv = dview.rearrange("p (s g c) -> p s g c", s=2, g=G)
        for sl_ in range(2):
            if load:
                dma(q, tile_ap[:, sl_, :, 0:n], dv[:, sl_, :, 0:n], writes=[buf])
            else:
                dma(q, dv[:, sl_, :, 0:n], tile_ap[:, sl_, :, 0:n], reads=[buf])

    def s5_runs(d):
        RL = S5RL
        if d == 0:
            return [(c0, min(NC8, c0 + RL)) for c0 in range(0, NC8, RL)]
        cc = LC // 8
        runs = [(c0, min(cc, c0 + RL)) for c0 in reversed(range(0, cc, RL))]
        c1 = NC8
        while c1 > cc:
            c0 = max(cc, c1 - RL)
            runs.append((c0, c1))
            c1 = c0
        return runs

    def s5_phaseA(li):
        RL = S5RL
        GB = max(1, 512 // RL)
        for d in range(2):
            if skip():
                continue
            with contextlib.ExitStack() as st:
                Wm, (ARR, ARb), (AXX, AXb), _unused = s5_prep(li, d, st)
                op("dve", lambda e: e.tensor_copy(out=s5tab[:, d, 0], in_=ARR[:]), reads=[ARb], writes=[s5tab_b])
                op("dve", lambda e: e.tensor_copy(out=s5tab[:, d, 1], in_=AXX[:]), reads=[AXb], writes=[s5tab_b])
                dma("act", wmD[d, 0], Wm["toep"][0][:].rearrange("p g m -> p (g m)"), reads=[Wm["toep"][1]])
                dma("act", wmD[d, 1], Wm["cout"][0][:].rearrange("p g m -> p (g m)"), reads=[Wm["cout"][1]])
                runs = s5_runs(d)
                U8 = [kb.tile(st, [128, G, RL], BF16, "U8") for _ in range(2)]
                VVs = [kb.tile(st, [128, 2, G, RL], F32, "VV") for _ in range(2)]
                for ri, (c0, c1) in enumerate(runs):
                    n = c1 - c0
                    u8t, u8b = U8[ri % 2]
                    VV, Vb = VVs[ri % 2]
                    dma("sp", u8t[:, :, 0:n], u8D[:, :, c0:c1].rearrange("g p c -> p g c"), writes=[u8b])
                    for g0 in range(0, G, GB):
                        g1 = min(G, g0 + GB)
                        for slot, nm in ((0, "bin"), (1, "bins")):
                            pt, pb = next_ps()
                            for g in range(g0, g1):
                                mmg(pt[:, (g - g0) * RL:(g - g0) * RL + n], [(Wm[nm][0][:, g, :], u8t[:, g, 0:n])],
                                    reads=[Wm[nm][1], u8b], writes=[pb])
                            pv = pt[:, 0:(g1 - g0) * RL].rearrange("p (g c) -> p g c", c=RL)
                            if slot == 0:
                                op("act", lambda e: e.activation(out=VV[:, slot, g0:g1, 0:n], in_=pv[:, :, 0:n], func=AF.Copy), reads=[pb], writes=[Vb])
                            else:
                                op("dve", lambda e: e.tensor_copy(out=VV[:, slot, g0:g1, 0:n], in_=pv[:, :, 0:n]), reads=[pb], writes=[Vb])
                    vv_dma("act", VV[:], Vb, vvD[d, ri], n, False)
                barrier()

    def s5_recur_gen(li, st):
        RL = S5RL
        VVs = [kb.tile(st, [128, 2, G, RL], F32, "rVV") for _ in range(2)]
        SAs = [kb.tile(st, [128, G, RL], BF16, "rSA") for _ in range(2)]
        X = [kb.tile(st, [128, 2, G], F32, "rX") for _ in range(2)]
        ta, tab = kb.tile(st, [128, 2, G], F32, "rta")
        tb2, tb2b = kb.tile(st, [128, 2, G], F32, "rtb")

        def _gen():
          for d in range(2):
              runs = s5_runs(d)
              ARR = s5tab[:, d, 0]
              AXX = s5tab[:, d, 1]
              cur = 0
              op("dve", lambda e: e.memset(X[0][0][:], 0.0), writes=[X[0][1]])
              n0 = runs[0][1] - runs[0][0]
              vv_dma("pool", VVs[0][0][:], VVs[0][1], vvD[d, 0], n0, True)
              for ri, (c0, c1) in enumerate(runs):
                  n = c1 - c0
                  VV, Vb = VVs[ri % 2]
                  SA, SAb = SAs[ri % 2]
                  if ri + 1 < len(runs):
                      n1 = runs[ri + 1][1] - runs[ri + 1][0]
                      vv_dma("pool", VVs[(ri + 1) % 2][0][:], VVs[(ri + 1) % 2][1], vvD[d, ri + 1], n1, True)
                  order = range(n) if d == 0 else range(n - 1, -1, -1)
                  for c in order:
                      x_t, x_b = X[cur]
                      n_t, n_b = X[1 - cur]
                      op("pool", lambda e: e.tensor_copy(out=SA[:, :, c], in_=x_t[:, 0, :]), reads=[x_b], writes=[SAb])
                      op("dve", lambda e: e.tensor_tensor(out=ta[:], in0=ARR, in1=x_t[:], op=ALU.mult), reads=[s5tab_b, x_b], writes=[tab])
                      op("dve", lambda e: e.tensor_tensor(out=tb2[:], in0=AXX, in1=x_t[:, ::-1, :], op=ALU.mult), reads=[s5tab_b, x_b], writes=[tb2b])
                      op("dve", lambda e: e.tensor_tensor(out=ta[:], in0=ta[:], in1=tb2[:], op=ALU.add), reads=[tab, tb2b], writes=[tab])
                      op("dve", lambda e: e.tensor_tensor(out=n_t[:], in0=ta[:], in1=VV[:, :, :, c], op=ALU.add), reads=[tab, Vb], writes=[n_b])
                      cur = 1 - cur
                      yield
                  dma("pool", saD[d, ri].rearrange("p (g c) -> p g c", g=G)[:, :, 0:n], SA[:, :, 0:n], reads=[SAb])
              if cur == 1:
                  pass
        return _gen()

    def s5_phaseC(li):
        RL = S5RL
        GB = max(1, 512 // RL)
        for d in range(2):
            if skip():
                continue
            with contextlib.ExitStack() as st:
                TO, TOb = kb.tile(st, [128, G, 128], BF16, "cTO")
                CO, COb = kb.tile(st, [128, G, 128], BF16, "cCO")
                dma("sp", TO[:].rearrange("p g m -> p (g m)"), wmD[d, 0], writes=[TOb])
                dma("sp", CO[:].rearrange("p g m -> p (g m)"), wmD[d, 1], writes=[COb])
                runs = s5_runs(d)
                U8 = [kb.tile(st, [128, G, RL], BF16, "cU8") for _ in range(2)]
                SAs = [kb.tile(st, [128, G, RL], BF16, "cSA") for _ in range(2)]
                YO = [kb.tile(st, [128, G, RL], F32, "cYO") for _ in range(2)]
                for ri, (c0, c1) in enumerate(runs):
                    n = c1 - c0
                    u8t, u8b = U8[ri % 2]
                    SA, SAb = SAs[ri % 2]
                    yt, yb = YO[ri % 2]
                    dma("sp", u8t[:, :, 0:n], u8D[:, :, c0:c1].rearrange("g p c -> p g c"), writes=[u8b])
                    dma("sp", SA[:, :, 0:n], saD[d, ri].rearrange("p (g c) -> p g c", g=G)[:, :, 0:n], writes=[SAb])
                    for gi, g0 in enumerate(range(0, G, GB)):
                        g1 = min(G, g0 + GB)
                        pt, pb = next_ps()
                        for g in range(g0, g1):
                            mmg(pt[:, (g - g0) * RL:(g - g0) * RL + n],
                                [(TO[:, g, :], u8t[:, g, 0:n]), (CO[:, g, :], SA[:, g, 0:n])],
                                reads=[TOb, COb, u8b, SAb], writes=[pb])
                        pv = pt[:, 0:(g1 - g0) * RL].rearrange("p (g c) -> p g c", c=RL)
                        if gi % 2 == 0:
                            op("act", lambda e: e.activation(out=yt[:, g0:g1, 0:n], in_=pv[:, :, 0:n], func=AF.Copy), reads=[pb], writes=[yb])
                        else:
                            op("dve", lambda e: e.tensor_copy(out=yt[:, g0:g1, 0:n], in_=pv[:, :, 0:n]), reads=[pb], writes=[yb])
                    dma("act", y8D[d, :, :, c0:c1].rearrange("g p c -> p g c"), yt[:, :, 0:n], reads=[yb])
                barrier()

    def s5_stage(li):
        def s5_combine():
            if skip():
                return
            with contextlib.ExitStack() as st:
                dsk, dskb = kb.tile(st, [128, SW // 128], F32, "dsk")
                load_T(dsk[:], dskb, s5_d[li].rearrange("(k g) h -> k (g h)", g=8), SW // 128)
                NB_ = 4
                YA = [kb.tile(st, [128, 8, NC8], F32, "YA") for _ in range(2)]
                YB = [kb.tile(st, [128, 8, NC8], F32, "YB") for _ in range(2)]
                ut = [kb.tile(st, [128, 512], F32, "ut") for _ in range(NB_)]
                yt_ = [kb.tile(st, [128, 512], F32, "yt") for _ in range(NB_)]
                t5 = [kb.tile(st, [128, 512], F32, "t5") for _ in range(NB_)]
                go = [kb.tile(st, [128, 512], BF16, "go") for _ in range(NB_)]
                it = 0
                for k in range(SW // 128):
                    A_t, a_b = YA[k % 2]
                    B_t, b_b = YB[k % 2]
                    for g8 in range(8):
                        dma("sp" if g8 % 2 == 0 else "act", A_t[g8 * 16:(g8 + 1) * 16, :, :], y8D[0, k * 8 + g8, :, :].rearrange("(j h) c -> h j c", h=16), writes=[a_b])
                        dma("act" if g8 % 2 == 0 else "sp", B_t[g8 * 16:(g8 + 1) * 16, :, :], y8D[1, k * 8 + g8, :, :].rearrange("(j h) c -> h j c", h=16), writes=[b_b])
                    for (t0, tl) in tok_blocks(512):
                        i = it % NB_
                        it += 1
                        cA, nA = t0 // 8, tl // 8
                        a_t = A_t[:, :, cA:cA + nA]
                        b_t = B_t[:, :, cA:cA + nA]
                        u_t, u_b = ut[i]
                        y_t, y_b = yt_[i]
                        w_t, w_b = t5[i]
                        g_t, g_b = go[i]
                        dma("sp", u_t[:, 0:tl], uTD[k, :, t0:t0 + tl], writes=[u_b])
                        yv = y_t[:, 0:tl].rearrange("p (c j) -> p j c", j=8)
                        op("dve", lambda e: e.tensor_tensor(out=yv, in0=a_t, in1=b_t, op=ALU.add), reads=[a_b, b_b], writes=[y_b])
                        op("dve", lambda e: e.scalar_tensor_tensor(out=y_t[:, 0:tl], in0=u_t[:, 0:tl], scalar=dsk[:, k:k + 1], in1=y_t[:, 0:tl],
                                                                   op0=ALU.mult, op1=ALU.add), reads=[u_b, y_b, dskb], writes=[y_b])
                        dma("act", ytotT[k, :, t0:t0 + tl], y_t[:, 0:tl], reads=[y_b])
                        op("pool", lambda e: e.tensor_tensor(out=w_t[:, 0:tl], in0=y_t[:, 0:tl], in1=y_t[:, 0:tl], op=ALU.mult), reads=[y_b], writes=[w_b])
                        op("dve", lambda e: e.tensor_scalar(out=w_t[:, 0:tl], in0=w_t[:, 0:tl], scalar1=0.044715, scalar2=1.0, op0=ALU.mult, op1=ALU.add),
                           reads=[w_b], writes=[w_b])
                        op("dve", lambda e: e.tensor_tensor(out=w_t[:, 0:tl], in0=w_t[:, 0:tl], in1=y_t[:, 0:tl], op=ALU.mult), reads=[w_b, y_b], writes=[w_b])
                        op("act", lambda e: e.activation(out=w_t[:, 0:tl], in_=w_t[:, 0:tl], func=AF.Sigmoid, scale=1.5957691216057308), reads=[w_b], writes=[w_b])
                        op("dve", lambda e: e.tensor_tensor(out=g_t[:, 0:tl], in0=w_t[:, 0:tl], in1=y_t[:, 0:tl], op=ALU.mult), reads=[w_b, y_b], writes=[g_b])
                        dma("act", geluT[k, :, t0:t0 + tl], g_t[:, 0:tl], reads=[g_b])
                barrier()

        s5_combine()
        def setup(st):
            bg, bgb = kb.tile(st, [128, SW // 128], F32, "bglu")
            load_T(bg[:], bgb, b_glu[li:li + 1, :].rearrange("o (k p) -> (o k) p", p=128), SW // 128)
            return dict(bg=(bg, bgb), y=[kb.tile(st, [128, 512], F32, "gy") for _ in range(3)],
                        s=[kb.tile(st, [128, 512], F32, "gs") for _ in range(3)],
                        o=[kb.tile(st, [128, 512], BF16, "gout") for _ in range(3)], ctr=[0])

        def ep(ri, ci, pss, t0, sl, env):
            i = env["ctr"][0] % 3
            env["ctr"][0] += 1
            (pt, pb), = pss
            y_t, y_b = env["y"][i]
            s_t, s_b = env["s"][i]
            o_t, o_b = env["o"][i]
            bg, bgb = env["bg"]
            dma("sp", y_t[:, 0:sl], ytotT[ci, :, t0:t0 + sl], writes=[y_b])
            op("act", lambda e: e.activation(out=s_t[:, 0:sl], in_=pt[:, 0:sl], func=AF.Sigmoid, bias=bg[:, ci:ci + 1]), reads=[pb, bgb], writes=[s_b])
            op("dve", lambda e: e.tensor_tensor(out=o_t[:, 0:sl], in0=s_t[:, 0:sl], in1=y_t[:, 0:sl], op=ALU.mult), reads=[s_b, y_b], writes=[o_b])
            dma("pool", brT[1][ci, :, t0:t0 + sl], o_t[:, 0:sl], reads=[o_b])
        linear([geluT], [[(0, w_glu[li], 0)]], [(i * 128, 128) for i in range(SW // 128)], ep, tb=1024, setup=setup)

    def mixer_stage(li):
        norm_stage(xT, hT, D, gain_fn(1), shift_fn(1))
        Wi = w_in[li]
        def su_u(st):
            return dict(f=[kb.tile(st, [128, 512], F32, "uf") for _ in range(2)],
                        b=[kb.tile(st, [128, 8, 64], BF16, "ub") for _ in range(2)], ctr=[0])

        def ep_u(ri, ci, pss, t0, sl, env):
            i = env["ctr"][0] % 2
            env["ctr"][0] += 1
            (pt, pb), = pss
            f_t, f_b = env["f"][i]
            b_t, b_b = env["b"][i]
            nA = sl // 8
            op("act", lambda e: e.activation(out=f_t[:, 0:sl], in_=pt[:, 0:sl], func=AF.Copy), reads=[pb], writes=[f_b])
            dma("act", uTD[ci, :, t0:t0 + sl], f_t[:, 0:sl], reads=[f_b])
            op("dve", lambda e: e.tensor_copy(out=b_t[:, :, 0:nA], in_=pt[:, 0:sl].rearrange("p (c i) -> p i c", i=8)), reads=[pb], writes=[b_b])
            for g8 in range(8):
                dma("act", u8D[ci * 8 + g8, :, t0 // 8:t0 // 8 + nA].rearrange("(i h) c -> h i c", h=16), b_t[g8 * 16:(g8 + 1) * 16, :, 0:nA], reads=[b_b])
        su_cq, ep_cq = ep_store(cqT, F32)
        su_ckv, ep_ckv = ep_store(ckvT, F32)
        su_r, ep_kr = ep_rope(krT, 64, None)
        _, ep_gq = ep_rope(gqT, 128, 0)
        _, ep_gk = ep_rope(gkT, 128, 1)
        segs = [(OFF_CQ, QL // 128, 128, ep_cq, "cq"), (OFF_CKV, KVL // 128, 128, ep_ckv, "ckv"), (OFF_KR, 1, 64, ep_kr, "r"),
                (OFF_U, SW // 128, 128, ep_u, "u"), (OFF_GQ, HQ, 128, ep_gq, "r"), (OFF_GK, HKV, 128, ep_gk, "r")]
        chunks_all = []
        seg_of = []
        for si_, (o0, nch, m, _e, _k) in enumerate(segs):
            for i in range(nch):
                chunks_all.append((o0 + i * 128, m))
                seg_of.append((si_, i))

        def su_all(st):
            return dict(cq=su_cq(st), ckv=su_ckv(st), r=su_r(st), u=su_u(st))

        def ep_all(ri, ci, pss, t0, sl, env):
            si_, i = seg_of[ci]
            segs[si_][3](ri, i, pss, t0, sl, env[segs[si_][4]])
        linear([hT], [[(0, Wi, 0)]], chunks_all, ep_all, setup=su_all)
        def su_tm(st):
            return ([kb.tile(st, [128, 512], BF16, "tmo") for _ in range(3)], [0])

        def ep_gv(pt, pb, tok0, col0, w, env):
            tiles, ctr = env
            o_t, o_b = tiles[ctr[0] % 3]
            ctr[0] += 1
            op("act", lambda e: e.activation(out=o_t[:, 0:w], in_=pt[:, 0:w], func=AF.Copy), reads=[pb], writes=[o_b])
            dma("act", gvD[tok0:tok0 + 128, col0:col0 + w], o_t[:, 0:w], reads=[o_b])
        linear_tm(hT, Wi, OFF_GV, HKV * 128, ep_gv, setup=su_tm)
        gcq_t = {}

        def ld_gain(src_ap, n, key):
            def f(st_unused=None):
                pass
            return f
        with contextlib.ExitStack() as st:
            gq_t, gq_b = kb.tile(st, [128, QL // 128], F32, "gcq")
            gkv_t, gkv_b = kb.tile(st, [128, KVL // 128], F32, "gckv")
            load_T(gq_t[:], gq_b, g_cq[li:li + 1, :].rearrange("o (k p) -> (o k) p", p=128), QL // 128)
            load_T(gkv_t[:], gkv_b, g_ckv[li:li + 1, :].rearrange("o (k p) -> (o k) p", p=128), KVL // 128)
            barrier()
            norm_stage(cqT, cqnT, QL, lambda t0: (gq_t[:], gq_b), None)
            norm_stage(ckvT, ckvnT, KVL, lambda t0: (gkv_t[:], gkv_b), None)
        Wq = w_uq[li]
        su_n, ep_n = ep_store(qmnT, BF16)
        su_rr, ep_rr = ep_rope(qmrT, 64, None)
        chunks_q = []
        for h in range(HM):
            chunks_q += [(h * 192, 128), (h * 192 + 128, 64)]

        def su_q(st):
            return dict(n=su_n(st), r=su_rr(st))

        def ep_q(ri, ci, pss, t0, sl, env):
            if ci % 2 == 0:
                ep_n(ri, ci // 2, pss, t0, sl, env["n"])
            else:
                ep_rr(ri, ci // 2, pss, t0, sl, env["r"])
        linear([cqnT], [[(0, Wq, 0)]], chunks_q, ep_q, setup=su_q, wgw=384)
        Wkv = w_ukv[li]
        su, ep = ep_store(kmT, BF16)
        linear([ckvnT], [[(0, Wkv, 0)]], [(h * 256, 128) for h in range(HM)], ep, setup=su, wgw=128)

        def ep_vm(pt, pb, tok0, col0, w, env):
            tiles, ctr = env
            o_t, o_b = tiles[ctr[0] % 3]
            ctr[0] += 1
            op("act", lambda e: e.activation(out=o_t[:, 0:w], in_=pt[:, 0:w], func=AF.Copy), reads=[pb], writes=[o_b])
            for hh in range(w // 256):
                h = (col0 + hh * 256) // 256
                dma("act", vmD[tok0:tok0 + 128, h * 128:(h + 1) * 128], o_t[:, hh * 256 + 128:hh * 256 + 256], reads=[o_b])
        linear_tm(ckvnT, Wkv, 0, HM * 256, ep_vm, setup=su_tm)
        s5_phaseA(li)
        with contextlib.ExitStack() as bst:
            gen = s5_recur_gen(li, bst) if not skip() else iter(())
            done = [False]

            def bg(nsteps):
                if done[0]:
                    return
                for _ in range(nsteps):
                    try:
                        next(gen)
                    except StopIteration:
                        done[0] = True
                        return
            heads = [dict(q=[(qmnT[h], 128), (qmrT[h], 64)], k=[(kmT[h], 128), (krT[0], 64)], vcol=h * 128, out=brT[0][h]) for h in range(HM)]
            attention_stage(heads, vmD, 192 ** -0.5, bg=bg)
            grp = HQ // HKV
            heads = [dict(q=[(gqT[h], 128)], k=[(gkT[h // grp], 128)], vcol=(h // grp) * 128, out=brT[2][h]) for h in range(HQ)]
            attention_stage(heads, gvD, 128 ** -0.5, bg=bg)
            while not done[0]:
                bg(64)
            barrier()
        s5_phaseC(li)
        s5_stage(li)
        def su_m(st):
            bgt, bgb = kb.tile(st, [128, 3, KC], F32, "bgate")
            load_T(bgt[:].rearrange("p n k -> p (n k)"), bgb, b_gate[li].rearrange("n (k p) -> (n k) p", p=128), 3 * KC)
            return dict(bg=(bgt, bgb), s=[kb.tile(st, [128, 512], F32, "ms") for _ in range(2)],
                        acc=[kb.tile(st, [128, 512], F32, "macc") for _ in range(2)],
                        o=[kb.tile(st, [128, 512], BF16, "mo") for _ in range(2)], ctr=[0])

        def ep_m(ri, ci, pss, t0, sl, env):
            if ri == 0:
                env["ctr"][0] += 1
            i = env["ctr"][0] % 2
            (pg, pgb), (pbr, pbrb) = pss
            s_t, s_b = env["s"][i]
            a_t, a_b = env["acc"][i]
            o_t, o_b = env["o"][i]
            bgt, bgb = env["bg"]
            op("act", lambda e: e.activation(out=s_t[:, 0:sl], in_=pg[:, 0:sl], func=AF.Sigmoid, bias=bgt[:, ri, ci:ci + 1]), reads=[pgb, bgb], writes=[s_b])
            if ri == 0:
                op("dve", lambda e: e.tensor_tensor(out=a_t[:, 0:sl], in0=s_t[:, 0:sl], in1=pbr[:, 0:sl], op=ALU.mult), reads=[s_b, pbrb], writes=[a_b])
            else:
                op("dve", lambda e: e.tensor_tensor(out=s_t[:, 0:sl], in0=s_t[:, 0:sl], in1=pbr[:, 0:sl], op=ALU.mult), reads=[s_b, pbrb], writes=[s_b])
                if ri == 1:
                    op("dve", lambda e: e.tensor_tensor(out=a_t[:, 0:sl], in0=a_t[:, 0:sl], in1=s_t[:, 0:sl], op=ALU.add), reads=[s_b, a_b], writes=[a_b])
                else:
                    op("dve", lambda e: e.tensor_tensor(out=o_t[:, 0:sl], in0=a_t[:, 0:sl], in1=s_t[:, 0:sl], op=ALU.add), reads=[s_b, a_b], writes=[o_b])
                    dma("pool", ygT[ci, :, t0:t0 + sl], o_t[:, 0:sl], reads=[o_b])
        rounds = [[(0, w_gate[li, n], 0), (1 + n, w_branch[li, n], 0)] for n in range(3)]
        linear([hT, brT[0], brT[1], brT[2]], rounds, [(i * 128, 128) for i in range(KC)], ep_m, tb=1024, wgw=256, setup=su_m)
        su, epr = ep_residual(1)
        linear([ygT], [[(0, w_o[li], 0)]], [(i * 128, 128) for i in range(KC)], epr, tb=1024, setup=su)

    def final_stage():
        if skip():
            return
        with contextlib.ExitStack() as st:
            fg, fgb = kb.tile(st, [128, KC], F32, "fg")
            load_T(fg[:], fgb, final_g.rearrange("o (k p) -> (o k) p", p=128), KC)
            xs = [kb.tile(st, [128, KC, 512], F32, "fx") for _ in range(2)]
            sq = [kb.tile(st, [128, 512], F32, "fsq") for _ in range(3)]
            rs = [kb.tile(st, [128, 512], F32, "frs") for _ in range(2)]
            yn = [kb.tile(st, [128, 512], F32, "fyn") for _ in range(3)]
            ob = [kb.tile(st, [128, D], F32, "fob") for _ in range(2)]
            oi = 0
            for bi, t0 in enumerate(range(LC, T, 512)):
                tl = 512
                xt, xb = xs[bi % 2]
                dma("sp", xt[:], xT[:, :, t0:t0 + tl].rearrange("k p t -> p k t"), writes=[xb])
                pt, pb = next_ps()
                for k in range(KC):
                    sqt, sqb = sq[k % 3]
                    op("act", lambda e: e.activation(out=sqt[:], in_=xt[:, k, :], func=AF.Square), reads=[xb], writes=[sqb])
                    mmg(pt[:, 0:tl], [(mats[:, ONESF, :], sqt[:])], reads=[sqb, mats_b], writes=[pb], start=(k == 0), stop=(k == KC - 1))
                rt, rb = rs[bi % 2]
                op("act", lambda e: e.activation(out=rt[:], in_=pt[:, 0:tl], func=AF.Sqrt, bias=epsT[:, 0:1], scale=1.0 / D), reads=[pb, epsT_b], writes=[rb])
                op("dve", lambda e: e.reciprocal(out=rt[:], in_=rt[:]), reads=[rb], writes=[rb])
                for q in range(4):
                    ot, obf = ob[oi % 2]
                    oi += 1
                    for k in range(KC):
                        yt, yb = yn[k % 3]
                        op("dve", lambda e: e.scalar_tensor_tensor(out=yt[:, 0:128], in0=xt[:, k, q * 128:(q + 1) * 128], scalar=fg[:, k:k + 1],
                                                                   in1=rt[:, q * 128:(q + 1) * 128], op0=ALU.mult, op1=ALU.mult),
                           reads=[xb, rb, fgb], writes=[yb])
                        p2, p2b = next_ps()
                        op("pe", lambda e: e.transpose(p2[:, 0:128], yt[:, 0:128], mats[:, IDENT, :]), reads=[yb, mats_b], writes=[p2b])
                        op("act", lambda e: e.activation(out=ot[:, k * 128:(k + 1) * 128], in_=p2[:, 0:128], func=AF.Copy), reads=[p2b], writes=[obf])
                    r0 = t0 - LC + q * 128
                    dma("act", y_out[r0:r0 + 128, :], ot[:], reads=[obf])
            barrier()

    input_stage()
    for li in range(DEPTH):
        mod_stage(li)
        ffn_stage(li, 0, 0)
        mixer_stage(li)
        ffn_stage(li, 1, 2)
    final_stage()
    barrier()
    es.close()
    return nc


def rope_tables(L, GW, d_rot, LC):
    half = d_rot // 2
    freqs = (10000.0 ** (-np.arange(0, half, 2, dtype=np.float32) / np.float32(half))).astype(np.float32)
    t = np.arange(L)
    rows = (t // GW).astype(np.float32)
    cols = (t % GW).astype(np.float32)

    def tab(pos):
        ang = pos[:, None] * freqs[None, :]
        ang = np.concatenate([ang, ang], axis=-1)
        return np.cos(ang), np.sin(ang)
    cr, sr = tab(rows)
    cc, sc = tab(cols)
    cos = np.concatenate([cr, cc], axis=-1).astype(np.float32)
    sin = np.concatenate([sr, sc], axis=-1).astype(np.float32)
    cosT = np.concatenate([np.ones((d_rot, LC), np.float32), cos.T], axis=1)
    sinT = np.concatenate([np.zeros((d_rot, LC), np.float32), sin.T], axis=1)
    return np.ascontiguousarray(cosT), np.ascontiguousarray(sinT)


def rot_matrix(d_rot):
    half = d_rot // 2
    q = half // 2
    R = np.zeros((128, 128), np.float32)
    for b in (0, half):
        for j in range(q):
            R[b + q + j, b + j] = -1.0
            R[b + j, b + q + j] = 1.0
    return R


def host_constants(cfg):
    L, LC, GW = cfg["L"], cfg["LC"], cfg["GW"]
    c64, s64 = rope_tables(L, GW, 64, LC)
    c128, s128 = rope_tables(L, GW, 128, LC)
    mats = np.zeros((7, 128, 128), np.float32)
    mats[0] = np.eye(128, dtype=np.float32)
    mats[1] = rot_matrix(128)
    mats[2] = rot_matrix(64)
    sw = np.zeros((128, 128), np.float32)
    for p in range(64):
        sw[64 + p, p] = 1.0
        sw[p, 64 + p] = 1.0
    mats[3] = sw
    mf = np.zeros((128, 128), np.float32)
    mr = np.zeros((128, 128), np.float32)
    for i in range(8):
        for j in range(8):
            if j >= i:
                mf[i * 16:(i + 1) * 16, j * 16:(j + 1) * 16] = 1.0
            if j <= i:
                mr[i * 16:(i + 1) * 16, j * 16:(j + 1) * 16] = 1.0
    mats[4] = mf
    mats[5] = mr
    mats[6] = 1.0
    jf = np.tile(np.arange(-7, 9, dtype=np.float32)[None, :], (128, 1))
    return dict(k_cos64=c64, k_sin64=s64, k_cos128=c128, k_sin128=s128, k_mats=mats, k_jf=np.ascontiguousarray(jf))


_PROG_CACHE = {}


def run(cfg, inputs, ncores=NCORES, raw=False):
    key = tuple(sorted((k, str(v)) for k, v in cfg.items()))
    if key not in _PROG_CACHE:
        _PROG_CACHE[key] = build_program(cfg)
    nc = _PROG_CACHE[key]
    consts = host_constants(cfg)
    f32 = lambda a: np.ascontiguousarray(np.asarray(a, dtype=np.float32))
    shared = {k: f32(v) for k, v in inputs.items() if k not in ("x", "c", "ctx", "c_ctx", "final_g")}
    shared["c_ctx"] = f32(inputs["c_ctx"]).reshape(1, -1)
    shared["final_g"] = f32(inputs["final_g"]).reshape(1, -1)
    shared.update(consts)
    x = f32(inputs["x"])
    c = f32(inputs["c"])
    ctx = f32(inputs["ctx"])
    in_maps = []
    for b in range(ncores):
        m = dict(shared)
        m["x"] = x[b]
        m["c"] = c[b:b + 1]
        m["ctx"] = ctx[b]
        in_maps.append(m)
    res = run_bass_kernel_spmd(nc, in_maps, core_ids=list(range(ncores)))
    if raw:
        return res.results
    return np.stack([np.asarray(r["y"]) for r in res.results], axis=0)


def kernel(**inputs):
    return run(FULL, inputs)
```

```python
import math
import contextlib
import numpy as np
import concourse.bass as bass
import concourse.mybir as mybir
from concourse.bass_utils import run_bass_kernel_spmd

F32 = mybir.dt.float32
BF16 = mybir.dt.bfloat16
AF = mybir.ActivationFunctionType
ALU = mybir.AluOpType

FULL = dict(D=2048, L=4096, LC=256, DFF=5632, HM=8, QL=1536, KVL=512, G=64, HQ=8, HKV=2, GW=64, DEPTH=2)
EPS = 1e-6
NCORES = 8


class Buf:
    __slots__ = ("w", "r", "excl")

    def __init__(self, excl=False):
        self.w = None
        self.r = {}
        self.excl = excl


def _split(reads, writes):
    ex = [b for b in reads if b.excl]
    if not ex:
        return reads, writes
    return [b for b in reads if not b.excl], list(writes) + ex


class Eng:
    def __init__(self, name, handle, sem, dsems):
        self.name = name
        self.h = handle
        self.sem = sem
        self.count = 0
        self.seen = {}
        self.dsems = dsems
        self.dvals = [0] * len(dsems)
        self.di = 0


class KB:
    def __init__(self, nc, es):
        self.nc = nc
        self.es = es
        self.engs = {}
        for name, h, nd in (("pe", nc.tensor, 0), ("act", nc.scalar, 8), ("dve", nc.vector, 0),
                            ("pool", nc.gpsimd, 24), ("sp", nc.sync, 24)):
            sem = es.enter_context(nc.semaphore("s_" + name))
            ds = [es.enter_context(nc.semaphore(f"d_{name}{i}")) for i in range(nd)]
            self.engs[name] = Eng(name, h, sem, ds)
        self.uid = 0

    def _waits(self, eng, reads, writes, same_sync):
        need = {}

        def add(tok):
            s, v = tok
            if (s is eng.sem) and not same_sync:
                return
            k = id(s)
            if k not in need or need[k][1] < v:
                need[k] = (s, v)
        for b in reads:
            if b.w is not None:
                add(b.w)
        for b in writes:
            if b.w is not None:
                add(b.w)
            for tok in b.r.values():
                add(tok)
        for k, (s, v) in need.items():
            if eng.seen.get(k, 0) < v:
                eng.h.wait_ge(s, v)
                eng.seen[k] = v

    def _mark(self, tok, reads, writes):
        for b in writes:
            b.w = tok
            b.r = {}
        k = id(tok[0])
        for b in reads:
            b.r[k] = tok

    def op(self, en, fn, reads=(), writes=()):
        reads, writes = _split(reads, writes)
        eng = self.engs[en]
        self._waits(eng, reads, writes, same_sync=(en != "pe"))
        ins = fn(eng.h)
        eng.count += 1
        ins.then_inc(eng.sem, 1)
        self._mark((eng.sem, eng.count), reads, writes)

    def mmg(self, out_ap, pairs, reads, writes, start=True, stop=True):
        reads, writes = _split(reads, writes)
        eng = self.engs["pe"]
        self._waits(eng, reads, writes, same_sync=False)
        n = len(pairs)
        ins = None
        for i, (l, r) in enumerate(pairs):
            ins = eng.h.matmul(out_ap, l, r, start=(start and i == 0), stop=(stop and i == n - 1))
        eng.count += 1
        ins.then_inc(eng.sem, 1)
        self._mark((eng.sem, eng.count), reads, writes)

    def dma(self, qn, out_ap, in_ap, reads=(), writes=(), slow=False):
        eng = self.engs[qn]
        i = eng.di
        eng.di = (i + 1) % len(eng.dsems)
        s = eng.dsems[i]
        pv = eng.dvals[i]
        if pv > 0 and eng.seen.get(id(s), 0) < pv:
            eng.h.wait_ge(s, pv)
            eng.seen[id(s)] = pv
        self._waits(eng, reads, writes, same_sync=True)
        if slow:
            ins = eng.h.dma_start(out=out_ap, in_=in_ap, allow_slow_non_contiguous=True)
        else:
            ins = eng.h.dma_start(out=out_ap, in_=in_ap)
        ins.then_inc(s, 16)
        eng.dvals[i] = pv + 16
        self._mark((s, pv + 16), reads, writes)

    def barrier(self):
        toks = []
        for e in self.engs.values():
            if e.count:
                toks.append((e.sem, e.count))
            for s, v in zip(e.dsems, e.dvals):
                if v:
                    toks.append((s, v))
        for e in self.engs.values():
            for s, v in toks:
                if e.seen.get(id(s), 0) < v:
                    e.h.wait_ge(s, v)
                    e.seen[id(s)] = v

    def tile(self, st, shape, dt, name=None):
        self.uid += 1
        t = st.enter_context(self.nc.sbuf_tensor(f"{name or 't'}_{self.uid}", list(shape), dt))
        return t, Buf()


def ceil_div(a, b):
    return (a + b - 1) // b


def build_program(cfg):
    D, L, LC, DFF = cfg["D"], cfg["L"], cfg["LC"], cfg["DFF"]
    HM, QL, KVL, G, HQ, HKV, GW, DEPTH = (cfg[k] for k in ("HM", "QL", "KVL", "G", "HQ", "HKV", "GW", "DEPTH"))
    T = L + LC
    KC = D // 128
    FC = DFF // 128
    SW = G * 16
    BW = SW
    assert HM * 128 == BW and HQ * 128 == BW
    DIN = QL + KVL + 64 + SW + HQ * 128 + 2 * HKV * 128
    OFF_CQ, OFF_CKV, OFF_KR = 0, QL, QL + KVL
    OFF_U = OFF_KR + 64
    OFF_GQ = OFF_U + SW
    OFF_GK = OFF_GQ + HQ * 128
    OFF_GV = OFF_GK + HKV * 128
    NC8 = T // 8
    NMODC = 9 * KC

    nc = bass.Bass("TRN2", target_bir_lowering=False)

    def din(name, shape, dt=F32):
        return nc.dram_tensor(name, list(shape), dt, kind="ExternalInput").ap()

    DBG = cfg.get("dbg", ())
    STOP = cfg.get("stop", 10 ** 9)
    unit = [0]

    def skip():
        unit[0] += 1
        return unit[0] > STOP or unit[0] < cfg.get('start', 0)

    def dscr(name, shape, dt):
        if name in DBG:
            return nc.dram_tensor(name, list(shape), dt, kind="ExternalOutput").ap()
        return nc.dram_tensor(name, list(shape), dt).ap()

    x_in = din("x", [L, D])
    c_in = din("c", [1, D])
    ctx_in = din("ctx", [LC, D])
    cctx_in = din("c_ctx", [1, D])
    w_mod = din("w_mod", [DEPTH, D, 9 * D])
    b_mod = din("b_mod", [DEPTH, 9 * D])
    norm_g = din("norm_g", [DEPTH, 3, D])
    w_up = din("w_ffn_up", [DEPTH, 2, D, 2 * DFF])
    w_down = din("w_ffn_down", [DEPTH, 2, DFF, D])
    w_in = din("w_in", [DEPTH, D, DIN])
    g_cq = din("mla_g_cq", [DEPTH, QL])
    g_ckv = din("mla_g_ckv", [DEPTH, KVL])
    w_uq = din("mla_w_uq", [DEPTH, QL, HM * 192])
    w_ukv = din("mla_w_ukv", [DEPTH, KVL, HM * 256])
    g_q = din("gqa_g_q", [DEPTH, 128])
    g_k = din("gqa_g_k", [DEPTH, 128])
    lam_re = din("s5_lam_re", [DEPTH, 2, G, 64])
    lam_im = din("s5_lam_im", [DEPTH, 2, G, 64])
    log_dt = din("s5_log_dt", [DEPTH, 2, G])
    b_re = din("s5_b_re", [DEPTH, 2, G, 64, 16])
    b_im = din("s5_b_im", [DEPTH, 2, G, 64, 16])
    c_re = din("s5_c_re", [DEPTH, 2, G, 16, 64])
    c_im = din("s5_c_im", [DEPTH, 2, G, 16, 64])
    s5_d = din("s5_d", [DEPTH, G, 16])
    w_glu = din("s5_w_glu", [DEPTH, SW, SW])
    b_glu = din("s5_b_glu", [DEPTH, SW])
    w_gate = din("w_gate", [DEPTH, 3, D, D])
    b_gate = din("b_gate", [DEPTH, 3, D])
    w_branch = din("w_branch", [DEPTH, 3, BW, D])
    w_o = din("w_o", [DEPTH, D, D])
    final_g = din("final_g", [1, D])
    k_cos64 = din("k_cos64", [64, T])
    k_sin64 = din("k_sin64", [64, T])
    k_cos128 = din("k_cos128", [128, T])
    k_sin128 = din("k_sin128", [128, T])
    k_mats = din("k_mats", [7, 128, 128])
    k_jf = din("k_jf", [128, 16])
    y_out = nc.dram_tensor("y", [L, D], F32, kind="ExternalOutput").ap()

    xT = dscr("xT", [KC, 128, T], F32)
    hT = dscr("hT", [KC, 128, T], BF16)
    actT = dscr("actT", [FC, 128, T], BF16)
    scT = dscr("scT", [KC, 128, 2], BF16)
    cqT = dscr("cqT", [QL // 128, 128, T], F32)
    cqnT = dscr("cqnT", [QL // 128, 128, T], BF16)
    ckvT = dscr("ckvT", [KVL // 128, 128, T], F32)
    ckvnT = dscr("ckvnT", [KVL // 128, 128, T], BF16)
    qmnT = dscr("qmnT", [HM, 128, T], BF16)
    qmrT = dscr("qmrT", [HM, 64, T], BF16)
    kmT = dscr("kmT", [HM, 128, T], BF16)
    krT = dscr("krT", [1, 64, T], BF16)
    vmD = dscr("vmD", [T, HM * 128], BF16)
    u8D = dscr("u8D", [G, 128, NC8], BF16)
    uTD = dscr("uTD", [G // 8, 128, T], F32)
    gqT = dscr("gqT", [HQ, 128, T], BF16)
    gkT = dscr("gkT", [HKV, 128, T], BF16)
    gvD = dscr("gvD", [T, HKV * 128], BF16)
    y8D = dscr("y8D", [2, G, 128, NC8], F32)
    brT = [dscr(f"brT{n}", [BW // 128, 128, T], BF16) for n in range(3)]
    ygT = dscr("ygT", [KC, 128, T], BF16)
    geluT = dscr("geluT", [SW // 128, 128, T], BF16)
    S5RL = 64
    S5NR = (T // 8 + S5RL - 1) // S5RL + 2
    vvD = dscr("vvD", [2, S5NR, 128, 2 * G * S5RL], F32)
    saD = dscr("saD", [2, S5NR, 128, G * S5RL], BF16)
    wmD = dscr("wmD", [2, 2, 128, G * 128], BF16)
    ytotT = dscr("ytotT", [SW // 128, 128, T], F32)

    es = contextlib.ExitStack()
    kb = KB(nc, es)
    op, mmg, dma, barrier = kb.op, kb.mmg, kb.dma, kb.barrier

    PS = []
    for i in range(8):
        t = es.enter_context(nc.psum_tensor(f"ps{i}", [128, 512], F32))
        PS.append((t, Buf(excl=True)))
    ps_rr = [0]

    def next_ps():
        i = ps_rr[0]
        ps_rr[0] = (i + 1) % 8
        return PS[i]

    mats, mats_b = kb.tile(es, [128, 7, 128], F32, "mats")
    onesb, onesb_b = kb.tile(es, [128, 128], BF16, "onesb")
    matsb, matsb_b = kb.tile(es, [128, 2, 128], BF16, "matsb")
    modT, modT_b = kb.tile(es, [128, NMODC, 2], F32, "modT")
    bmodT, bmodT_b = kb.tile(es, [128, NMODC], F32, "bmodT")
    ngT, ngT_b = kb.tile(es, [128, 3, KC], F32, "ngT")
    gains, gains_b = kb.tile(es, [128, 3, 2, KC], F32, "gains")
    shifts, shifts_b = kb.tile(es, [128, 3, 2, KC], F32, "shifts")
    rgate, rgate_b = kb.tile(es, [128, 3, 2, KC], F32, "rgate")
    smallv, smallv_b = kb.tile(es, [128, 8], F32, "smallv")
    epsT, epsT_b = kb.tile(es, [128, 1], F32, "epsT")
    scS, scS_b = kb.tile(es, [128, KC, 2], BF16, "scS")
    s5tab, s5tab_b = kb.tile(es, [128, 2, 2, 2, G], F32, "s5tab")
    IDENT, R128, R64, SWAPN, MASKF, MASKR, ONESF = range(7)

    dma("sp", mats[:], k_mats.rearrange("m p q -> p m q"), writes=[mats_b])
    op("dve", lambda e: e.memset(onesb[:], 1.0), writes=[onesb_b])
    op("dve", lambda e: e.memset(epsT[:], EPS), writes=[epsT_b])
    op("dve", lambda e: e.tensor_copy(out=matsb[:], in_=mats[:, 1:3, :]), reads=[mats_b], writes=[matsb_b])
    barrier()

    ldtmp = [kb.tile(es, [128, 128], F32, "ldtmp") for _ in range(2)]
    ldctr = [0]

    def load_T(dst_ap, dst_buf, rows_ap, n, dup64=False):
        tt, tb_ = ldtmp[ldctr[0] % 2]
        ldctr[0] += 1
        if dup64:
            dma("sp", tt[0:n, 0:64], rows_ap, writes=[tb_])
            dma("sp", tt[0:n, 64:128], rows_ap, writes=[tb_])
        else:
            dma("sp", tt[0:n, :], rows_ap, writes=[tb_])
        pt, pb = next_ps()
        op("pe", lambda e: e.transpose(pt[:, 0:n], tt[0:n, :], mats[0:n, IDENT, 0:n]), reads=[tb_, mats_b], writes=[pb])
        op("dve", lambda e: e.tensor_copy(out=dst_ap, in_=pt[:, 0:n]), reads=[pb], writes=[dst_buf])

    def stream_of(t0):
        return 1 if t0 < LC else 0

    def tok_blocks(maxlen):
        blks = [(0, LC)] if LC <= maxlen else [(i, min(maxlen, LC - i)) for i in range(0, LC, maxlen)]
        for i in range(LC, T, maxlen):
            blks.append((i, min(maxlen, T - i)))
        return blks

    def sub_tiles(tl):
        return [(s, min(512, tl - s)) for s in range(0, tl, 512)]

    def linear(srcs, rounds, chunks, epilogue, tb=1024, wgw=512, setup=None, nbufw=2, tblocks=None, sb_srcs=None):
        if skip():
            return
        with contextlib.ExitStack() as st:
            kcs = [s.shape[0] for s in srcs]
            src_bytes = sum(k * tb * 2 for k in kcs)
            nsb = 2 if src_bytes * 2 <= 72 * 1024 else 1
            sb = [[kb.tile(st, [128, k, tb], BF16, f"src{i}") for _ in range(nsb)] for i, k in enumerate(kcs)]
            groups = []
            cur = []
            for ci, (off, m) in enumerate(chunks):
                if cur and (off + m - chunks[cur[0]][0] > wgw or off != chunks[cur[-1]][0] + chunks[cur[-1]][1]):
                    groups.append(cur)
                    cur = []
                cur.append(ci)
            if cur:
                groups.append(cur)
            terms = [t for r in rounds for t in r]
            wt = {}
            for ti, (si, W, cb) in enumerate(terms):
                wt[ti] = [kb.tile(st, [128, kcs[si], wgw], BF16, f"w{ti}") for _ in range(nbufw)]
            env = setup(st) if setup else None
            blocks = tblocks if tblocks is not None else tok_blocks(tb)
            wctr = 0
            for bi, (t0, tl) in enumerate(blocks):
                cur_sb = []
                for i, s in enumerate(srcs):
                    if sb_srcs and i in sb_srcs:
                        cur_sb.append(sb_srcs[i])
                        continue
                    tl_, b_ = sb[i][bi % nsb]
                    dma("sp", tl_[:, :, 0:tl], s[:, :, t0:t0 + tl].rearrange("k p t -> p k t"), writes=[b_])
                    cur_sb.append((tl_, b_))
                for grp in groups:
                    g0 = chunks[grp[0]][0]
                    g1 = chunks[grp[-1]][0] + chunks[grp[-1]][1]
                    wcur = {}
                    ti = 0
                    for r in rounds:
                        for (si, W, cb) in r:
                            wtile, wb = wt[ti][wctr % nbufw]
                            dma("pool", wtile[:, :, 0:g1 - g0],
                                W[:, cb + g0:cb + g1].rearrange("(k p) n -> p k n", p=128), writes=[wb])
                            wcur[ti] = (wtile, wb)
                            ti += 1
                    wctr += 1
                    for ci in grp:
                        off, m = chunks[ci]
                        for (s0, sl) in sub_tiles(tl):
                            ti = 0
                            for ri, r in enumerate(rounds):
                                pss = []
                                for (si, W, cb) in r:
                                    wtile, wb = wcur[ti]
                                    stile, sbf = cur_sb[si]
                                    pt, pb = next_ps()
                                    mmg(pt[0:m, 0:sl],
                                        [(wtile[:, k, off - g0:off - g0 + m], stile[:, k, s0:s0 + sl])
                                         for k in range(kcs[si])],
                                        reads=[wb, sbf], writes=[pb])
                                    pss.append((pt, pb))
                                    ti += 1
                                epilogue(ri, ci, pss, t0 + s0, sl, env)
            barrier()

    def linear_tm(src, W, col_base, ncols, epilogue, setup=None):
        if skip():
            return
        with contextlib.ExitStack() as st:
            kc = src.shape[0]
            tb = 512
            sb = [kb.tile(st, [128, kc, tb], BF16, "srctm") for _ in range(2)]
            wtile, wb = kb.tile(st, [128, kc, ncols], BF16, "wtm")
            env = setup(st) if setup else None
            dma("pool", wtile[:], W[:, col_base:col_base + ncols].rearrange("(k p) n -> p k n", p=128), writes=[wb])
            for bi, (t0, tl) in enumerate(tok_blocks(tb)):
                stile, sbf = sb[bi % 2]
                dma("sp", stile[:, :, 0:tl], src[:, :, t0:t0 + tl].rearrange("k p t -> p k t"), writes=[sbf])
                for q in range(tl // 128):
                    for c0 in range(0, ncols, 512):
                        w = min(512, ncols - c0)
                        pt, pb = next_ps()
                        mmg(pt[:, 0:w], [(stile[:, k, q * 128:(q + 1) * 128], wtile[:, k, c0:c0 + w]) for k in range(kc)],
                            reads=[wb, sbf], writes=[pb])
                        epilogue(pt, pb, t0 + q * 128, c0, w, env)
            barrier()

    def ep_store(dst, dt, rowmap=None, func=AF.Copy):
        def setup(st):
            return [kb.tile(st, [128, 512], dt, "stg") for _ in range(3)], [0]

        def ep(ri, ci, pss, t0, sl, env):
            stg, ctr = env
            (pt, pb), = pss
            tl_, b_ = stg[ctr[0] % 3]
            ctr[0] += 1
            ch, m = rowmap(ci) if rowmap else (ci, 128)
            op("act", lambda e: e.activation(out=tl_[0:m, 0:sl], in_=pt[0:m, 0:sl], func=func), reads=[pb], writes=[b_])
            dma("act", dst[ch, 0:m, t0:t0 + sl], tl_[0:m, 0:sl], reads=[b_])
        return setup, ep

    def norm_stage(src, dst, nfeat, gain_ap_fn, shift_ap_fn, blocks=None):
        if skip():
            return
        kc = src.shape[0]
        with contextlib.ExitStack() as st:
            xs = [kb.tile(st, [128, kc, 512], F32, "nx") for _ in range(2)]
            sq = [kb.tile(st, [128, 512], BF16, "nsq") for _ in range(4)]
            rs = [kb.tile(st, [128, 512], F32, "nrs") for _ in range(2)]
            tmp = [kb.tile(st, [128, 512], F32, "ntmp") for _ in range(3)]
            ob = [kb.tile(st, [128, kc, 512], BF16, "nob") for _ in range(2)]
            for bi, (t0, tl) in enumerate(blocks or tok_blocks(512)):
                xt, xb = xs[bi % 2]
                dma("sp", xt[:, :, 0:tl], src[:, :, t0:t0 + tl].rearrange("k p t -> p k t"), writes=[xb])
                pt, pb = next_ps()
                for k in range(kc):
                    sqt, sqb = sq[k % 4]
                    if k % 3 == 2:
                        op("pool", lambda e: e.tensor_tensor(out=sqt[:, 0:tl], in0=xt[:, k, 0:tl], in1=xt[:, k, 0:tl], op=ALU.mult),
                           reads=[xb], writes=[sqb])
                    else:
                        op("act", lambda e: e.activation(out=sqt[:, 0:tl], in_=xt[:, k, 0:tl], func=AF.Square),
                           reads=[xb], writes=[sqb])
                    mmg(pt[:, 0:tl], [(onesb[:, :], sqt[:, 0:tl])], reads=[sqb, onesb_b], writes=[pb],
                        start=(k == 0), stop=(k == kc - 1))
                rt, rb = rs[bi % 2]
                op("act", lambda e: e.activation(out=rt[:, 0:tl], in_=pt[:, 0:tl], func=AF.Sqrt,
                                                 bias=epsT[:, 0:1], scale=1.0 / nfeat), reads=[pb, epsT_b], writes=[rb])
                op("dve", lambda e: e.reciprocal(out=rt[:, 0:tl], in_=rt[:, 0:tl]), reads=[rb], writes=[rb])
                ot, obf = ob[bi % 2]
                gain = gain_ap_fn(t0)
                shift = shift_ap_fn(t0) if shift_ap_fn else None
                for k in range(kc):
                    tt, tbf = tmp[k % 3]
                    op("dve", lambda e: e.scalar_tensor_tensor(out=tt[:, 0:tl], in0=xt[:, k, 0:tl], scalar=gain[0][:, k:k + 1],
                                                               in1=rt[:, 0:tl], op0=ALU.mult, op1=ALU.mult),
                       reads=[xb, rb, gain[1]], writes=[tbf])
                    if shift is not None:
                        op("act", lambda e: e.activation(out=ot[:, k, 0:tl], in_=tt[:, 0:tl], func=AF.Identity,
                                                         bias=shift[0][:, k:k + 1]), reads=[tbf, shift[1]], writes=[obf])
                    else:
                        op("act", lambda e: e.activation(out=ot[:, k, 0:tl], in_=tt[:, 0:tl], func=AF.Copy),
                           reads=[tbf], writes=[obf])
                dma("act", dst[:, :, t0:t0 + tl].rearrange("k p t -> p k t"), ot[:, :, 0:tl], reads=[obf])
            barrier()

    def input_stage():
        if skip():
            return
        with contextlib.ExitStack() as st:
            xin = [kb.tile(st, [128, D], F32, "xin") for _ in range(2)]
            xo = [kb.tile(st, [128, KC, 128], F32, "xo") for _ in range(2)]
            for bi in range(T // 128):
                t0 = bi * 128
                it, ib = xin[bi % 2]
                srcap = ctx_in[t0:t0 + 128, :] if t0 < LC else x_in[t0 - LC:t0 - LC + 128, :]
                dma("sp", it[:], srcap, writes=[ib])
                ot, obf = xo[bi % 2]
                for k in range(KC):
                    pt, pb = next_ps()
                    op("pe", lambda e: e.transpose(pt[:, 0:128], it[:, k * 128:(k + 1) * 128], mats[:, IDENT, :]),
                       reads=[ib, mats_b], writes=[pb])
                    if k % 2 == 0:
                        op("dve", lambda e: e.tensor_copy(out=ot[:, k, :], in_=pt[:, 0:128]), reads=[pb], writes=[obf])
                    else:
                        op("act", lambda e: e.activation(out=ot[:, k, :], in_=pt[:, 0:128], func=AF.Copy), reads=[pb], writes=[obf])
                dma("act", xT[:, :, t0:t0 + 128].rearrange("k p t -> p k t"), ot[:], reads=[obf])
            if cfg.get('iv') == 1:
                barrier()
                return
            ct, cb_ = kb.tile(st, [128, KC, 2], F32, "ct")
            load_T(ct[:, :, 0], cb_, c_in.rearrange("o (k p) -> (o k) p", p=128), KC)
            load_T(ct[:, :, 1], cb_, cctx_in.rearrange("o (k p) -> (o k) p", p=128), KC)
            op("act", lambda e: e.activation(out=scS[:], in_=ct[:], func=AF.Silu), reads=[cb_], writes=[scS_b])
            barrier()

    def mod_stage(li):
        if skip():
            return
        bm_rows = b_mod[li:li + 1, :].rearrange("o (n p) -> (o n) p", p=128)
        for r0 in range(0, NMODC, 72):
            r1 = min(NMODC, r0 + 72)
            load_T(bmodT[:, r0:r1], bmodT_b, bm_rows[r0:r1, :], r1 - r0)
        load_T(ngT[:].rearrange("p j k -> p (j k)"), ngT_b, norm_g[li].rearrange("j (k p) -> (j k) p", p=128), 3 * KC)
        load_T(smallv[:, 0:1], smallv_b, g_q[li:li + 1, :], 1)
        load_T(smallv[:, 1:2], smallv_b, g_k[li:li + 1, :], 1)

        def ep(ri, ci, pss, t0, sl, env):
            (pt, pb), = pss
            op("dve", lambda e: e.tensor_scalar(out=modT[:, ci, :], in0=pt[:, 0:2], scalar1=bmodT[:, ci:ci + 1], scalar2=None,
                                                op0=ALU.add), reads=[pb, bmodT_b], writes=[modT_b])
        linear([scT], [[(0, w_mod[li], 0)]], [(i * 128, 128) for i in range(NMODC)], ep, tb=2, tblocks=[(0, 2)], nbufw=3, sb_srcs={0: (scS, scS_b)})
        for j in range(3):
            for s in range(2):
                sc = modT[:, (3 * j + 1) * KC:(3 * j + 2) * KC, s]
                sh = modT[:, (3 * j) * KC:(3 * j + 1) * KC, s]
                gt = modT[:, (3 * j + 2) * KC:(3 * j + 3) * KC, s]
                op("dve", lambda e: e.scalar_tensor_tensor(out=gains[:, j, s, :], in0=sc, scalar=1.0, in1=ngT[:, j, :],
                                                           op0=ALU.add, op1=ALU.mult), reads=[modT_b, ngT_b], writes=[gains_b])
                op("dve", lambda e: e.tensor_copy(out=shifts[:, j, s, :], in_=sh), reads=[modT_b], writes=[shifts_b])
                op("dve", lambda e: e.tensor_scalar(out=rgate[:, j, s, :], in0=gt, scalar1=(1.0 if j == 1 else 0.5), scalar2=None,
                                                    op0=ALU.mult), reads=[modT_b], writes=[rgate_b])
        barrier()

    def gain_fn(j):
        return lambda t0: (gains[:, j, stream_of(t0), :], gains_b)

    def shift_fn(j):
        return lambda t0: (shifts[:, j, stream_of(t0), :], shifts_b)

    def ep_residual(j):
        def setup(st):
            return ([kb.tile(st, [128, 512], F32, "rx") for _ in range(3)], [0])

        def ep(ri, ci, pss, t0, sl, env):
            xs, ctr = env
            (pt, pb), = pss
            xt, xb = xs[ctr[0] % 3]
            ctr[0] += 1
            dma("sp", xt[:, 0:sl], xT[ci, :, t0:t0 + sl], writes=[xb])
            s = stream_of(t0)
            op("dve", lambda e: e.scalar_tensor_tensor(out=xt[:, 0:sl], in0=pt[:, 0:sl], scalar=rgate[:, j, s, ci:ci + 1],
                                                       in1=xt[:, 0:sl], op0=ALU.mult, op1=ALU.add),
               reads=[pb, xb, rgate_b], writes=[xb])
            dma("act", xT[ci, :, t0:t0 + sl], xt[:, 0:sl], reads=[xb])
        return setup, ep

    def ffn_stage(li, which, j):
        norm_stage(xT, hT, D, gain_fn(j), shift_fn(j))
        Wu = w_up[li, which]
        Wd = w_down[li, which]

        def setup(st):
            return ([kb.tile(st, [128, 512], F32, "sg") for _ in range(3)],
                    [kb.tile(st, [128, 512], BF16, "ao") for _ in range(3)], [0])

        def ep(ri, ci, pss, t0, sl, env):
            sgs, aos, ctr = env
            (pg, pgb), (pu, pub) = pss
            sg, sgb = sgs[ctr[0] % 3]
            ao, aob = aos[ctr[0] % 3]
            ctr[0] += 1
            op("act", lambda e: e.activation(out=sg[:, 0:sl], in_=pg[:, 0:sl], func=AF.Silu), reads=[pgb], writes=[sgb])
            op("dve", lambda e: e.tensor_tensor(out=ao[:, 0:sl], in0=sg[:, 0:sl], in1=pu[:, 0:sl], op=ALU.mult),
               reads=[sgb, pub], writes=[aob])
            dma("act", actT[ci, :, t0:t0 + sl], ao[:, 0:sl], reads=[aob])
        linear([hT], [[(0, Wu, 0), (0, Wu, DFF)]], [(i * 128, 128) for i in range(FC)], ep, tb=1024, wgw=512, setup=setup)
        su, epr = ep_residual(j)
        linear([actT], [[(0, Wd, 0)]], [(i * 128, 128) for i in range(KC)], epr, tb=1024, wgw=256, setup=su)

    def ep_rope(dst, rows, norm_col, rowmap=None):
        cosD, sinD = (k_cos128, k_sin128) if rows == 128 else (k_cos64, k_sin64)
        RM = R128 if rows == 128 else R64

        def setup(st):
            return dict(q=[kb.tile(st, [128, 512], F32, "rq") for _ in range(4)],
                        cs=[kb.tile(st, [128, 2, 512], F32, "rcs") for _ in range(4)],
                        t=[kb.tile(st, [128, 512], F32, "rt") for _ in range(4)],
                        qh=[kb.tile(st, [128, 512], BF16, "rqh") for _ in range(4)],
                        sqh=[kb.tile(st, [128, 512], BF16, "rsqh") for _ in range(4)],
                        r=[kb.tile(st, [128, 512], F32, "rr") for _ in range(4)],
                        o=[kb.tile(st, [128, 512], BF16, "ro") for _ in range(4)], ctr=[0])

        def ep(ri, ci, pss, t0, sl, env):
            i = env["ctr"][0] % 4
            env["ctr"][0] += 1
            (pt, pb), = pss
            q, qb = env["q"][i]
            cs, csb = env["cs"][i]
            tt, tb_ = env["t"][i]
            rr, rrb = env["r"][i]
            o, ob_ = env["o"][i]
            qh, qhb = env["qh"][i]
            sqh, sqhb = env["sqh"][i]
            ch = rowmap(ci) if rowmap else ci
            dma("sp", cs[0:rows, 0, 0:sl], cosD[:, t0:t0 + sl], writes=[csb])
            dma("sp", cs[0:rows, 1, 0:sl], sinD[:, t0:t0 + sl], writes=[csb])
            if norm_col is not None:
                op("act", lambda e: e.activation(out=sqh[0:rows, 0:sl], in_=pt[0:rows, 0:sl], func=AF.Square), reads=[pb], writes=[sqhb])
                p2, p2b = next_ps()
                mmg(p2[0:rows, 0:sl], [(onesb[0:rows, 0:rows], sqh[0:rows, 0:sl])], reads=[sqhb, onesb_b], writes=[p2b])
                op("act", lambda e: e.activation(out=rr[0:rows, 0:sl], in_=p2[0:rows, 0:sl], func=AF.Sqrt, bias=epsT[0:rows, 0:1],
                                                 scale=1.0 / rows), reads=[p2b, epsT_b], writes=[rrb])
                op("dve", lambda e: e.reciprocal(out=rr[0:rows, 0:sl], in_=rr[0:rows, 0:sl]), reads=[rrb], writes=[rrb])
                op("dve", lambda e: e.scalar_tensor_tensor(out=qh[0:rows, 0:sl], in0=pt[0:rows, 0:sl],
                                                           scalar=smallv[0:rows, norm_col:norm_col + 1], in1=rr[0:rows, 0:sl],
                                                           op0=ALU.mult, op1=ALU.mult), reads=[pb, rrb, smallv_b], writes=[qhb])
            else:
                op("act", lambda e: e.activation(out=qh[0:rows, 0:sl], in_=pt[0:rows, 0:sl], func=AF.Copy), reads=[pb], writes=[qhb])
            p3, p3b = next_ps()
            mmg(p3[0:rows, 0:sl], [(matsb[0:rows, RM - 1, 0:rows], qh[0:rows, 0:sl])], reads=[qhb, matsb_b], writes=[p3b])
            op("pool", lambda e: e.tensor_tensor(out=q[0:rows, 0:sl], in0=qh[0:rows, 0:sl], in1=cs[0:rows, 0, 0:sl], op=ALU.mult),
               reads=[qhb, csb], writes=[qb])
            op("dve", lambda e: e.tensor_tensor(out=tt[0:rows, 0:sl], in0=p3[0:rows, 0:sl], in1=cs[0:rows, 1, 0:sl], op=ALU.mult),
               reads=[p3b, csb], writes=[tb_])
            op("dve", lambda e: e.tensor_tensor(out=o[0:rows, 0:sl], in0=q[0:rows, 0:sl], in1=tt[0:rows, 0:sl], op=ALU.add),
               reads=[qb, tb_], writes=[ob_])
            dma("pool", dst[ch, 0:rows, t0:t0 + sl], o[0:rows, 0:sl], reads=[ob_])
        return setup, ep

    def attention_stage(heads, Vd, scale, bg=None):
        if skip():
            return
        nkc = T // 128
        with contextlib.ExitStack() as st:
            kt = [[kb.tile(st, [128, T], BF16, "ak") for _ in range(2)] for _ in range(2)]
            vt = [kb.tile(st, [128, nkc, 128], BF16, "av") for _ in range(2)]
            qt = [[kb.tile(st, [128, 512], BF16, "aq") for _ in range(2)] for _ in range(2)]
            pts = [kb.tile(st, [128, 512], BF16, "ap") for _ in range(4)]
            rd = [kb.tile(st, [128, 512], F32, "ard") for _ in range(2)]
            ot = [kb.tile(st, [128, 512], BF16, "ao") for _ in range(2)]
            qblocks = tok_blocks(512)
            item = 0
            pctr = 0
            for hi, hd in enumerate(heads):
                nk = len(hd["k"])
                for j, (kap, rows) in enumerate(hd["k"]):
                    dma("sp", kt[j][hi % 2][0][0:rows, :], kap, writes=[kt[j][hi % 2][1]])
                vtile, vb = vt[hi % 2]
                vc = hd["vcol"]
                dma("sp", vtile[:], Vd[:, vc:vc + 128].rearrange("(c p) d -> p c d", p=128), writes=[vb])
                for (q0, ql) in qblocks:
                    nkeys = LC if q0 < LC else T
                    for j, (qap, rows) in enumerate(hd["q"]):
                        dma("sp", qt[j][item % 2][0][0:rows, 0:ql], qap[:, q0:q0 + ql], writes=[qt[j][item % 2][1]])
                    pso, psob = PS[4 + (item % 2)]
                    psd, psdb = PS[6 + (item % 2)]
                    ncks = nkeys // 128

                    def score(c, slot):
                        pt, pb = PS[slot]
                        pairs = []
                        rds = []
                        for j, (kap, rows) in enumerate(hd["k"]):
                            pairs.append((kt[j][hi % 2][0][0:rows, c * 128:(c + 1) * 128], qt[j][item % 2][0][0:rows, 0:ql]))
                            rds += [kt[j][hi % 2][1], qt[j][item % 2][1]]
                        mmg(pt[:, 0:ql], pairs, reads=rds, writes=[pb])
                    score(0, pctr % 4)
                    for c in range(ncks):
                        if c + 1 < ncks:
                            score(c + 1, (pctr + 1) % 4)
                        pt, pb = PS[pctr % 4]
                        ptile, ptb = pts[pctr % 4]
                        pctr += 1
                        op("act", lambda e: e.activation(out=ptile[:, 0:ql], in_=pt[:, 0:ql], func=AF.Exp, scale=scale),
                           reads=[pb], writes=[ptb])
                        mmg(pso[:, 0:ql], [(vtile[:, c, :], ptile[:, 0:ql])], reads=[vb, ptb], writes=[psob],
                            start=(c == 0), stop=(c == ncks - 1))
                        mmg(psd[:, 0:ql], [(onesb[:, :], ptile[:, 0:ql])], reads=[onesb_b, ptb], writes=[psdb],
                            start=(c == 0), stop=(c == ncks - 1))
                    rt, rb = rd[item % 2]
                    o, ob_ = ot[item % 2]
                    op("dve", lambda e: e.reciprocal(out=rt[:, 0:ql], in_=psd[:, 0:ql]), reads=[psdb], writes=[rb])
                    op("dve", lambda e: e.tensor_tensor(out=o[:, 0:ql], in0=pso[:, 0:ql], in1=rt[:, 0:ql], op=ALU.mult),
                       reads=[psob, rb], writes=[ob_])
                    dma("pool", hd["out"][:, q0:q0 + ql], o[:, 0:ql], reads=[ob_])
                    item += 1
                    if bg is not None:
                        bg(8)
            barrier()

    def s5_prep(li, d, st):
        Wm = {}
        for nm in ("toep", "bin", "bins", "cout"):
            Wm[nm] = kb.tile(st, [128, G, 128], BF16, "s5" + nm)
        ARR, ARb = kb.tile(st, [128, 2, G], F32, "s5ARR")
        AXX, AXb = kb.tile(st, [128, 2, G], F32, "s5AXX")
        AXsb = AXb
        AR = ARR[:, 0, :]
        AX = AXX[:, 0, :]
        AXs = AXX[:, 1, :]
        with contextlib.ExitStack() as s2:
            def tl(shape, name):
                return kb.tile(s2, shape, F32, name)
            lr, lrb = tl([128, G], "lr")
            li_, lib = tl([128, G], "li")
            dt, dtb = tl([128, G], "dt")
            jf, jfb = tl([128, 16], "jf")
            dma("sp", jf[:], k_jf, writes=[jfb])
            load_T(lr[:], lrb, lam_re[li, d], G, dup64=True)
            load_T(li_[:], lib, lam_im[li, d], G, dup64=True)
            dma("sp", dt[:], log_dt[li, d:d + 1, :].broadcast_to([128, G]), writes=[dtb])
            op("dve", lambda e: e.tensor_scalar(out=lr[:], in0=lr[:], scalar1=-1e-4, scalar2=None, op0=ALU.min), reads=[lrb], writes=[lrb])
            op("act", lambda e: e.activation(out=dt[:], in_=dt[:], func=AF.Exp), reads=[dtb], writes=[dtb])
            ld, ldb = tl([128, G], "ld")
            an, anb = tl([128, G], "an")
            op("dve", lambda e: e.tensor_tensor(out=ld[:], in0=lr[:], in1=dt[:], op=ALU.mult), reads=[lrb, dtb], writes=[ldb])
            op("dve", lambda e: e.tensor_tensor(out=an[:], in0=li_[:], in1=dt[:], op=ALU.mult), reads=[lib, dtb], writes=[anb])
            mg, mgb = tl([128, G, 16], "mg")
            ag, agb = tl([128, G, 16], "ag")
            PR, PRb = tl([128, G, 16], "PR")
            PI, PIb = tl([128, G, 16], "PI")
            kk, kkb = tl([128, G, 16], "kk")
            ki = kb.tile(s2, [128, G, 16], mybir.dt.int32, "ki")
            for g in range(G):
                op("dve", lambda e: e.tensor_scalar(out=mg[:, g, :], in0=jf[:], scalar1=ld[:, g:g + 1], scalar2=None, op0=ALU.mult),
                   reads=[jfb, ldb], writes=[mgb])
                op("pool", lambda e: e.tensor_scalar(out=ag[:, g, :], in0=jf[:], scalar1=an[:, g:g + 1], scalar2=None, op0=ALU.mult),
                   reads=[jfb, anb], writes=[agb])
            op("act", lambda e: e.activation(out=mg[:], in_=mg[:], func=AF.Exp), reads=[mgb], writes=[mgb])

            def sin_of(out_t, out_b, shift):
                TWO_PI = 2.0 * math.pi
                op("dve", lambda e: e.tensor_scalar(out=kk[:], in0=ag[:], scalar1=shift, scalar2=1.0 / TWO_PI, op0=ALU.add, op1=ALU.mult),
                   reads=[agb], writes=[kkb])
                op("dve", lambda e: e.tensor_copy(out=ki[0][:], in_=kk[:]), reads=[kkb], writes=[ki[1]])
                op("dve", lambda e: e.tensor_copy(out=kk[:], in_=ki[0][:]), reads=[ki[1]], writes=[kkb])
                op("dve", lambda e: e.scalar_tensor_tensor(out=kk[:], in0=kk[:], scalar=-TWO_PI, in1=ag[:], op0=ALU.mult, op1=ALU.add),
                   reads=[kkb, agb], writes=[kkb])
                op("dve", lambda e: e.tensor_scalar(out=kk[:], in0=kk[:], scalar1=shift, scalar2=None, op0=ALU.add), reads=[kkb], writes=[kkb])
                op("dve", lambda e: e.tensor_scalar(out=out_t[:], in0=kk[:], scalar1=math.pi, scalar2=-TWO_PI, op0=ALU.is_gt, op1=ALU.mult),
                   reads=[kkb], writes=[out_b])
                op("dve", lambda e: e.tensor_tensor(out=kk[:], in0=kk[:], in1=out_t[:], op=ALU.add), reads=[kkb, out_b], writes=[kkb])
                op("dve", lambda e: e.tensor_scalar(out=out_t[:], in0=kk[:], scalar1=-math.pi, scalar2=TWO_PI, op0=ALU.is_lt, op1=ALU.mult),
                   reads=[kkb], writes=[out_b])
                op("dve", lambda e: e.tensor_tensor(out=kk[:], in0=kk[:], in1=out_t[:], op=ALU.add), reads=[kkb, out_b], writes=[kkb])
                op("dve", lambda e: e.tensor_scalar(out=kk[:], in0=kk[:], scalar1=math.pi, scalar2=-math.pi, op0=ALU.min, op1=ALU.max),
                   reads=[kkb], writes=[kkb])
                op("act", lambda e: e.activation(out=out_t[:], in_=kk[:], func=AF.Sin), reads=[kkb], writes=[out_b])
            sin_of(PI, PIb, 0.0)
            sin_of(PR, PRb, math.pi / 2)
            op("dve", lambda e: e.tensor_tensor(out=PR[:], in0=PR[:], in1=mg[:], op=ALU.mult), reads=[PRb, mgb], writes=[PRb])
            op("dve", lambda e: e.tensor_tensor(out=PI[:], in0=PI[:], in1=mg[:], op=ALU.mult), reads=[PIb, mgb], writes=[PIb])

            def P(e_):
                return e_ + 7
            op("dve", lambda e: e.tensor_copy(out=ARR[:, 0, :], in_=PR[:, :, P(8)]), reads=[PRb], writes=[ARb])
            op("dve", lambda e: e.tensor_copy(out=ARR[:, 1, :], in_=PR[:, :, P(8)]), reads=[PRb], writes=[ARb])
            op("dve", lambda e: e.tensor_scalar(out=AXX[0:64, 0, :], in0=PI[0:64, :, P(8)], scalar1=-1.0, scalar2=None, op0=ALU.mult),
               reads=[PIb], writes=[AXb])
            op("dve", lambda e: e.tensor_copy(out=AXX[64:128, 0, :], in_=PI[64:128, :, P(8)]), reads=[PIb], writes=[AXb])
            op("dve", lambda e: e.tensor_scalar(out=AXX[:, 1, :], in0=AXX[:, 0, :], scalar1=-1.0, scalar2=None, op0=ALU.mult), reads=[AXb], writes=[AXb])
            den, denb = tl([128, G], "den")
            t1, t1b = tl([128, G], "t1")
            fr, frb = tl([128, G], "fr")
            fi, fib = tl([128, G], "fi")
            nr, nrb = tl([128, G], "nr")
            op("dve", lambda e: e.tensor_tensor(out=den[:], in0=lr[:], in1=lr[:], op=ALU.mult), reads=[lrb], writes=[denb])
            op("dve", lambda e: e.tensor_tensor(out=t1[:], in0=li_[:], in1=li_[:], op=ALU.mult), reads=[lib], writes=[t1b])
            op("dve", lambda e: e.tensor_tensor(out=den[:], in0=den[:], in1=t1[:], op=ALU.add), reads=[denb, t1b], writes=[denb])
            op("dve", lambda e: e.reciprocal(out=den[:], in_=den[:]), reads=[denb], writes=[denb])
            op("dve", lambda e: e.tensor_scalar(out=nr[:], in0=PR[:, :, P(1)], scalar1=-1.0, scalar2=None, op0=ALU.add), reads=[PRb], writes=[nrb])
            op("dve", lambda e: e.tensor_tensor(out=fr[:], in0=nr[:], in1=lr[:], op=ALU.mult), reads=[nrb, lrb], writes=[frb])
            op("dve", lambda e: e.tensor_tensor(out=t1[:], in0=PI[:, :, P(1)], in1=li_[:], op=ALU.mult), reads=[PIb, lib], writes=[t1b])
            op("dve", lambda e: e.tensor_tensor(out=fr[:], in0=fr[:], in1=t1[:], op=ALU.add), reads=[frb, t1b], writes=[frb])
            op("dve", lambda e: e.tensor_tensor(out=fr[:], in0=fr[:], in1=den[:], op=ALU.mult), reads=[frb, denb], writes=[frb])
            op("dve", lambda e: e.tensor_tensor(out=fi[:], in0=PI[:, :, P(1)], in1=lr[:], op=ALU.mult), reads=[PIb, lrb], writes=[fib])
            op("dve", lambda e: e.tensor_tensor(out=t1[:], in0=nr[:], in1=li_[:], op=ALU.mult), reads=[nrb, lib], writes=[t1b])
            op("dve", lambda e: e.tensor_tensor(out=fi[:], in0=fi[:], in1=t1[:], op=ALU.subtract), reads=[fib, t1b], writes=[fib])
            op("dve", lambda e: e.tensor_tensor(out=fi[:], in0=fi[:], in1=den[:], op=ALU.mult), reads=[fib, denb], writes=[fib])
            br_, brb = tl([128, G, 16], "br")
            bi_, bib = tl([128, G, 16], "bi")
            cr_, crb = tl([128, G, 16], "cr")
            ci_, cib = tl([128, G, 16], "ci")
            for h in range(2):
                for g0 in range(0, G, 8):
                    g1 = min(G, g0 + 8)
                    dma("sp", br_[h * 64:(h + 1) * 64, g0:g1, :], b_re[li, d, g0:g1].rearrange("g p h -> p g h"), writes=[brb])
                    dma("sp", bi_[h * 64:(h + 1) * 64, g0:g1, :], b_im[li, d, g0:g1].rearrange("g p h -> p g h"), writes=[bib])
            for g0 in range(0, G, 8):
                g1 = min(G, g0 + 8)
                nr_ = (g1 - g0) * 16
                load_T(cr_[:, g0:g1, :].rearrange("p g h -> p (g h)"), crb, c_re[li, d, g0:g1].rearrange("g h p -> (g h) p"), nr_, dup64=True)
                load_T(ci_[:, g0:g1, :].rearrange("p g h -> p (g h)"), cib, c_im[li, d, g0:g1].rearrange("g h p -> (g h) p"), nr_, dup64=True)
            bbr, bbrb = kk, kkb
            bbi, bbib = tl([128, G, 16], "bbi")
            t3, t3b = mg, mgb
            for g in range(G):
                e1 = "dve" if g % 2 == 0 else "pool"
                op(e1, lambda e: e.tensor_scalar(out=bbr[:, g, :], in0=br_[:, g, :], scalar1=fr[:, g:g + 1], scalar2=None, op0=ALU.mult),
                   reads=[brb, frb], writes=[bbrb])
                op(e1, lambda e: e.tensor_scalar(out=t3[:, g, :], in0=bi_[:, g, :], scalar1=fi[:, g:g + 1], scalar2=None, op0=ALU.mult),
                   reads=[bib, fib], writes=[t3b])
                op(e1, lambda e: e.tensor_scalar(out=bbi[:, g, :], in0=bi_[:, g, :], scalar1=fr[:, g:g + 1], scalar2=None, op0=ALU.mult),
                   reads=[bib, frb], writes=[bbib])
                op(e1, lambda e: e.tensor_scalar(out=br_[:, g, :], in0=br_[:, g, :], scalar1=fi[:, g:g + 1], scalar2=None, op0=ALU.mult),
                   reads=[brb, fib], writes=[brb])
            op("dve", lambda e: e.tensor_tensor(out=bbr[:], in0=bbr[:], in1=t3[:], op=ALU.subtract), reads=[bbrb, t3b], writes=[bbrb])
            op("dve", lambda e: e.tensor_tensor(out=bbi[:], in0=bbi[:], in1=br_[:], op=ALU.add), reads=[bbib, brb], writes=[bbib])
            if d == 0:
                eX = [-i for i in range(8)]
                eY = [j for j in range(8)]
                eB = [7 - i for i in range(8)]
                eC = [j + 1 for j in range(8)]
                MK = MASKF
            else:
                eX = [i - 7 for i in range(8)]
                eY = [7 - j for j in range(8)]
                eB = [i for i in range(8)]
                eC = [8 - j for j in range(8)]
                MK = MASKR
            GH = max(1, G // 2)
            X, Xb = tl([128, GH, 8, 16], "X")
            Y, Yb = tl([128, GH, 8, 16], "Y")
            t4, t4b = ag, agb

            def cmul_rows(out_t, out_b, i, pw, ur, ui, urb, uib, sign_im_rows, g0):
                prb = PR[:, g0:g0 + GH, pw:pw + 1].broadcast_to([128, GH, 16])
                pib = PI[:, g0:g0 + GH, pw:pw + 1].broadcast_to([128, GH, 16])
                o = out_t[:, :, i, :]
                ur_ = ur[:, g0:g0 + GH, :]
                ui_ = ui[:, g0:g0 + GH, :]
                t4_ = t4[:, 0:GH, :]
                op("dve", lambda e: e.tensor_tensor(out=o[0:64], in0=ur_[0:64], in1=prb[0:64], op=ALU.mult), reads=[urb, PRb], writes=[out_b])
                op("dve", lambda e: e.tensor_tensor(out=t4_[0:64], in0=ui_[0:64], in1=pib[0:64], op=ALU.mult), reads=[uib, PIb], writes=[t4b])
                op("dve", lambda e: e.tensor_tensor(out=o[0:64], in0=o[0:64], in1=t4_[0:64], op=ALU.subtract), reads=[out_b, t4b], writes=[out_b])
                op("dve", lambda e: e.tensor_tensor(out=o[64:128], in0=ui_[64:128], in1=prb[64:128], op=ALU.mult), reads=[uib, PRb], writes=[out_b])
                op("dve", lambda e: e.tensor_tensor(out=t4_[64:128], in0=ur_[64:128], in1=pib[64:128], op=ALU.mult), reads=[urb, PIb], writes=[t4b])
                op("dve", lambda e: e.tensor_tensor(out=o[64:128], in0=o[64:128], in1=t4_[64:128], op=ALU.add), reads=[out_b, t4b], writes=[out_b])
                if sign_im_rows < 0:
                    op("dve", lambda e: e.tensor_scalar(out=o[64:128], in0=o[64:128], scalar1=-1.0, scalar2=None, op0=ALU.mult),
                       reads=[out_b], writes=[out_b])
            for g0 in range(0, G, GH):
                for i in range(8):
                    cmul_rows(X, Xb, i, P(eX[i]), bbr, bbi, bbrb, bbib, +1, g0)
                    cmul_rows(Y, Yb, i, P(eY[i]), cr_, ci_, crb, cib, -1, g0)
                for gg in range(GH):
                    g = g0 + gg
                    xg = X[:, gg].rearrange("p i h -> p (i h)")
                    yg = Y[:, gg].rearrange("p i h -> p (i h)")
                    p1, p1b = next_ps()
                    mmg(p1[:, 0:128], [(xg, yg)], reads=[Xb, Yb], writes=[p1b])
                    op("dve", lambda e: e.tensor_tensor(out=Wm["toep"][0][:, g, :], in0=p1[:, 0:128], in1=mats[:, MK, :], op=ALU.mult),
                       reads=[p1b, mats_b], writes=[Wm["toep"][1]])
                for i in range(8):
                    cmul_rows(X, Xb, i, P(eB[i]), bbr, bbi, bbrb, bbib, +1, g0)
                    cmul_rows(Y, Yb, i, P(eC[i]), cr_, ci_, crb, cib, -1, g0)
                op("act", lambda e: e.activation(out=Wm["cout"][0][:, g0:g0 + GH, :], in_=Y[:].rearrange("p g i h -> p g (i h)"), func=AF.Copy),
                   reads=[Yb], writes=[Wm["cout"][1]])
                for gg in range(GH):
                    g = g0 + gg
                    xbg = X[:, gg].rearrange("p i h -> p (i h)")
                    p2, p2b = next_ps()
                    mmg(p2[:, 0:128], [(xbg, mats[:, IDENT, :])], reads=[Xb, mats_b], writes=[p2b])
                    op("act", lambda e: e.activation(out=Wm["bin"][0][:, g, :], in_=p2[:, 0:128], func=AF.Copy), reads=[p2b], writes=[Wm["bin"][1]])
                    p3, p3b = next_ps()
                    mmg(p3[:, 0:128], [(xbg, mats[:, SWAPN, :])], reads=[Xb, mats_b], writes=[p3b])
                    op("act", lambda e: e.activation(out=Wm["bins"][0][:, g, :], in_=p3[:, 0:128], func=AF.Copy), reads=[p3b], writes=[Wm["bins"][1]])
            barrier()
        return Wm, (ARR, ARb), (AXX, AXb), (None, None)

    def vv_dma(q, tile_ap, buf, dview, n, load):
        if n == S5RL:
            flat = tile_ap.rearrange("p s g c -> p (s g c)")
            if load:
                dma(q, flat, dview, writes=[buf])
            else:
                dma(q, dview, flat, reads=[buf])
            return
        dv = dview.rearrange("p (s g c) -> p s g c", s=2, g=G)
        for sl_ in range(2):
            if load:
                dma(q, tile_ap[:, sl_, :, 0:n], dv[:, sl_, :, 0:n], writes=[buf])
            else:
                dma(q, dv[:, sl_, :, 0:n], tile_ap[:, sl_, :, 0:n], reads=[buf])

    def s5_runs(d):
        RL = S5RL
        if d == 0:
            return [(c0, min(NC8, c0 + RL)) for c0 in range(0, NC8, RL)]
        cc = LC // 8
        runs = [(c0, min(cc, c0 + RL)) for c0 in reversed(range(0, cc, RL))]
        c1 = NC8
        while c1 > cc:
            c0 = max(cc, c1 - RL)
            runs.append((c0, c1))
            c1 = c0
        return runs

    def s5_phaseA(li):
        RL = S5RL
        GB = max(1, 512 // RL)
        for d in range(2):
            if skip():
                continue
            with contextlib.ExitStack() as st:
                Wm, (ARR, ARb), (AXX, AXb), _unused = s5_prep(li, d, st)
                op("dve", lambda e: e.tensor_copy(out=s5tab[:, d, 0], in_=ARR[:]), reads=[ARb], writes=[s5tab_b])
                op("dve", lambda e: e.tensor_copy(out=s5tab[:, d, 1], in_=AXX[:]), reads=[AXb], writes=[s5tab_b])
                dma("act", wmD[d, 0], Wm["toep"][0][:].rearrange("p g m -> p (g m)"), reads=[Wm["toep"][1]])
                dma("act", wmD[d, 1], Wm["cout"][0][:].rearrange("p g m -> p (g m)"), reads=[Wm["cout"][1]])
                runs = s5_runs(d)
                U8 = [kb.tile(st, [128, G, RL], BF16, "U8") for _ in range(2)]
                VVs = [kb.tile(st, [128, 2, G, RL], F32, "VV") for _ in range(2)]
                for ri, (c0, c1) in enumerate(runs):
                    n = c1 - c0
                    u8t, u8b = U8[ri % 2]
                    VV, Vb = VVs[ri % 2]
                    dma("sp", u8t[:, :, 0:n], u8D[:, :, c0:c1].rearrange("g p c -> p g c"), writes=[u8b])
                    for g0 in range(0, G, GB):
                        g1 = min(G, g0 + GB)
                        for slot, nm in ((0, "bin"), (1, "bins")):
                            pt, pb = next_ps()
                            for g in range(g0, g1):
                                mmg(pt[:, (g - g0) * RL:(g - g0) * RL + n], [(Wm[nm][0][:, g, :], u8t[:, g, 0:n])],
                                    reads=[Wm[nm][1], u8b], writes=[pb])
                            pv = pt[:, 0:(g1 - g0) * RL].rearrange("p (g c) -> p g c", c=RL)
                            if slot == 0:
                                op("act", lambda e: e.activation(out=VV[:, slot, g0:g1, 0:n], in_=pv[:, :, 0:n], func=AF.Copy), reads=[pb], writes=[Vb])
                            else:
                                op("dve", lambda e: e.tensor_copy(out=VV[:, slot, g0:g1, 0:n], in_=pv[:, :, 0:n]), reads=[pb], writes=[Vb])
                    vv_dma("act", VV[:], Vb, vvD[d, ri], n, False)
                barrier()

    def s5_recur_gen(li, st):
        RL = S5RL
        VVs = [kb.tile(st, [128, 2, G, RL], F32, "rVV") for _ in range(2)]
        SAs = [kb.tile(st, [128, G, RL], BF16, "rSA") for _ in range(2)]
        X = [kb.tile(st, [128, 2, G], F32, "rX") for _ in range(2)]
        ta, tab = kb.tile(st, [128, 2, G], F32, "rta")
        tb2, tb2b = kb.tile(st, [128, 2, G], F32, "rtb")

        def _gen():
          for d in range(2):
              runs = s5_runs(d)
              ARR = s5tab[:, d, 0]
              AXX = s5tab[:, d, 1]
              cur = 0
              op("dve", lambda e: e.memset(X[0][0][:], 0.0), writes=[X[0][1]])
              n0 = runs[0][1] - runs[0][0]
              vv_dma("pool", VVs[0][0][:], VVs[0][1], vvD[d, 0], n0, True)
              for ri, (c0, c1) in enumerate(runs):
                  n = c1 - c0
                  VV, Vb = VVs[ri % 2]
                  SA, SAb = SAs[ri % 2]
                  if ri + 1 < len(runs):
                      n1 = runs[ri + 1][1] - runs[ri + 1][0]
                      vv_dma("pool", VVs[(ri + 1) % 2][0][:], VVs[(ri + 1) % 2][1], vvD[d, ri + 1], n1, True)
                  order = range(n) if d == 0 else range(n - 1, -1, -1)
                  for c in order:
                      x_t, x_b = X[cur]
                      n_t, n_b = X[1 - cur]
                      op("pool", lambda e: e.tensor_copy(out=SA[:, :, c], in_=x_t[:, 0, :]), reads=[x_b], writes=[SAb])
                      op("dve", lambda e: e.tensor_tensor(out=ta[:], in0=ARR, in1=x_t[:], op=ALU.mult), reads=[s5tab_b, x_b], writes=[tab])
                      op("dve", lambda e: e.tensor_tensor(out=tb2[:], in0=AXX, in1=x_t[:, ::-1, :], op=ALU.mult), reads=[s5tab_b, x_b], writes=[tb2b])
                      op("dve", lambda e: e.tensor_tensor(out=ta[:], in0=ta[:], in1=tb2[:], op=ALU.add), reads=[tab, tb2b], writes=[tab])
                      op("dve", lambda e: e.tensor_tensor(out=n_t[:], in0=ta[:], in1=VV[:, :, :, c], op=ALU.add), reads=[tab, Vb], writes=[n_b])
                      cur = 1 - cur
                      yield
                  dma("pool", saD[d, ri].rearrange("p (g c) -> p g c", g=G)[:, :, 0:n], SA[:, :, 0:n], reads=[SAb])
              if cur == 1:
                  pass
        return _gen()

    def s5_phaseC(li):
        RL = S5RL
        GB = max(1, 512 // RL)
        for d in range(2):
            if skip():
                continue
            with contextlib.ExitStack() as st:
                TO, TOb = kb.tile(st, [128, G, 128], BF16, "cTO")
                CO, COb = kb.tile(st, [128, G, 128], BF16, "cCO")
                dma("sp", TO[:].rearrange("p g m -> p (g m)"), wmD[d, 0], writes=[TOb])
                dma("sp", CO[:].rearrange("p g m -> p (g m)"), wmD[d, 1], writes=[COb])
                runs = s5_runs(d)
                U8 = [kb.tile(st, [128, G, RL], BF16, "cU8") for _ in range(2)]
                SAs = [kb.tile(st, [128, G, RL], BF16, "cSA") for _ in range(2)]
                YO = [kb.tile(st, [128, G, RL], F32, "cYO") for _ in range(2)]
                for ri, (c0, c1) in enumerate(runs):
                    n = c1 - c0
                    u8t, u8b = U8[ri % 2]
                    SA, SAb = SAs[ri % 2]
                    yt, yb = YO[ri % 2]
                    dma("sp", u8t[:, :, 0:n], u8D[:, :, c0:c1].rearrange("g p c -> p g c"), writes=[u8b])
                    dma("sp", SA[:, :, 0:n], saD[d, ri].rearrange("p (g c) -> p g c", g=G)[:, :, 0:n], writes=[SAb])
                    for gi, g0 in enumerate(range(0, G, GB)):
                        g1 = min(G, g0 + GB)
                        pt, pb = next_ps()
                        for g in range(g0, g1):
                            mmg(pt[:, (g - g0) * RL:(g - g0) * RL + n],
                                [(TO[:, g, :], u8t[:, g, 0:n]), (CO[:, g, :], SA[:, g, 0:n])],
                                reads=[TOb, COb, u8b, SAb], writes=[pb])
                        pv = pt[:, 0:(g1 - g0) * RL].rearrange("p (g c) -> p g c", c=RL)
                        if gi % 2 == 0:
                            op("act", lambda e: e.activation(out=yt[:, g0:g1, 0:n], in_=pv[:, :, 0:n], func=AF.Copy), reads=[pb], writes=[yb])
                        else:
                            op("dve", lambda e: e.tensor_copy(out=yt[:, g0:g1, 0:n], in_=pv[:, :, 0:n]), reads=[pb], writes=[yb])
                    dma("act", y8D[d, :, :, c0:c1].rearrange("g p c -> p g c"), yt[:, :, 0:n], reads=[yb])
                barrier()

    def s5_stage(li):
        def s5_combine():
            if skip():
                return
            with contextlib.ExitStack() as st:
                dsk, dskb = kb.tile(st, [128, SW // 128], F32, "dsk")
                load_T(dsk[:], dskb, s5_d[li].rearrange("(k g) h -> k (g h)", g=8), SW // 128)
                NB_ = 4
                YA = [kb.tile(st, [128, 8, NC8], F32, "YA") for _ in range(2)]
                YB = [kb.tile(st, [128, 8, NC8], F32, "YB") for _ in range(2)]
                ut = [kb.tile(st, [128, 512], F32, "ut") for _ in range(NB_)]
                yt_ = [kb.tile(st, [128, 512], F32, "yt") for _ in range(NB_)]
                t5 = [kb.tile(st, [128, 512], F32, "t5") for _ in range(NB_)]
                go = [kb.tile(st, [128, 512], BF16, "go") for _ in range(NB_)]
                it = 0
                for k in range(SW // 128):
                    A_t, a_b = YA[k % 2]
                    B_t, b_b = YB[k % 2]
                    for g8 in range(8):
                        dma("sp" if g8 % 2 == 0 else "act", A_t[g8 * 16:(g8 + 1) * 16, :, :], y8D[0, k * 8 + g8, :, :].rearrange("(j h) c -> h j c", h=16), writes=[a_b])
                        dma("act" if g8 % 2 == 0 else "sp", B_t[g8 * 16:(g8 + 1) * 16, :, :], y8D[1, k * 8 + g8, :, :].rearrange("(j h) c -> h j c", h=16), writes=[b_b])
                    for (t0, tl) in tok_blocks(512):
                        i = it % NB_
                        it += 1
                        cA, nA = t0 // 8, tl // 8
                        a_t = A_t[:, :, cA:cA + nA]
                        b_t = B_t[:, :, cA:cA + nA]
                        u_t, u_b = ut[i]
                        y_t, y_b = yt_[i]
                        w_t, w_b = t5[i]
                        g_t, g_b = go[i]
                        dma("sp", u_t[:, 0:tl], uTD[k, :, t0:t0 + tl], writes=[u_b])
                        yv = y_t[:, 0:tl].rearrange("p (c j) -> p j c", j=8)
                        op("dve", lambda e: e.tensor_tensor(out=yv, in0=a_t, in1=b_t, op=ALU.add), reads=[a_b, b_b], writes=[y_b])
                        op("dve", lambda e: e.scalar_tensor_tensor(out=y_t[:, 0:tl], in0=u_t[:, 0:tl], scalar=dsk[:, k:k + 1], in1=y_t[:, 0:tl],
                                                                   op0=ALU.mult, op1=ALU.add), reads=[u_b, y_b, dskb], writes=[y_b])
                        dma("act", ytotT[k, :, t0:t0 + tl], y_t[:, 0:tl], reads=[y_b])
                        op("pool", lambda e: e.tensor_tensor(out=w_t[:, 0:tl], in0=y_t[:, 0:tl], in1=y_t[:, 0:tl], op=ALU.mult), reads=[y_b], writes=[w_b])
                        op("dve", lambda e: e.tensor_scalar(out=w_t[:, 0:tl], in0=w_t[:, 0:tl], scalar1=0.044715, scalar2=1.0, op0=ALU.mult, op1=ALU.add),
                           reads=[w_b], writes=[w_b])
                        op("dve", lambda e: e.tensor_tensor(out=w_t[:, 0:tl], in0=w_t[:, 0:tl], in1=y_t[:, 0:tl], op=ALU.mult), reads=[w_b, y_b], writes=[w_b])
                        op("act", lambda e: e.activation(out=w_t[:, 0:tl], in_=w_t[:, 0:tl], func=AF.Sigmoid, scale=1.5957691216057308), reads=[w_b], writes=[w_b])
                        op("dve", lambda e: e.tensor_tensor(out=g_t[:, 0:tl], in0=w_t[:, 0:tl], in1=y_t[:, 0:tl], op=ALU.mult), reads=[w_b, y_b], writes=[g_b])
                        dma("act", geluT[k, :, t0:t0 + tl], g_t[:, 0:tl], reads=[g_b])
                barrier()

        s5_combine()
        def setup(st):
            bg, bgb = kb.tile(st, [128, SW // 128], F32, "bglu")
            load_T(bg[:], bgb, b_glu[li:li + 1, :].rearrange("o (k p) -> (o k) p", p=128), SW // 128)
            return dict(bg=(bg, bgb), y=[kb.tile(st, [128, 512], F32, "gy") for _ in range(3)],
                        s=[kb.tile(st, [128, 512], F32, "gs") for _ in range(3)],
                        o=[kb.tile(st, [128, 512], BF16, "gout") for _ in range(3)], ctr=[0])

        def ep(ri, ci, pss, t0, sl, env):
            i = env["ctr"][0] % 3
            env["ctr"][0] += 1
            (pt, pb), = pss
            y_t, y_b = env["y"][i]
            s_t, s_b = env["s"][i]
            o_t, o_b = env["o"][i]
            bg, bgb = env["bg"]
            dma("sp", y_t[:, 0:sl], ytotT[ci, :, t0:t0 + sl], writes=[y_b])
            op("act", lambda e: e.activation(out=s_t[:, 0:sl], in_=pt[:, 0:sl], func=AF.Sigmoid, bias=bg[:, ci:ci + 1]), reads=[pb, bgb], writes=[s_b])
            op("dve", lambda e: e.tensor_tensor(out=o_t[:, 0:sl], in0=s_t[:, 0:sl], in1=y_t[:, 0:sl], op=ALU.mult), reads=[s_b, y_b], writes=[o_b])
            dma("act", brT[1][ci, :, t0:t0 + sl], o_t[:, 0:sl], reads=[o_b])
        linear([geluT], [[(0, w_glu[li], 0)]], [(i * 128, 128) for i in range(SW // 128)], ep, tb=1024, setup=setup)

    def mixer_stage(li):
        norm_stage(xT, hT, D, gain_fn(1), shift_fn(1))
        Wi = w_in[li]
        def su_u(st):
            return dict(f=[kb.tile(st, [128, 512], F32, "uf") for _ in range(2)],
                        b=[kb.tile(st, [128, 8, 64], BF16, "ub") for _ in range(2)], ctr=[0])

        def ep_u(ri, ci, pss, t0, sl, env):
            i = env["ctr"][0] % 2
            env["ctr"][0] += 1
            (pt, pb), = pss
            f_t, f_b = env["f"][i]
            b_t, b_b = env["b"][i]
            nA = sl // 8
            op("act", lambda e: e.activation(out=f_t[:, 0:sl], in_=pt[:, 0:sl], func=AF.Copy), reads=[pb], writes=[f_b])
            dma("act", uTD[ci, :, t0:t0 + sl], f_t[:, 0:sl], reads=[f_b])
            op("dve", lambda e: e.tensor_copy(out=b_t[:, :, 0:nA], in_=pt[:, 0:sl].rearrange("p (c i) -> p i c", i=8)), reads=[pb], writes=[b_b])
            for g8 in range(8):
                dma("act", u8D[ci * 8 + g8, :, t0 // 8:t0 // 8 + nA].rearrange("(i h) c -> h i c", h=16), b_t[g8 * 16:(g8 + 1) * 16, :, 0:nA], reads=[b_b])
        su_cq, ep_cq = ep_store(cqT, F32)
        su_ckv, ep_ckv = ep_store(ckvT, F32)
        su_r, ep_kr = ep_rope(krT, 64, None)
        _, ep_gq = ep_rope(gqT, 128, 0)
        _, ep_gk = ep_rope(gkT, 128, 1)
        segs = [(OFF_CQ, QL // 128, 128, ep_cq, "cq"), (OFF_CKV, KVL // 128, 128, ep_ckv, "ckv"), (OFF_KR, 1, 64, ep_kr, "r"),
                (OFF_U, SW // 128, 128, ep_u, "u"), (OFF_GQ, HQ, 128, ep_gq, "r"), (OFF_GK, HKV, 128, ep_gk, "r")]
        chunks_all = []
        seg_of = []
        for si_, (o0, nch, m, _e, _k) in enumerate(segs):
            for i in range(nch):
                chunks_all.append((o0 + i * 128, m))
                seg_of.append((si_, i))

        def su_all(st):
            return dict(cq=su_cq(st), ckv=su_ckv(st), r=su_r(st), u=su_u(st))

        def ep_all(ri, ci, pss, t0, sl, env):
            si_, i = seg_of[ci]
            segs[si_][3](ri, i, pss, t0, sl, env[segs[si_][4]])
        linear([hT], [[(0, Wi, 0)]], chunks_all, ep_all, setup=su_all)
        def su_tm(st):
            return ([kb.tile(st, [128, 512], BF16, "tmo") for _ in range(3)], [0])

        def ep_gv(pt, pb, tok0, col0, w, env):
            tiles, ctr = env
            o_t, o_b = tiles[ctr[0] % 3]
            ctr[0] += 1
            op("act", lambda e: e.activation(out=o_t[:, 0:w], in_=pt[:, 0:w], func=AF.Copy), reads=[pb], writes=[o_b])
            dma("act", gvD[tok0:tok0 + 128, col0:col0 + w], o_t[:, 0:w], reads=[o_b])
        linear_tm(hT, Wi, OFF_GV, HKV * 128, ep_gv, setup=su_tm)
        gcq_t = {}

        def ld_gain(src_ap, n, key):
            def f(st_unused=None):
                pass
            return f
        with contextlib.ExitStack() as st:
            gq_t, gq_b = kb.tile(st, [128, QL // 128], F32, "gcq")
            gkv_t, gkv_b = kb.tile(st, [128, KVL // 128], F32, "gckv")
            load_T(gq_t[:], gq_b, g_cq[li:li + 1, :].rearrange("o (k p) -> (o k) p", p=128), QL // 128)
            load_T(gkv_t[:], gkv_b, g_ckv[li:li + 1, :].rearrange("o (k p) -> (o k) p", p=128), KVL // 128)
            barrier()
            norm_stage(cqT, cqnT, QL, lambda t0: (gq_t[:], gq_b), None)
            norm_stage(ckvT, ckvnT, KVL, lambda t0: (gkv_t[:], gkv_b), None)
        Wq = w_uq[li]
        su_n, ep_n = ep_store(qmnT, BF16)
        su_rr, ep_rr = ep_rope(qmrT, 64, None)
        chunks_q = []
        for h in range(HM):
            chunks_q += [(h * 192, 128), (h * 192 + 128, 64)]

        def su_q(st):
            return dict(n=su_n(st), r=su_rr(st))

        def ep_q(ri, ci, pss, t0, sl, env):
            if ci % 2 == 0:
                ep_n(ri, ci // 2, pss, t0, sl, env["n"])
            else:
                ep_rr(ri, ci // 2, pss, t0, sl, env["r"])
        linear([cqnT], [[(0, Wq, 0)]], chunks_q, ep_q, setup=su_q, wgw=384)
        Wkv = w_ukv[li]
        su, ep = ep_store(kmT, BF16)
        linear([ckvnT], [[(0, Wkv, 0)]], [(h * 256, 128) for h in range(HM)], ep, setup=su, wgw=128)

        def ep_vm(pt, pb, tok0, col0, w, env):
            tiles, ctr = env
            o_t, o_b = tiles[ctr[0] % 3]
            ctr[0] += 1
            op("act", lambda e: e.activation(out=o_t[:, 0:w], in_=pt[:, 0:w], func=AF.Copy), reads=[pb], writes=[o_b])
            for hh in range(w // 256):
                h = (col0 + hh * 256) // 256
                dma("act", vmD[tok0:tok0 + 128, h * 128:(h + 1) * 128], o_t[:, hh * 256 + 128:hh * 256 + 256], reads=[o_b])
        linear_tm(ckvnT, Wkv, 0, HM * 256, ep_vm, setup=su_tm)
        s5_phaseA(li)
        with contextlib.ExitStack() as bst:
            gen = s5_recur_gen(li, bst) if not skip() else iter(())
            done = [False]

            def bg(nsteps):
                if done[0]:
                    return
                for _ in range(nsteps):
                    try:
                        next(gen)
                    except StopIteration:
                        done[0] = True
                        return
            heads = [dict(q=[(qmnT[h], 128), (qmrT[h], 64)], k=[(kmT[h], 128), (krT[0], 64)], vcol=h * 128, out=brT[0][h]) for h in range(HM)]
            attention_stage(heads, vmD, 192 ** -0.5, bg=bg)
            grp = HQ // HKV
            heads = [dict(q=[(gqT[h], 128)], k=[(gkT[h // grp], 128)], vcol=(h // grp) * 128, out=brT[2][h]) for h in range(HQ)]
            attention_stage(heads, gvD, 128 ** -0.5, bg=bg)
            while not done[0]:
                bg(64)
            barrier()
        s5_phaseC(li)
        s5_stage(li)
        def su_m(st):
            bgt, bgb = kb.tile(st, [128, 3, KC], F32, "bgate")
            load_T(bgt[:].rearrange("p n k -> p (n k)"), bgb, b_gate[li].rearrange("n (k p) -> (n k) p", p=128), 3 * KC)
            return dict(bg=(bgt, bgb), s=[kb.tile(st, [128, 512], F32, "ms") for _ in range(2)],
                        acc=[kb.tile(st, [128, 512], F32, "macc") for _ in range(2)],
                        o=[kb.tile(st, [128, 512], BF16, "mo") for _ in range(2)], ctr=[0])

        def ep_m(ri, ci, pss, t0, sl, env):
            if ri == 0:
                env["ctr"][0] += 1
            i = env["ctr"][0] % 2
            (pg, pgb), (pbr, pbrb) = pss
            s_t, s_b = env["s"][i]
            a_t, a_b = env["acc"][i]
            o_t, o_b = env["o"][i]
            bgt, bgb = env["bg"]
            op("act", lambda e: e.activation(out=s_t[:, 0:sl], in_=pg[:, 0:sl], func=AF.Sigmoid, bias=bgt[:, ri, ci:ci + 1]), reads=[pgb, bgb], writes=[s_b])
            if ri == 0:
                op("dve", lambda e: e.tensor_tensor(out=a_t[:, 0:sl], in0=s_t[:, 0:sl], in1=pbr[:, 0:sl], op=ALU.mult), reads=[s_b, pbrb], writes=[a_b])
            else:
                op("dve", lambda e: e.tensor_tensor(out=s_t[:, 0:sl], in0=s_t[:, 0:sl], in1=pbr[:, 0:sl], op=ALU.mult), reads=[s_b, pbrb], writes=[s_b])
                if ri == 1:
                    op("dve", lambda e: e.tensor_tensor(out=a_t[:, 0:sl], in0=a_t[:, 0:sl], in1=s_t[:, 0:sl], op=ALU.add), reads=[s_b, a_b], writes=[a_b])
                else:
                    op("dve", lambda e: e.tensor_tensor(out=o_t[:, 0:sl], in0=a_t[:, 0:sl], in1=s_t[:, 0:sl], op=ALU.add), reads=[s_b, a_b], writes=[o_b])
                    dma("act", ygT[ci, :, t0:t0 + sl], o_t[:, 0:sl], reads=[o_b])
        rounds = [[(0, w_gate[li, n], 0), (1 + n, w_branch[li, n], 0)] for n in range(3)]
        linear([hT, brT[0], brT[1], brT[2]], rounds, [(i * 128, 128) for i in range(KC)], ep_m, tb=1024, wgw=256, setup=su_m)
        su, epr = ep_residual(1)
        linear([ygT], [[(0, w_o[li], 0)]], [(i * 128, 128) for i in range(KC)], epr, tb=1024, setup=su)

    def final_stage():
        if skip():
            return
        with contextlib.ExitStack() as st:
            fg, fgb = kb.tile(st, [128, KC], F32, "fg")
            load_T(fg[:], fgb, final_g.rearrange("o (k p) -> (o k) p", p=128), KC)
            xs = [kb.tile(st, [128, KC, 512], F32, "fx") for _ in range(2)]
            sq = [kb.tile(st, [128, 512], F32, "fsq") for _ in range(3)]
            rs = [kb.tile(st, [128, 512], F32, "frs") for _ in range(2)]
            yn = [kb.tile(st, [128, 512], F32, "fyn") for _ in range(3)]
            ob = [kb.tile(st, [128, D], F32, "fob") for _ in range(2)]
            oi = 0
            for bi, t0 in enumerate(range(LC, T, 512)):
                tl = 512
                xt, xb = xs[bi % 2]
                dma("sp", xt[:], xT[:, :, t0:t0 + tl].rearrange("k p t -> p k t"), writes=[xb])
                pt, pb = next_ps()
                for k in range(KC):
                    sqt, sqb = sq[k % 3]
                    op("act", lambda e: e.activation(out=sqt[:], in_=xt[:, k, :], func=AF.Square), reads=[xb], writes=[sqb])
                    mmg(pt[:, 0:tl], [(mats[:, ONESF, :], sqt[:])], reads=[sqb, mats_b], writes=[pb], start=(k == 0), stop=(k == KC - 1))
                rt, rb = rs[bi % 2]
                op("act", lambda e: e.activation(out=rt[:], in_=pt[:, 0:tl], func=AF.Sqrt, bias=epsT[:, 0:1], scale=1.0 / D), reads=[pb, epsT_b], writes=[rb])
                op("dve", lambda e: e.reciprocal(out=rt[:], in_=rt[:]), reads=[rb], writes=[rb])
                for q in range(4):
                    ot, obf = ob[oi % 2]
                    oi += 1
                    for k in range(KC):
                        yt, yb = yn[k % 3]
                        op("dve", lambda e: e.scalar_tensor_tensor(out=yt[:, 0:128], in0=xt[:, k, q * 128:(q + 1) * 128], scalar=fg[:, k:k + 1],
                                                                   in1=rt[:, q * 128:(q + 1) * 128], op0=ALU.mult, op1=ALU.mult),
                           reads=[xb, rb, fgb], writes=[yb])
                        p2, p2b = next_ps()
                        op("pe", lambda e: e.transpose(p2[:, 0:128], yt[:, 0:128], mats[:, IDENT, :]), reads=[yb, mats_b], writes=[p2b])
                        op("act", lambda e: e.activation(out=ot[:, k * 128:(k + 1) * 128], in_=p2[:, 0:128], func=AF.Copy), reads=[p2b], writes=[obf])
                    r0 = t0 - LC + q * 128
                    dma("act", y_out[r0:r0 + 128, :], ot[:], reads=[obf])
            barrier()

    input_stage()
    for li in range(DEPTH):
        mod_stage(li)
        ffn_stage(li, 0, 0)
        mixer_stage(li)
        ffn_stage(li, 1, 2)
    final_stage()
    barrier()
    es.close()
    return nc


def rope_tables(L, GW, d_rot, LC):
    half = d_rot // 2
    freqs = (10000.0 ** (-np.arange(0, half, 2, dtype=np.float32) / np.float32(half))).astype(np.float32)
    t = np.arange(L)
    rows = (t // GW).astype(np.float32)
    cols = (t % GW).astype(np.float32)

    def tab(pos):
        ang = pos[:, None] * freqs[None, :]
        ang = np.concatenate([ang, ang], axis=-1)
        return np.cos(ang), np.sin(ang)
    cr, sr = tab(rows)
    cc, sc = tab(cols)
    cos = np.concatenate([cr, cc], axis=-1).astype(np.float32)
    sin = np.concatenate([sr, sc], axis=-1).astype(np.float32)
    cosT = np.concatenate([np.ones((d_rot, LC), np.float32), cos.T], axis=1)
    sinT = np.concatenate([np.zeros((d_rot, LC), np.float32), sin.T], axis=1)
    return np.ascontiguousarray(cosT), np.ascontiguousarray(sinT)


def rot_matrix(d_rot):
    half = d_rot // 2
    q = half // 2
    R = np.zeros((128, 128), np.float32)
    for b in (0, half):
        for j in range(q):
            R[b + q + j, b + j] = -1.0
            R[b + j, b + q + j] = 1.0
    return R


def host_constants(cfg):
    L, LC, GW = cfg["L"], cfg["LC"], cfg["GW"]
    c64, s64 = rope_tables(L, GW, 64, LC)
    c128, s128 = rope_tables(L, GW, 128, LC)
    mats = np.zeros((7, 128, 128), np.float32)
    mats[0] = np.eye(128, dtype=np.float32)
    mats[1] = rot_matrix(128)
    mats[2] = rot_matrix(64)
    sw = np.zeros((128, 128), np.float32)
    for p in range(64):
        sw[64 + p, p] = 1.0
        sw[p, 64 + p] = 1.0
    mats[3] = sw
    mf = np.zeros((128, 128), np.float32)
    mr = np.zeros((128, 128), np.float32)
    for i in range(8):
        for j in range(8):
            if j >= i:
                mf[i * 16:(i + 1) * 16, j * 16:(j + 1) * 16] = 1.0
            if j <= i:
                mr[i * 16:(i + 1) * 16, j * 16:(j + 1) * 16] = 1.0
    mats[4] = mf
    mats[5] = mr
    mats[6] = 1.0
    jf = np.tile(np.arange(-7, 9, dtype=np.float32)[None, :], (128, 1))
    return dict(k_cos64=c64, k_sin64=s64, k_cos128=c128, k_sin128=s128, k_mats=mats, k_jf=np.ascontiguousarray(jf))


_PROG_CACHE = {}


def run(cfg, inputs, ncores=NCORES, raw=False):
    key = tuple(sorted((k, str(v)) for k, v in cfg.items()))
    if key not in _PROG_CACHE:
        _PROG_CACHE[key] = build_program(cfg)
    nc = _PROG_CACHE[key]
    consts = host_constants(cfg)
    f32 = lambda a: np.ascontiguousarray(np.asarray(a, dtype=np.float32))
    shared = {k: f32(v) for k, v in inputs.items() if k not in ("x", "c", "ctx", "c_ctx", "final_g")}
    shared["c_ctx"] = f32(inputs["c_ctx"]).reshape(1, -1)
    shared["final_g"] = f32(inputs["final_g"]).reshape(1, -1)
    shared.update(consts)
    x = f32(inputs["x"])
    c = f32(inputs["c"])
    ctx = f32(inputs["ctx"])
    in_maps = []
    for b in range(ncores):
        m = dict(shared)
        m["x"] = x[b]
        m["c"] = c[b:b + 1]
        m["ctx"] = ctx[b]
        in_maps.append(m)
    res = run_bass_kernel_spmd(nc, in_maps, core_ids=list(range(ncores)))
    if raw:
        return res.results
    return np.stack([np.asarray(r["y"]) for r in res.results], axis=0)


def kernel(**inputs):
    return run(FULL, inputs)
```

```python
import math
import contextlib
import numpy as np
import concourse.bass as bass
import concourse.mybir as mybir
from concourse.bass_utils import run_bass_kernel_spmd

F32 = mybir.dt.float32
BF16 = mybir.dt.bfloat16
AF = mybir.ActivationFunctionType
ALU = mybir.AluOpType

FULL = dict(D=2048, L=4096, LC=256, DFF=5632, HM=8, QL=1536, KVL=512, G=64, HQ=8, HKV=2, GW=64, DEPTH=2)
EPS = 1e-6
NCORES = 8


class Buf:
    __slots__ = ("w", "r", "excl")

    def __init__(self, excl=False):
        self.w = None
        self.r = {}
        self.excl = excl


def _split(reads, writes):
    ex = [b for b in reads if b.excl]
    if not ex:
        return reads, writes
    return [b for b in reads if not b.excl], list(writes) + ex


class Eng:
    def __init__(self, name, handle, sem, dsems):
        self.name = name
        self.h = handle
        self.sem = sem
        self.count = 0
        self.seen = {}
        self.dsems = dsems
        self.dvals = [0] * len(dsems)
        self.di = 0


class KB:
    def __init__(self, nc, es):
        self.nc = nc
        self.es = es
        self.engs = {}
        for name, h, nd in (("pe", nc.tensor, 0), ("act", nc.scalar, 8), ("dve", nc.vector, 0),
                            ("pool", nc.gpsimd, 24), ("sp", nc.sync, 24)):
            sem = es.enter_context(nc.semaphore("s_" + name))
            ds = [es.enter_context(nc.semaphore(f"d_{name}{i}")) for i in range(nd)]
            self.engs[name] = Eng(name, h, sem, ds)
        self.uid = 0

    def _waits(self, eng, reads, writes, same_sync):
        need = {}

        def add(tok):
            s, v = tok
            if (s is eng.sem) and not same_sync:
                return
            k = id(s)
            if k not in need or need[k][1] < v:
                need[k] = (s, v)
        for b in reads:
            if b.w is not None:
                add(b.w)
        for b in writes:
            if b.w is not None:
                add(b.w)
            for tok in b.r.values():
                add(tok)
        for k, (s, v) in need.items():
            if eng.seen.get(k, 0) < v:
                eng.h.wait_ge(s, v)
                eng.seen[k] = v

    def _mark(self, tok, reads, writes):
        for b in writes:
            b.w = tok
            b.r = {}
        k = id(tok[0])
        for b in reads:
            b.r[k] = tok

    def op(self, en, fn, reads=(), writes=()):
        reads, writes = _split(reads, writes)
        eng = self.engs[en]
        self._waits(eng, reads, writes, same_sync=(en != "pe"))
        ins = fn(eng.h)
        eng.count += 1
        ins.then_inc(eng.sem, 1)
        self._mark((eng.sem, eng.count), reads, writes)

    def mmg(self, out_ap, pairs, reads, writes, start=True, stop=True):
        reads, writes = _split(reads, writes)
        eng = self.engs["pe"]
        self._waits(eng, reads, writes, same_sync=False)
        n = len(pairs)
        ins = None
        for i, (l, r) in enumerate(pairs):
            ins = eng.h.matmul(out_ap, l, r, start=(start and i == 0), stop=(stop and i == n - 1))
        eng.count += 1
        ins.then_inc(eng.sem, 1)
        self._mark((eng.sem, eng.count), reads, writes)

    def dma(self, qn, out_ap, in_ap, reads=(), writes=(), slow=False):
        eng = self.engs[qn]
        i = eng.di
        eng.di = (i + 1) % len(eng.dsems)
        s = eng.dsems[i]
        pv = eng.dvals[i]
        if pv > 0 and eng.seen.get(id(s), 0) < pv:
            eng.h.wait_ge(s, pv)
            eng.seen[id(s)] = pv
        self._waits(eng, reads, writes, same_sync=True)
        if slow:
            ins = eng.h.dma_start(out=out_ap, in_=in_ap, allow_slow_non_contiguous=True)
        else:
            ins = eng.h.dma_start(out=out_ap, in_=in_ap)
        ins.then_inc(s, 16)
        eng.dvals[i] = pv + 16
        self._mark((s, pv + 16), reads, writes)

    def barrier(self):
        toks = []
        for e in self.engs.values():
            if e.count:
                toks.append((e.sem, e.count))
            for s, v in zip(e.dsems, e.dvals):
                if v:
                    toks.append((s, v))
        for e in self.engs.values():
            for s, v in toks:
                if e.seen.get(id(s), 0) < v:
                    e.h.wait_ge(s, v)
                    e.seen[id(s)] = v

    def tile(self, st, shape, dt, name=None):
        self.uid += 1
        t = st.enter_context(self.nc.sbuf_tensor(f"{name or 't'}_{self.uid}", list(shape), dt))
        return t, Buf()


def ceil_div(a, b):
    return (a + b - 1) // b


def build_program(cfg):
    D, L, LC, DFF = cfg["D"], cfg["L"], cfg["LC"], cfg["DFF"]
    HM, QL, KVL, G, HQ, HKV, GW, DEPTH = (cfg[k] for k in ("HM", "QL", "KVL", "G", "HQ", "HKV", "GW", "DEPTH"))
    T = L + LC
    KC = D // 128
    FC = DFF // 128
    SW = G * 16
    BW = SW
    assert HM * 128 == BW and HQ * 128 == BW
    DIN = QL + KVL + 64 + SW + HQ * 128 + 2 * HKV * 128
    OFF_CQ, OFF_CKV, OFF_KR = 0, QL, QL + KVL
    OFF_U = OFF_KR + 64
    OFF_GQ = OFF_U + SW
    OFF_GK = OFF_GQ + HQ * 128
    OFF_GV = OFF_GK + HKV * 128
    NC8 = T // 8
    NMODC = 9 * KC

    nc = bass.Bass("TRN2", target_bir_lowering=False)

    def din(name, shape, dt=F32):
        return nc.dram_tensor(name, list(shape), dt, kind="ExternalInput").ap()

    DBG = cfg.get("dbg", ())
    STOP = cfg.get("stop", 10 ** 9)
    unit = [0]

    def skip():
        unit[0] += 1
        return unit[0] > STOP or unit[0] < cfg.get('start', 0)

    def dscr(name, shape, dt):
        if name in DBG:
            return nc.dram_tensor(name, list(shape), dt, kind="ExternalOutput").ap()
        return nc.dram_tensor(name, list(shape), dt).ap()

    x_in = din("x", [L, D])
    c_in = din("c", [1, D])
    ctx_in = din("ctx", [LC, D])
    cctx_in = din("c_ctx", [1, D])
    w_mod = din("w_mod", [DEPTH, D, 9 * D])
    b_mod = din("b_mod", [DEPTH, 9 * D])
    norm_g = din("norm_g", [DEPTH, 3, D])
    w_up = din("w_ffn_up", [DEPTH, 2, D, 2 * DFF])
    w_down = din("w_ffn_down", [DEPTH, 2, DFF, D])
    w_in = din("w_in", [DEPTH, D, DIN])
    g_cq = din("mla_g_cq", [DEPTH, QL])
    g_ckv = din("mla_g_ckv", [DEPTH, KVL])
    w_uq = din("mla_w_uq", [DEPTH, QL, HM * 192])
    w_ukv = din("mla_w_ukv", [DEPTH, KVL, HM * 256])
    g_q = din("gqa_g_q", [DEPTH, 128])
    g_k = din("gqa_g_k", [DEPTH, 128])
    lam_re = din("s5_lam_re", [DEPTH, 2, G, 64])
    lam_im = din("s5_lam_im", [DEPTH, 2, G, 64])
    log_dt = din("s5_log_dt", [DEPTH, 2, G])
    b_re = din("s5_b_re", [DEPTH, 2, G, 64, 16])
    b_im = din("s5_b_im", [DEPTH, 2, G, 64, 16])
    c_re = din("s5_c_re", [DEPTH, 2, G, 16, 64])
    c_im = din("s5_c_im", [DEPTH, 2, G, 16, 64])
    s5_d = din("s5_d", [DEPTH, G, 16])
    w_glu = din("s5_w_glu", [DEPTH, SW, SW])
    b_glu = din("s5_b_glu", [DEPTH, SW])
    w_gate = din("w_gate", [DEPTH, 3, D, D])
    b_gate = din("b_gate", [DEPTH, 3, D])
    w_branch = din("w_branch", [DEPTH, 3, BW, D])
    w_o = din("w_o", [DEPTH, D, D])
    final_g = din("final_g", [1, D])
    k_cos64 = din("k_cos64", [64, T])
    k_sin64 = din("k_sin64", [64, T])
    k_cos128 = din("k_cos128", [128, T])
    k_sin128 = din("k_sin128", [128, T])
    k_mats = din("k_mats", [7, 128, 128])
    k_jf = din("k_jf", [128, 16])
    y_out = nc.dram_tensor("y", [L, D], F32, kind="ExternalOutput").ap()

    xT = dscr("xT", [KC, 128, T], F32)
    hT = dscr("hT", [KC, 128, T], BF16)
    actT = dscr("actT", [FC, 128, T], BF16)
    scT = dscr("scT", [KC, 128, 2], BF16)
    cqT = dscr("cqT", [QL // 128, 128, T], F32)
    cqnT = dscr("cqnT", [QL // 128, 128, T], BF16)
    ckvT = dscr("ckvT", [KVL // 128, 128, T], F32)
    ckvnT = dscr("ckvnT", [KVL // 128, 128, T], BF16)
    qmnT = dscr("qmnT", [HM, 128, T], BF16)
    qmrT = dscr("qmrT", [HM, 64, T], BF16)
    kmT = dscr("kmT", [HM, 128, T], BF16)
    krT = dscr("krT", [1, 64, T], BF16)
    vmD = dscr("vmD", [T, HM * 128], BF16)
    u8D = dscr("u8D", [G, 128, NC8], BF16)
    uTD = dscr("uTD", [G // 8, 128, T], F32)
    gqT = dscr("gqT", [HQ, 128, T], BF16)
    gkT = dscr("gkT", [HKV, 128, T], BF16)
    gvD = dscr("gvD", [T, HKV * 128], BF16)
    y8D = dscr("y8D", [2, G, 128, NC8], F32)
    brT = [dscr(f"brT{n}", [BW // 128, 128, T], BF16) for n in range(3)]
    ygT = dscr("ygT", [KC, 128, T], BF16)
    geluT = dscr("geluT", [SW // 128, 128, T], BF16)
    S5RL = 64
    S5NR = (T // 8 + S5RL - 1) // S5RL + 2
    vvD = dscr("vvD", [2, S5NR, 128, 2 * G * S5RL], F32)
    saD = dscr("saD", [2, S5NR, 128, G * S5RL], BF16)
    wmD = dscr("wmD", [2, 2, 128, G * 128], BF16)
    ytotT = dscr("ytotT", [SW // 128, 128, T], F32)

    es = contextlib.ExitStack()
    kb = KB(nc, es)
    op, mmg, dma, barrier = kb.op, kb.mmg, kb.dma, kb.barrier

    PS = []
    for i in range(8):
        t = es.enter_context(nc.psum_tensor(f"ps{i}", [128, 512], F32))
        PS.append((t, Buf(excl=True)))
    ps_rr = [0]

    def next_ps():
        i = ps_rr[0]
        ps_rr[0] = (i + 1) % 8
        return PS[i]

    mats, mats_b = kb.tile(es, [128, 7, 128], F32, "mats")
    onesb, onesb_b = kb.tile(es, [128, 128], BF16, "onesb")
    matsb, matsb_b = kb.tile(es, [128, 2, 128], BF16, "matsb")
    modT, modT_b = kb.tile(es, [128, NMODC, 2], F32, "modT")
    bmodT, bmodT_b = kb.tile(es, [128, NMODC], F32, "bmodT")
    ngT, ngT_b = kb.tile(es, [128, 3, KC], F32, "ngT")
    gains, gains_b = kb.tile(es, [128, 3, 2, KC], F32, "gains")
    shifts, shifts_b = kb.tile(es, [128, 3, 2, KC], F32, "shifts")
    rgate, rgate_b = kb.tile(es, [128, 3, 2, KC], F32, "rgate")
    smallv, smallv_b = kb.tile(es, [128, 8], F32, "smallv")
    epsT, epsT_b = kb.tile(es, [128, 1], F32, "epsT")
    scS, scS_b = kb.tile(es, [128, KC, 2], BF16, "scS")
    s5tab, s5tab_b = kb.tile(es, [128, 2, 2, 2, G], F32, "s5tab")
    IDENT, R128, R64, SWAPN, MASKF, MASKR, ONESF = range(7)

    dma("sp", mats[:], k_mats.rearrange("m p q -> p m q"), writes=[mats_b])
    op("dve", lambda e: e.memset(onesb[:], 1.0), writes=[onesb_b])
    op("dve", lambda e: e.memset(epsT[:], EPS), writes=[epsT_b])
    op("dve", lambda e: e.tensor_copy(out=matsb[:], in_=mats[:, 1:3, :]), reads=[mats_b], writes=[matsb_b])
    barrier()

    ldtmp = [kb.tile(es, [128, 128], F32, "ldtmp") for _ in range(2)]
    ldctr = [0]

    def load_T(dst_ap, dst_buf, rows_ap, n, dup64=False):
        tt, tb_ = ldtmp[ldctr[0] % 2]
        ldctr[0] += 1
        if dup64:
            dma("sp", tt[0:n, 0:64], rows_ap, writes=[tb_])
            dma("sp", tt[0:n, 64:128], rows_ap, writes=[tb_])
        else:
            dma("sp", tt[0:n, :], rows_ap, writes=[tb_])
        pt, pb = next_ps()
        op("pe", lambda e: e.transpose(pt[:, 0:n], tt[0:n, :], mats[0:n, IDENT, 0:n]), reads=[tb_, mats_b], writes=[pb])
        op("dve", lambda e: e.tensor_copy(out=dst_ap, in_=pt[:, 0:n]), reads=[pb], writes=[dst_buf])

    def stream_of(t0):
        return 1 if t0 < LC else 0

    def tok_blocks(maxlen):
        blks = [(0, LC)] if LC <= maxlen else [(i, min(maxlen, LC - i)) for i in range(0, LC, maxlen)]
        for i in range(LC, T, maxlen):
            blks.append((i, min(maxlen, T - i)))
        return blks

    def sub_tiles(tl):
        return [(s, min(512, tl - s)) for s in range(0, tl, 512)]

    def linear(srcs, rounds, chunks, epilogue, tb=1024, wgw=512, setup=None, nbufw=2, tblocks=None, sb_srcs=None):
        if skip():
            return
        with contextlib.ExitStack() as st:
            kcs = [s.shape[0] for s in srcs]
            src_bytes = sum(k * tb * 2 for k in kcs)
            nsb = 2 if src_bytes * 2 <= 72 * 1024 else 1
            sb = [[kb.tile(st, [128, k, tb], BF16, f"src{i}") for _ in range(nsb)] for i, k in enumerate(kcs)]
            groups = []
            cur = []
            for ci, (off, m) in enumerate(chunks):
                if cur and (off + m - chunks[cur[0]][0] > wgw or off != chunks[cur[-1]][0] + chunks[cur[-1]][1]):
                    groups.append(cur)
                    cur = []
                cur.append(ci)
            if cur:
                groups.append(cur)
            terms = [t for r in rounds for t in r]
            wt = {}
            for ti, (si, W, cb) in enumerate(terms):
                wt[ti] = [kb.tile(st, [128, kcs[si], wgw], BF16, f"w{ti}") for _ in range(nbufw)]
            env = setup(st) if setup else None
            blocks = tblocks if tblocks is not None else tok_blocks(tb)
            wctr = 0
            pend = [None]
            for bi, (t0, tl) in enumerate(blocks):
                cur_sb = []
                for i, s in enumerate(srcs):
                    if sb_srcs and i in sb_srcs:
                        cur_sb.append(sb_srcs[i])
                        continue
                    tl_, b_ = sb[i][bi % nsb]
                    dma("sp", tl_[:, :, 0:tl], s[:, :, t0:t0 + tl].rearrange("k p t -> p k t"), writes=[b_])
                    cur_sb.append((tl_, b_))
                for grp in groups:
                    g0 = chunks[grp[0]][0]
                    g1 = chunks[grp[-1]][0] + chunks[grp[-1]][1]
                    wcur = {}
                    ti = 0
                    for r in rounds:
                        for (si, W, cb) in r:
                            wtile, wb = wt[ti][wctr % nbufw]
                            dma("pool", wtile[:, :, 0:g1 - g0],
                                W[:, cb + g0:cb + g1].rearrange("(k p) n -> p k n", p=128), writes=[wb])
                            wcur[ti] = (wtile, wb)
                            ti += 1
                    wctr += 1
                    for ci in grp:
                        off, m = chunks[ci]
                        for (s0, sl) in sub_tiles(tl):
                            ti = 0
                            for ri, r in enumerate(rounds):
                                pss = []
                                for (si, W, cb) in r:
                                    wtile, wb = wcur[ti]
                                    stile, sbf = cur_sb[si]
                                    pt, pb = next_ps()
                                    mmg(pt[0:m, 0:sl],
                                        [(wtile[:, k, off - g0:off - g0 + m], stile[:, k, s0:s0 + sl])
                                         for k in range(kcs[si])],
                                        reads=[wb, sbf], writes=[pb])
                                    pss.append((pt, pb))
                                    ti += 1
                                if pend[0] is not None:
                                    epilogue(*pend[0])
                                pend[0] = (ri, ci, pss, t0 + s0, sl, env)
            if pend[0] is not None:
                epilogue(*pend[0])
                pend[0] = None
            barrier()

    def linear_tm(src, W, col_base, ncols, epilogue, setup=None):
        if skip():
            return
        with contextlib.ExitStack() as st:
            kc = src.shape[0]
            tb = 512
            sb = [kb.tile(st, [128, kc, tb], BF16, "srctm") for _ in range(2)]
            wtile, wb = kb.tile(st, [128, kc, ncols], BF16, "wtm")
            env = setup(st) if setup else None
            dma("pool", wtile[:], W[:, col_base:col_base + ncols].rearrange("(k p) n -> p k n", p=128), writes=[wb])
            for bi, (t0, tl) in enumerate(tok_blocks(tb)):
                stile, sbf = sb[bi % 2]
                dma("sp", stile[:, :, 0:tl], src[:, :, t0:t0 + tl].rearrange("k p t -> p k t"), writes=[sbf])
                for q in range(tl // 128):
                    for c0 in range(0, ncols, 512):
                        w = min(512, ncols - c0)
                        pt, pb = next_ps()
                        mmg(pt[:, 0:w], [(stile[:, k, q * 128:(q + 1) * 128], wtile[:, k, c0:c0 + w]) for k in range(kc)],
                            reads=[wb, sbf], writes=[pb])
                        epilogue(pt, pb, t0 + q * 128, c0, w, env)
            barrier()

    def ep_store(dst, dt, rowmap=None, func=AF.Copy):
        def setup(st):
            return [kb.tile(st, [128, 512], dt, "stg") for _ in range(3)], [0]

        def ep(ri, ci, pss, t0, sl, env):
            stg, ctr = env
            (pt, pb), = pss
            tl_, b_ = stg[ctr[0] % 3]
            ctr[0] += 1
            ch, m = rowmap(ci) if rowmap else (ci, 128)
            op("act", lambda e: e.activation(out=tl_[0:m, 0:sl], in_=pt[0:m, 0:sl], func=func), reads=[pb], writes=[b_])
            dma("act", dst[ch, 0:m, t0:t0 + sl], tl_[0:m, 0:sl], reads=[b_])
        return setup, ep

    def norm_stage(src, dst, nfeat, gain_ap_fn, shift_ap_fn, blocks=None):
        if skip():
            return
        kc = src.shape[0]
        with contextlib.ExitStack() as st:
            xs = [kb.tile(st, [128, kc, 512], F32, "nx") for _ in range(2)]
            sq = [kb.tile(st, [128, 512], BF16, "nsq") for _ in range(4)]
            rs = [kb.tile(st, [128, 512], F32, "nrs") for _ in range(2)]
            tmp = [kb.tile(st, [128, 512], F32, "ntmp") for _ in range(3)]
            ob = [kb.tile(st, [128, kc, 512], BF16, "nob") for _ in range(2)]
            for bi, (t0, tl) in enumerate(blocks or tok_blocks(512)):
                xt, xb = xs[bi % 2]
                dma("sp", xt[:, :, 0:tl], src[:, :, t0:t0 + tl].rearrange("k p t -> p k t"), writes=[xb])
                pt, pb = next_ps()
                for k in range(kc):
                    sqt, sqb = sq[k % 4]
                    if k % 3 == 2:
                        op("pool", lambda e: e.tensor_tensor(out=sqt[:, 0:tl], in0=xt[:, k, 0:tl], in1=xt[:, k, 0:tl], op=ALU.mult),
                           reads=[xb], writes=[sqb])
                    else:
                        op("act", lambda e: e.activation(out=sqt[:, 0:tl], in_=xt[:, k, 0:tl], func=AF.Square),
                           reads=[xb], writes=[sqb])
                    mmg(pt[:, 0:tl], [(onesb[:, :], sqt[:, 0:tl])], reads=[sqb, onesb_b], writes=[pb],
                        start=(k == 0), stop=(k == kc - 1))
                rt, rb = rs[bi % 2]
                op("act", lambda e: e.activation(out=rt[:, 0:tl], in_=pt[:, 0:tl], func=AF.Sqrt,
                                                 bias=epsT[:, 0:1], scale=1.0 / nfeat), reads=[pb, epsT_b], writes=[rb])
                op("dve", lambda e: e.reciprocal(out=rt[:, 0:tl], in_=rt[:, 0:tl]), reads=[rb], writes=[rb])
                ot, obf = ob[bi % 2]
                gain = gain_ap_fn(t0)
                shift = shift_ap_fn(t0) if shift_ap_fn else None
                for k in range(kc):
                    tt, tbf = tmp[k % 3]
                    op("dve", lambda e: e.scalar_tensor_tensor(out=tt[:, 0:tl], in0=xt[:, k, 0:tl], scalar=gain[0][:, k:k + 1],
                                                               in1=rt[:, 0:tl], op0=ALU.mult, op1=ALU.mult),
                       reads=[xb, rb, gain[1]], writes=[tbf])
                    if shift is not None:
                        op("act", lambda e: e.activation(out=ot[:, k, 0:tl], in_=tt[:, 0:tl], func=AF.Identity,
                                                         bias=shift[0][:, k:k + 1]), reads=[tbf, shift[1]], writes=[obf])
                    else:
                        op("act", lambda e: e.activation(out=ot[:, k, 0:tl], in_=tt[:, 0:tl], func=AF.Copy),
                           reads=[tbf], writes=[obf])
                dma("act", dst[:, :, t0:t0 + tl].rearrange("k p t -> p k t"), ot[:, :, 0:tl], reads=[obf])
            barrier()

    def input_stage():
        if skip():
            return
        with contextlib.ExitStack() as st:
            xin = [kb.tile(st, [128, D], F32, "xin") for _ in range(2)]
            xo = [kb.tile(st, [128, KC, 128], F32, "xo") for _ in range(2)]
            for bi in range(T // 128):
                t0 = bi * 128
                it, ib = xin[bi % 2]
                srcap = ctx_in[t0:t0 + 128, :] if t0 < LC else x_in[t0 - LC:t0 - LC + 128, :]
                dma("sp", it[:], srcap, writes=[ib])
                ot, obf = xo[bi % 2]
                for k in range(KC):
                    pt, pb = next_ps()
                    op("pe", lambda e: e.transpose(pt[:, 0:128], it[:, k * 128:(k + 1) * 128], mats[:, IDENT, :]),
                       reads=[ib, mats_b], writes=[pb])
                    if k % 2 == 0:
                        op("dve", lambda e: e.tensor_copy(out=ot[:, k, :], in_=pt[:, 0:128]), reads=[pb], writes=[obf])
                    else:
                        op("act", lambda e: e.activation(out=ot[:, k, :], in_=pt[:, 0:128], func=AF.Copy), reads=[pb], writes=[obf])
                dma("act", xT[:, :, t0:t0 + 128].rearrange("k p t -> p k t"), ot[:], reads=[obf])
            if cfg.get('iv') == 1:
                barrier()
                return
            ct, cb_ = kb.tile(st, [128, KC, 2], F32, "ct")
            load_T(ct[:, :, 0], cb_, c_in.rearrange("o (k p) -> (o k) p", p=128), KC)
            load_T(ct[:, :, 1], cb_, cctx_in.rearrange("o (k p) -> (o k) p", p=128), KC)
            op("act", lambda e: e.activation(out=scS[:], in_=ct[:], func=AF.Silu), reads=[cb_], writes=[scS_b])
            barrier()

    def mod_stage(li):
        if skip():
            return
        bm_rows = b_mod[li:li + 1, :].rearrange("o (n p) -> (o n) p", p=128)
        for r0 in range(0, NMODC, 72):
            r1 = min(NMODC, r0 + 72)
            load_T(bmodT[:, r0:r1], bmodT_b, bm_rows[r0:r1, :], r1 - r0)
        load_T(ngT[:].rearrange("p j k -> p (j k)"), ngT_b, norm_g[li].rearrange("j (k p) -> (j k) p", p=128), 3 * KC)
        load_T(smallv[:, 0:1], smallv_b, g_q[li:li + 1, :], 1)
        load_T(smallv[:, 1:2], smallv_b, g_k[li:li + 1, :], 1)

        def ep(ri, ci, pss, t0, sl, env):
            (pt, pb), = pss
            op("dve", lambda e: e.tensor_scalar(out=modT[:, ci, :], in0=pt[:, 0:2], scalar1=bmodT[:, ci:ci + 1], scalar2=None,
                                                op0=ALU.add), reads=[pb, bmodT_b], writes=[modT_b])
        linear([scT], [[(0, w_mod[li], 0)]], [(i * 128, 128) for i in range(NMODC)], ep, tb=2, tblocks=[(0, 2)], nbufw=3, sb_srcs={0: (scS, scS_b)})
        for j in range(3):
            for s in range(2):
                sc = modT[:, (3 * j + 1) * KC:(3 * j + 2) * KC, s]
                sh = modT[:, (3 * j) * KC:(3 * j + 1) * KC, s]
                gt = modT[:, (3 * j + 2) * KC:(3 * j + 3) * KC, s]
                op("dve", lambda e: e.scalar_tensor_tensor(out=gains[:, j, s, :], in0=sc, scalar=1.0, in1=ngT[:, j, :],
                                                           op0=ALU.add, op1=ALU.mult), reads=[modT_b, ngT_b], writes=[gains_b])
                op("dve", lambda e: e.tensor_copy(out=shifts[:, j, s, :], in_=sh), reads=[modT_b], writes=[shifts_b])
                op("dve", lambda e: e.tensor_scalar(out=rgate[:, j, s, :], in0=gt, scalar1=(1.0 if j == 1 else 0.5), scalar2=None,
                                                    op0=ALU.mult), reads=[modT_b], writes=[rgate_b])
        barrier()

    def gain_fn(j):
        return lambda t0: (gains[:, j, stream_of(t0), :], gains_b)

    def shift_fn(j):
        return lambda t0: (shifts[:, j, stream_of(t0), :], shifts_b)

    def ep_residual(j):
        def setup(st):
            return ([kb.tile(st, [128, 512], F32, "rx") for _ in range(3)], [0])

        def ep(ri, ci, pss, t0, sl, env):
            xs, ctr = env
            (pt, pb), = pss
            xt, xb = xs[ctr[0] % 3]
            ctr[0] += 1
            dma("sp", xt[:, 0:sl], xT[ci, :, t0:t0 + sl], writes=[xb])
            s = stream_of(t0)
            op("dve", lambda e: e.scalar_tensor_tensor(out=xt[:, 0:sl], in0=pt[:, 0:sl], scalar=rgate[:, j, s, ci:ci + 1],
                                                       in1=xt[:, 0:sl], op0=ALU.mult, op1=ALU.add),
               reads=[pb, xb, rgate_b], writes=[xb])
            dma("act", xT[ci, :, t0:t0 + sl], xt[:, 0:sl], reads=[xb])
        return setup, ep

    def ffn_stage(li, which, j):
        norm_stage(xT, hT, D, gain_fn(j), shift_fn(j))
        Wu = w_up[li, which]
        Wd = w_down[li, which]

        def setup(st):
            return ([kb.tile(st, [128, 512], F32, "sg") for _ in range(3)],
                    [kb.tile(st, [128, 512], BF16, "ao") for _ in range(3)], [0])

        def ep(ri, ci, pss, t0, sl, env):
            sgs, aos, ctr = env
            (pg, pgb), (pu, pub) = pss
            sg, sgb = sgs[ctr[0] % 3]
            ao, aob = aos[ctr[0] % 3]
            ctr[0] += 1
            op("act", lambda e: e.activation(out=sg[:, 0:sl], in_=pg[:, 0:sl], func=AF.Silu), reads=[pgb], writes=[sgb])
            op("dve", lambda e: e.tensor_tensor(out=ao[:, 0:sl], in0=sg[:, 0:sl], in1=pu[:, 0:sl], op=ALU.mult),
               reads=[sgb, pub], writes=[aob])
            dma("act", actT[ci, :, t0:t0 + sl], ao[:, 0:sl], reads=[aob])
        linear([hT], [[(0, Wu, 0), (0, Wu, DFF)]], [(i * 128, 128) for i in range(FC)], ep, tb=1024, wgw=512, setup=setup)
        su, epr = ep_residual(j)
        linear([actT], [[(0, Wd, 0)]], [(i * 128, 128) for i in range(KC)], epr, tb=1024, wgw=256, setup=su)

    def ep_rope(dst, rows, norm_col, rowmap=None):
        cosD, sinD = (k_cos128, k_sin128) if rows == 128 else (k_cos64, k_sin64)
        RM = R128 if rows == 128 else R64

        def setup(st):
            return dict(q=[kb.tile(st, [128, 512], F32, "rq") for _ in range(4)],
                        cs=[kb.tile(st, [128, 2, 512], F32, "rcs") for _ in range(4)],
                        t=[kb.tile(st, [128, 512], F32, "rt") for _ in range(4)],
                        qh=[kb.tile(st, [128, 512], BF16, "rqh") for _ in range(4)],
                        sqh=[kb.tile(st, [128, 512], BF16, "rsqh") for _ in range(4)],
                        r=[kb.tile(st, [128, 512], F32, "rr") for _ in range(4)],
                        o=[kb.tile(st, [128, 512], BF16, "ro") for _ in range(4)], ctr=[0])

        def ep(ri, ci, pss, t0, sl, env):
            i = env["ctr"][0] % 4
            env["ctr"][0] += 1
            (pt, pb), = pss
            q, qb = env["q"][i]
            cs, csb = env["cs"][i]
            tt, tb_ = env["t"][i]
            rr, rrb = env["r"][i]
            o, ob_ = env["o"][i]
            qh, qhb = env["qh"][i]
            sqh, sqhb = env["sqh"][i]
            ch = rowmap(ci) if rowmap else ci
            dma("sp", cs[0:rows, 0, 0:sl], cosD[:, t0:t0 + sl], writes=[csb])
            dma("sp", cs[0:rows, 1, 0:sl], sinD[:, t0:t0 + sl], writes=[csb])
            if norm_col is not None:
                op("act", lambda e: e.activation(out=sqh[0:rows, 0:sl], in_=pt[0:rows, 0:sl], func=AF.Square), reads=[pb], writes=[sqhb])
                p2, p2b = next_ps()
                mmg(p2[0:rows, 0:sl], [(onesb[0:rows, 0:rows], sqh[0:rows, 0:sl])], reads=[sqhb, onesb_b], writes=[p2b])
                op("act", lambda e: e.activation(out=rr[0:rows, 0:sl], in_=p2[0:rows, 0:sl], func=AF.Sqrt, bias=epsT[0:rows, 0:1],
                                                 scale=1.0 / rows), reads=[p2b, epsT_b], writes=[rrb])
                op("dve", lambda e: e.reciprocal(out=rr[0:rows, 0:sl], in_=rr[0:rows, 0:sl]), reads=[rrb], writes=[rrb])
                op("dve", lambda e: e.scalar_tensor_tensor(out=qh[0:rows, 0:sl], in0=pt[0:rows, 0:sl],
                                                           scalar=smallv[0:rows, norm_col:norm_col + 1], in1=rr[0:rows, 0:sl],
                                                           op0=ALU.mult, op1=ALU.mult), reads=[pb, rrb, smallv_b], writes=[qhb])
            else:
                op("act", lambda e: e.activation(out=qh[0:rows, 0:sl], in_=pt[0:rows, 0:sl], func=AF.Copy), reads=[pb], writes=[qhb])
            p3, p3b = next_ps()
            mmg(p3[0:rows, 0:sl], [(matsb[0:rows, RM - 1, 0:rows], qh[0:rows, 0:sl])], reads=[qhb, matsb_b], writes=[p3b])
            op("pool", lambda e: e.tensor_tensor(out=q[0:rows, 0:sl], in0=qh[0:rows, 0:sl], in1=cs[0:rows, 0, 0:sl], op=ALU.mult),
               reads=[qhb, csb], writes=[qb])
            op("dve", lambda e: e.tensor_tensor(out=tt[0:rows, 0:sl], in0=p3[0:rows, 0:sl], in1=cs[0:rows, 1, 0:sl], op=ALU.mult),
               reads=[p3b, csb], writes=[tb_])
            op("dve", lambda e: e.tensor_tensor(out=o[0:rows, 0:sl], in0=q[0:rows, 0:sl], in1=tt[0:rows, 0:sl], op=ALU.add),
               reads=[qb, tb_], writes=[ob_])
            dma("pool", dst[ch, 0:rows, t0:t0 + sl], o[0:rows, 0:sl], reads=[ob_])
        return setup, ep

    def attention_stage(heads, Vd, scale, bg=None):
        if skip():
            return
        nkc = T // 128
        with contextlib.ExitStack() as st:
            kt = [[kb.tile(st, [128, T], BF16, "ak") for _ in range(2)] for _ in range(2)]
            vt = [kb.tile(st, [128, nkc, 128], BF16, "av") for _ in range(2)]
            qt = [[kb.tile(st, [128, 512], BF16, "aq") for _ in range(2)] for _ in range(2)]
            pts = [kb.tile(st, [128, 512], BF16, "ap") for _ in range(4)]
            rd = [kb.tile(st, [128, 512], F32, "ard") for _ in range(2)]
            ot = [kb.tile(st, [128, 512], BF16, "ao") for _ in range(2)]
            qblocks = tok_blocks(512)
            item = 0
            pctr = 0
            for hi, hd in enumerate(heads):
                nk = len(hd["k"])
                for j, (kap, rows) in enumerate(hd["k"]):
                    dma("sp", kt[j][hi % 2][0][0:rows, :], kap, writes=[kt[j][hi % 2][1]])
                vtile, vb = vt[hi % 2]
                vc = hd["vcol"]
                dma("sp", vtile[:], Vd[:, vc:vc + 128].rearrange("(c p) d -> p c d", p=128), writes=[vb])
                for (q0, ql) in qblocks:
                    nkeys = LC if q0 < LC else T
                    for j, (qap, rows) in enumerate(hd["q"]):
                        dma("sp", qt[j][item % 2][0][0:rows, 0:ql], qap[:, q0:q0 + ql], writes=[qt[j][item % 2][1]])
                    pso, psob = PS[4 + (item % 2)]
                    psd, psdb = PS[6 + (item % 2)]
                    ncks = nkeys // 128

                    def score(c, slot):
                        pt, pb = PS[slot]
                        pairs = []
                        rds = []
                        for j, (kap, rows) in enumerate(hd["k"]):
                            pairs.append((kt[j][hi % 2][0][0:rows, c * 128:(c + 1) * 128], qt[j][item % 2][0][0:rows, 0:ql]))
                            rds += [kt[j][hi % 2][1], qt[j][item % 2][1]]
                        mmg(pt[:, 0:ql], pairs, reads=rds, writes=[pb])
                    score(0, pctr % 4)
                    for c in range(ncks):
                        if c + 1 < ncks:
                            score(c + 1, (pctr + 1) % 4)
                        pt, pb = PS[pctr % 4]
                        ptile, ptb = pts[pctr % 4]
                        pctr += 1
                        op("act", lambda e: e.activation(out=ptile[:, 0:ql], in_=pt[:, 0:ql], func=AF.Exp, scale=scale),
                           reads=[pb], writes=[ptb])
                        mmg(pso[:, 0:ql], [(vtile[:, c, :], ptile[:, 0:ql])], reads=[vb, ptb], writes=[psob],
                            start=(c == 0), stop=(c == ncks - 1))
                        mmg(psd[:, 0:ql], [(onesb[:, :], ptile[:, 0:ql])], reads=[onesb_b, ptb], writes=[psdb],
                            start=(c == 0), stop=(c == ncks - 1))
                    rt, rb = rd[item % 2]
                    o, ob_ = ot[item % 2]
                    op("dve", lambda e: e.reciprocal(out=rt[:, 0:ql], in_=psd[:, 0:ql]), reads=[psdb], writes=[rb])
                    op("dve", lambda e: e.tensor_tensor(out=o[:, 0:ql], in0=pso[:, 0:ql], in1=rt[:, 0:ql], op=ALU.mult),
                       reads=[psob, rb], writes=[ob_])
                    dma("pool", hd["out"][:, q0:q0 + ql], o[:, 0:ql], reads=[ob_])
                    item += 1
                    if bg is not None:
                        bg(8)
            barrier()

    def s5_prep(li, d, st):
        Wm = {}
        for nm in ("toep", "bin", "bins", "cout"):
            Wm[nm] = kb.tile(st, [128, G, 128], BF16, "s5" + nm)
        ARR, ARb = kb.tile(st, [128, 2, G], F32, "s5ARR")
        AXX, AXb = kb.tile(st, [128, 2, G], F32, "s5AXX")
        AXsb = AXb
        AR = ARR[:, 0, :]
        AX = AXX[:, 0, :]
        AXs = AXX[:, 1, :]
        with contextlib.ExitStack() as s2:
            def tl(shape, name):
                return kb.tile(s2, shape, F32, name)
            lr, lrb = tl([128, G], "lr")
            li_, lib = tl([128, G], "li")
            dt, dtb = tl([128, G], "dt")
            jf, jfb = tl([128, 16], "jf")
            dma("sp", jf[:], k_jf, writes=[jfb])
            load_T(lr[:], lrb, lam_re[li, d], G, dup64=True)
            load_T(li_[:], lib, lam_im[li, d], G, dup64=True)
            dma("sp", dt[:], log_dt[li, d:d + 1, :].broadcast_to([128, G]), writes=[dtb])
            op("dve", lambda e: e.tensor_scalar(out=lr[:], in0=lr[:], scalar1=-1e-4, scalar2=None, op0=ALU.min), reads=[lrb], writes=[lrb])
            op("act", lambda e: e.activation(out=dt[:], in_=dt[:], func=AF.Exp), reads=[dtb], writes=[dtb])
            ld, ldb = tl([128, G], "ld")
            an, anb = tl([128, G], "an")
            op("dve", lambda e: e.tensor_tensor(out=ld[:], in0=lr[:], in1=dt[:], op=ALU.mult), reads=[lrb, dtb], writes=[ldb])
            op("dve", lambda e: e.tensor_tensor(out=an[:], in0=li_[:], in1=dt[:], op=ALU.mult), reads=[lib, dtb], writes=[anb])
            mg, mgb = tl([128, G, 16], "mg")
            ag, agb = tl([128, G, 16], "ag")
            PR, PRb = tl([128, G, 16], "PR")
            PI, PIb = tl([128, G, 16], "PI")
            kk, kkb = tl([128, G, 16], "kk")
            ki = kb.tile(s2, [128, G, 16], mybir.dt.int32, "ki")
            for g in range(G):
                op("dve", lambda e: e.tensor_scalar(out=mg[:, g, :], in0=jf[:], scalar1=ld[:, g:g + 1], scalar2=None, op0=ALU.mult),
                   reads=[jfb, ldb], writes=[mgb])
                op("pool", lambda e: e.tensor_scalar(out=ag[:, g, :], in0=jf[:], scalar1=an[:, g:g + 1], scalar2=None, op0=ALU.mult),
                   reads=[jfb, anb], writes=[agb])
            op("act", lambda e: e.activation(out=mg[:], in_=mg[:], func=AF.Exp), reads=[mgb], writes=[mgb])

            def sin_of(out_t, out_b, shift):
                TWO_PI = 2.0 * math.pi
                op("dve", lambda e: e.tensor_scalar(out=kk[:], in0=ag[:], scalar1=shift, scalar2=1.0 / TWO_PI, op0=ALU.add, op1=ALU.mult),
                   reads=[agb], writes=[kkb])
                op("dve", lambda e: e.tensor_copy(out=ki[0][:], in_=kk[:]), reads=[kkb], writes=[ki[1]])
                op("dve", lambda e: e.tensor_copy(out=kk[:], in_=ki[0][:]), reads=[ki[1]], writes=[kkb])
                op("dve", lambda e: e.scalar_tensor_tensor(out=kk[:], in0=kk[:], scalar=-TWO_PI, in1=ag[:], op0=ALU.mult, op1=ALU.add),
                   reads=[kkb, agb], writes=[kkb])
                op("dve", lambda e: e.tensor_scalar(out=kk[:], in0=kk[:], scalar1=shift, scalar2=None, op0=ALU.add), reads=[kkb], writes=[kkb])
                op("dve", lambda e: e.tensor_scalar(out=out_t[:], in0=kk[:], scalar1=math.pi, scalar2=-TWO_PI, op0=ALU.is_gt, op1=ALU.mult),
                   reads=[kkb], writes=[out_b])
                op("dve", lambda e: e.tensor_tensor(out=kk[:], in0=kk[:], in1=out_t[:], op=ALU.add), reads=[kkb, out_b], writes=[kkb])
                op("dve", lambda e: e.tensor_scalar(out=out_t[:], in0=kk[:], scalar1=-math.pi, scalar2=TWO_PI, op0=ALU.is_lt, op1=ALU.mult),
                   reads=[kkb], writes=[out_b])
                op("dve", lambda e: e.tensor_tensor(out=kk[:], in0=kk[:], in1=out_t[:], op=ALU.add), reads=[kkb, out_b], writes=[kkb])
                op("dve", lambda e: e.tensor_scalar(out=kk[:], in0=kk[:], scalar1=math.pi, scalar2=-math.pi, op0=ALU.min, op1=ALU.max),
                   reads=[kkb], writes=[kkb])
                op("act", lambda e: e.activation(out=out_t[:], in_=kk[:], func=AF.Sin), reads=[kkb], writes=[out_b])
            sin_of(PI, PIb, 0.0)
            sin_of(PR, PRb, math.pi / 2)
            op("dve", lambda e: e.tensor_tensor(out=PR[:], in0=PR[:], in1=mg[:], op=ALU.mult), reads=[PRb, mgb], writes=[PRb])
            op("dve", lambda e: e.tensor_tensor(out=PI[:], in0=PI[:], in1=mg[:], op=ALU.mult), reads=[PIb, mgb], writes=[PIb])

            def P(e_):
                return e_ + 7
            op("dve", lambda e: e.tensor_copy(out=ARR[:, 0, :], in_=PR[:, :, P(8)]), reads=[PRb], writes=[ARb])
            op("dve", lambda e: e.tensor_copy(out=ARR[:, 1, :], in_=PR[:, :, P(8)]), reads=[PRb], writes=[ARb])
            op("dve", lambda e: e.tensor_scalar(out=AXX[0:64, 0, :], in0=PI[0:64, :, P(8)], scalar1=-1.0, scalar2=None, op0=ALU.mult),
               reads=[PIb], writes=[AXb])
            op("dve", lambda e: e.tensor_copy(out=AXX[64:128, 0, :], in_=PI[64:128, :, P(8)]), reads=[PIb], writes=[AXb])
            op("dve", lambda e: e.tensor_scalar(out=AXX[:, 1, :], in0=AXX[:, 0, :], scalar1=-1.0, scalar2=None, op0=ALU.mult), reads=[AXb], writes=[AXb])
            den, denb = tl([128, G], "den")
            t1, t1b = tl([128, G], "t1")
            fr, frb = tl([128, G], "fr")
            fi, fib = tl([128, G], "fi")
            nr, nrb = tl([128, G], "nr")
            op("dve", lambda e: e.tensor_tensor(out=den[:], in0=lr[:], in1=lr[:], op=ALU.mult), reads=[lrb], writes=[denb])
            op("dve", lambda e: e.tensor_tensor(out=t1[:], in0=li_[:], in1=li_[:], op=ALU.mult), reads=[lib], writes=[t1b])
            op("dve", lambda e: e.tensor_tensor(out=den[:], in0=den[:], in1=t1[:], op=ALU.add), reads=[denb, t1b], writes=[denb])
            op("dve", lambda e: e.reciprocal(out=den[:], in_=den[:]), reads=[denb], writes=[denb])
            op("dve", lambda e: e.tensor_scalar(out=nr[:], in0=PR[:, :, P(1)], scalar1=-1.0, scalar2=None, op0=ALU.add), reads=[PRb], writes=[nrb])
            op("dve", lambda e: e.tensor_tensor(out=fr[:], in0=nr[:], in1=lr[:], op=ALU.mult), reads=[nrb, lrb], writes=[frb])
            op("dve", lambda e: e.tensor_tensor(out=t1[:], in0=PI[:, :, P(1)], in1=li_[:], op=ALU.mult), reads=[PIb, lib], writes=[t1b])
            op("dve", lambda e: e.tensor_tensor(out=fr[:], in0=fr[:], in1=t1[:], op=ALU.add), reads=[frb, t1b], writes=[frb])
            op("dve", lambda e: e.tensor_tensor(out=fr[:], in0=fr[:], in1=den[:], op=ALU.mult), reads=[frb, denb], writes=[frb])
            op("dve", lambda e: e.tensor_tensor(out=fi[:], in0=PI[:, :, P(1)], in1=lr[:], op=ALU.mult), reads=[PIb, lrb], writes=[fib])
            op("dve", lambda e: e.tensor_tensor(out=t1[:], in0=nr[:], in1=li_[:], op=ALU.mult), reads=[nrb, lib], writes=[t1b])
            op("dve", lambda e: e.tensor_tensor(out=fi[:], in0=fi[:], in1=t1[:], op=ALU.subtract), reads=[fib, t1b], writes=[fib])
            op("dve", lambda e: e.tensor_tensor(out=fi[:], in0=fi[:], in1=den[:], op=ALU.mult), reads=[fib, denb], writes=[fib])
            br_, brb = tl([128, G, 16], "br")
            bi_, bib = tl([128, G, 16], "bi")
            cr_, crb = tl([128, G, 16], "cr")
            ci_, cib = tl([128, G, 16], "ci")
            for h in range(2):
                for g0 in range(0, G, 8):
                    g1 = min(G, g0 + 8)
                    dma("sp", br_[h * 64:(h + 1) * 64, g0:g1, :], b_re[li, d, g0:g1].rearrange("g p h -> p g h"), writes=[brb])
                    dma("sp", bi_[h * 64:(h + 1) * 64, g0:g1, :], b_im[li, d, g0:g1].rearrange("g p h -> p g h"), writes=[bib])
            for g0 in range(0, G, 8):
                g1 = min(G, g0 + 8)
                nr_ = (g1 - g0) * 16
                load_T(cr_[:, g0:g1, :].rearrange("p g h -> p (g h)"), crb, c_re[li, d, g0:g1].rearrange("g h p -> (g h) p"), nr_, dup64=True)
                load_T(ci_[:, g0:g1, :].rearrange("p g h -> p (g h)"), cib, c_im[li, d, g0:g1].rearrange("g h p -> (g h) p"), nr_, dup64=True)
            bbr, bbrb = kk, kkb
            bbi, bbib = tl([128, G, 16], "bbi")
            t3, t3b = mg, mgb
            for g in range(G):
                e1 = "dve" if g % 2 == 0 else "pool"
                op(e1, lambda e: e.tensor_scalar(out=bbr[:, g, :], in0=br_[:, g, :], scalar1=fr[:, g:g + 1], scalar2=None, op0=ALU.mult),
                   reads=[brb, frb], writes=[bbrb])
                op(e1, lambda e: e.tensor_scalar(out=t3[:, g, :], in0=bi_[:, g, :], scalar1=fi[:, g:g + 1], scalar2=None, op0=ALU.mult),
                   reads=[bib, fib], writes=[t3b])
                op(e1, lambda e: e.tensor_scalar(out=bbi[:, g, :], in0=bi_[:, g, :], scalar1=fr[:, g:g + 1], scalar2=None, op0=ALU.mult),
                   reads=[bib, frb], writes=[bbib])
                op(e1, lambda e: e.tensor_scalar(out=br_[:, g, :], in0=br_[:, g, :], scalar1=fi[:, g:g + 1], scalar2=None, op0=ALU.mult),
                   reads=[brb, fib], writes=[brb])
            op("dve", lambda e: e.tensor_tensor(out=bbr[:], in0=bbr[:], in1=t3[:], op=ALU.subtract), reads=[bbrb, t3b], writes=[bbrb])
            op("dve", lambda e: e.tensor_tensor(out=bbi[:], in0=bbi[:], in1=br_[:], op=ALU.add), reads=[bbib, brb], writes=[bbib])
            if d == 0:
                eX = [-i for i in range(8)]
                eY = [j for j in range(8)]
                eB = [7 - i for i in range(8)]
                eC = [j + 1 for j in range(8)]
                MK = MASKF
            else:
                eX = [i - 7 for i in range(8)]
                eY = [7 - j for j in range(8)]
                eB = [i for i in range(8)]
                eC = [8 - j for j in range(8)]
                MK = MASKR
            GH = max(1, G // 2)
            X, Xb = tl([128, GH, 8, 16], "X")
            Y, Yb = tl([128, GH, 8, 16], "Y")
            t4, t4b = ag, agb

            def cmul_rows(out_t, out_b, i, pw, ur, ui, urb, uib, sign_im_rows, g0):
                prb = PR[:, g0:g0 + GH, pw:pw + 1].broadcast_to([128, GH, 16])
                pib = PI[:, g0:g0 + GH, pw:pw + 1].broadcast_to([128, GH, 16])
                o = out_t[:, :, i, :]
                ur_ = ur[:, g0:g0 + GH, :]
                ui_ = ui[:, g0:g0 + GH, :]
                t4_ = t4[:, 0:GH, :]
                op("dve", lambda e: e.tensor_tensor(out=o[0:64], in0=ur_[0:64], in1=prb[0:64], op=ALU.mult), reads=[urb, PRb], writes=[out_b])
                op("dve", lambda e: e.tensor_tensor(out=t4_[0:64], in0=ui_[0:64], in1=pib[0:64], op=ALU.mult), reads=[uib, PIb], writes=[t4b])
                op("dve", lambda e: e.tensor_tensor(out=o[0:64], in0=o[0:64], in1=t4_[0:64], op=ALU.subtract), reads=[out_b, t4b], writes=[out_b])
                op("dve", lambda e: e.tensor_tensor(out=o[64:128], in0=ui_[64:128], in1=prb[64:128], op=ALU.mult), reads=[uib, PRb], writes=[out_b])
                op("dve", lambda e: e.tensor_tensor(out=t4_[64:128], in0=ur_[64:128], in1=pib[64:128], op=ALU.mult), reads=[urb, PIb], writes=[t4b])
                op("dve", lambda e: e.tensor_tensor(out=o[64:128], in0=o[64:128], in1=t4_[64:128], op=ALU.add), reads=[out_b, t4b], writes=[out_b])
                if sign_im_rows < 0:
                    op("dve", lambda e: e.tensor_scalar(out=o[64:128], in0=o[64:128], scalar1=-1.0, scalar2=None, op0=ALU.mult),
                       reads=[out_b], writes=[out_b])
            for g0 in range(0, G, GH):
                for i in range(8):
                    cmul_rows(X, Xb, i, P(eX[i]), bbr, bbi, bbrb, bbib, +1, g0)
                    cmul_rows(Y, Yb, i, P(eY[i]), cr_, ci_, crb, cib, -1, g0)
                for gg in range(GH):
                    g = g0 + gg
                    xg = X[:, gg].rearrange("p i h -> p (i h)")
                    yg = Y[:, gg].rearrange("p i h -> p (i h)")
                    p1, p1b = next_ps()
                    mmg(p1[:, 0:128], [(xg, yg)], reads=[Xb, Yb], writes=[p1b])
                    op("dve", lambda e: e.tensor_tensor(out=Wm["toep"][0][:, g, :], in0=p1[:, 0:128], in1=mats[:, MK, :], op=ALU.mult),
                       reads=[p1b, mats_b], writes=[Wm["toep"][1]])
                for i in range(8):
                    cmul_rows(X, Xb, i, P(eB[i]), bbr, bbi, bbrb, bbib, +1, g0)
                    cmul_rows(Y, Yb, i, P(eC[i]), cr_, ci_, crb, cib, -1, g0)
                op("act", lambda e: e.activation(out=Wm["cout"][0][:, g0:g0 + GH, :], in_=Y[:].rearrange("p g i h -> p g (i h)"), func=AF.Copy),
                   reads=[Yb], writes=[Wm["cout"][1]])
                for gg in range(GH):
                    g = g0 + gg
                    xbg = X[:, gg].rearrange("p i h -> p (i h)")
                    p2, p2b = next_ps()
                    mmg(p2[:, 0:128], [(xbg, mats[:, IDENT, :])], reads=[Xb, mats_b], writes=[p2b])
                    op("act", lambda e: e.activation(out=Wm["bin"][0][:, g, :], in_=p2[:, 0:128], func=AF.Copy), reads=[p2b], writes=[Wm["bin"][1]])
                    p3, p3b = next_ps()
                    mmg(p3[:, 0:128], [(xbg, mats[:, SWAPN, :])], reads=[Xb, mats_b], writes=[p3b])
                    op("act", lambda e: e.activation(out=Wm["bins"][0][:, g, :], in_=p3[:, 0:128], func=AF.Copy), reads=[p3b], writes=[Wm["bins"][1]])
            barrier()
        return Wm, (ARR, ARb), (AXX, AXb), (None, None)

    def vv_dma(q, tile_ap, buf, dview, n, load):
        if n == S5RL:
            flat = tile_ap.rearrange("p s g c -> p (s g c)")
            if load:
                dma(q, flat, dview, writes=[buf])
            else:
                dma(q, dview, flat, reads=[buf])
            return
        dv = dview.rearrange("p (s g c) -> p s g c", s=2, g=G)
        for sl_ in range(2):
            if load:
                dma(q, tile_ap[:, sl_, :, 0:n], dv[:, sl_, :, 0:n], writes=[buf])
            else:
                dma(q, dv[:, sl_, :, 0:n], tile_ap[:, sl_, :, 0:n], reads=[buf])

    def s5_runs(d):
        RL = S5RL
        if d == 0:
            return [(c0, min(NC8, c0 + RL)) for c0 in range(0, NC8, RL)]
        cc = LC // 8
        runs = [(c0, min(cc, c0 + RL)) for c0 in reversed(range(0, cc, RL))]
        c1 = NC8
        while c1 > cc:
            c0 = max(cc, c1 - RL)
            runs.append((c0, c1))
            c1 = c0
        return runs

    def s5_phaseA(li):
        RL = S5RL
        GB = max(1, 512 // RL)
        for d in range(2):
            if skip():
                continue
            with contextlib.ExitStack() as st:
                Wm, (ARR, ARb), (AXX, AXb), _unused = s5_prep(li, d, st)
                op("dve", lambda e: e.tensor_copy(out=s5tab[:, d, 0], in_=ARR[:]), reads=[ARb], writes=[s5tab_b])
                op("dve", lambda e: e.tensor_copy(out=s5tab[:, d, 1], in_=AXX[:]), reads=[AXb], writes=[s5tab_b])
                dma("act", wmD[d, 0], Wm["toep"][0][:].rearrange("p g m -> p (g m)"), reads=[Wm["toep"][1]])
                dma("act", wmD[d, 1], Wm["cout"][0][:].rearrange("p g m -> p (g m)"), reads=[Wm["cout"][1]])
                runs = s5_runs(d)
                U8 = [kb.tile(st, [128, G, RL], BF16, "U8") for _ in range(2)]
                VVs = [kb.tile(st, [128, 2, G, RL], F32, "VV") for _ in range(2)]
                for ri, (c0, c1) in enumerate(runs):
                    n = c1 - c0
                    u8t, u8b = U8[ri % 2]
                    VV, Vb = VVs[ri % 2]
                    dma("sp", u8t[:, :, 0:n], u8D[:, :, c0:c1].rearrange("g p c -> p g c"), writes=[u8b])
                    for g0 in range(0, G, GB):
                        g1 = min(G, g0 + GB)
                        for slot, nm in ((0, "bin"), (1, "bins")):
                            pt, pb = next_ps()
                            for g in range(g0, g1):
                                mmg(pt[:, (g - g0) * RL:(g - g0) * RL + n], [(Wm[nm][0][:, g, :], u8t[:, g, 0:n])],
                                    reads=[Wm[nm][1], u8b], writes=[pb])
                            pv = pt[:, 0:(g1 - g0) * RL].rearrange("p (g c) -> p g c", c=RL)
                            if slot == 0:
                                op("act", lambda e: e.activation(out=VV[:, slot, g0:g1, 0:n], in_=pv[:, :, 0:n], func=AF.Copy), reads=[pb], writes=[Vb])
                            else:
                                op("dve", lambda e: e.tensor_copy(out=VV[:, slot, g0:g1, 0:n], in_=pv[:, :, 0:n]), reads=[pb], writes=[Vb])
                    vv_dma("act", VV[:], Vb, vvD[d, ri], n, False)
                barrier()

    def s5_recur_gen(li, st):
        RL = S5RL
        VVs = [kb.tile(st, [128, 2, G, RL], F32, "rVV") for _ in range(2)]
        SAs = [kb.tile(st, [128, G, RL], BF16, "rSA") for _ in range(2)]
        X = [kb.tile(st, [128, 2, G], F32, "rX") for _ in range(2)]
        ta, tab = kb.tile(st, [128, 2, G], F32, "rta")
        tb2, tb2b = kb.tile(st, [128, 2, G], F32, "rtb")

        def _gen():
          for d in range(2):
              runs = s5_runs(d)
              ARR = s5tab[:, d, 0]
              AXX = s5tab[:, d, 1]
              cur = 0
              op("dve", lambda e: e.memset(X[0][0][:], 0.0), writes=[X[0][1]])
              n0 = runs[0][1] - runs[0][0]
              vv_dma("pool", VVs[0][0][:], VVs[0][1], vvD[d, 0], n0, True)
              for ri, (c0, c1) in enumerate(runs):
                  n = c1 - c0
                  VV, Vb = VVs[ri % 2]
                  SA, SAb = SAs[ri % 2]
                  if ri + 1 < len(runs):
                      n1 = runs[ri + 1][1] - runs[ri + 1][0]
                      vv_dma("pool", VVs[(ri + 1) % 2][0][:], VVs[(ri + 1) % 2][1], vvD[d, ri + 1], n1, True)
                  order = range(n) if d == 0 else range(n - 1, -1, -1)
                  for c in order:
                      x_t, x_b = X[cur]
                      n_t, n_b = X[1 - cur]
                      op("pool", lambda e: e.tensor_copy(out=SA[:, :, c], in_=x_t[:, 0, :]), reads=[x_b], writes=[SAb])
                      op("dve", lambda e: e.tensor_tensor(out=ta[:], in0=ARR, in1=x_t[:], op=ALU.mult), reads=[s5tab_b, x_b], writes=[tab])
                      op("dve", lambda e: e.tensor_tensor(out=tb2[:], in0=AXX, in1=x_t[:, ::-1, :], op=ALU.mult), reads=[s5tab_b, x_b], writes=[tb2b])
                      op("dve", lambda e: e.tensor_tensor(out=ta[:], in0=ta[:], in1=tb2[:], op=ALU.add), reads=[tab, tb2b], writes=[tab])
                      op("dve", lambda e: e.tensor_tensor(out=n_t[:], in0=ta[:], in1=VV[:, :, :, c], op=ALU.add), reads=[tab, Vb], writes=[n_b])
                      cur = 1 - cur
                      yield
                  dma("pool", saD[d, ri].rearrange("p (g c) -> p g c", g=G)[:, :, 0:n], SA[:, :, 0:n], reads=[SAb])
              if cur == 1:
                  pass
        return _gen()

    def s5_phaseC(li):
        RL = S5RL
        GB = max(1, 512 // RL)
        for d in range(2):
            if skip():
                continue
            with contextlib.ExitStack() as st:
                TO, TOb = kb.tile(st, [128, G, 128], BF16, "cTO")
                CO, COb = kb.tile(st, [128, G, 128], BF16, "cCO")
                dma("sp", TO[:].rearrange("p g m -> p (g m)"), wmD[d, 0], writes=[TOb])
                dma("sp", CO[:].rearrange("p g m -> p (g m)"), wmD[d, 1], writes=[COb])
                runs = s5_runs(d)
                U8 = [kb.tile(st, [128, G, RL], BF16, "cU8") for _ in range(2)]
                SAs = [kb.tile(st, [128, G, RL], BF16, "cSA") for _ in range(2)]
                YO = [kb.tile(st, [128, G, RL], F32, "cYO") for _ in range(2)]
                for ri, (c0, c1) in enumerate(runs):
                    n = c1 - c0
                    u8t, u8b = U8[ri % 2]
                    SA, SAb = SAs[ri % 2]
                    yt, yb = YO[ri % 2]
                    dma("sp", u8t[:, :, 0:n], u8D[:, :, c0:c1].rearrange("g p c -> p g c"), writes=[u8b])
                    dma("sp", SA[:, :, 0:n], saD[d, ri].rearrange("p (g c) -> p g c", g=G)[:, :, 0:n], writes=[SAb])
                    for gi, g0 in enumerate(range(0, G, GB)):
                        g1 = min(G, g0 + GB)
                        pt, pb = next_ps()
                        for g in range(g0, g1):
                            mmg(pt[:, (g - g0) * RL:(g - g0) * RL + n],
                                [(TO[:, g, :], u8t[:, g, 0:n]), (CO[:, g, :], SA[:, g, 0:n])],
                                reads=[TOb, COb, u8b, SAb], writes=[pb])
                        pv = pt[:, 0:(g1 - g0) * RL].rearrange("p (g c) -> p g c", c=RL)
                        if gi % 2 == 0:
                            op("act", lambda e: e.activation(out=yt[:, g0:g1, 0:n], in_=pv[:, :, 0:n], func=AF.Copy), reads=[pb], writes=[yb])
                        else:
                            op("dve", lambda e: e.tensor_copy(out=yt[:, g0:g1, 0:n], in_=pv[:, :, 0:n]), reads=[pb], writes=[yb])
                    dma("act", y8D[d, :, :, c0:c1].rearrange("g p c -> p g c"), yt[:, :, 0:n], reads=[yb])
                barrier()

    def s5_stage(li):
        def s5_combine():
            if skip():
                return
            with contextlib.ExitStack() as st:
                dsk, dskb = kb.tile(st, [128, SW // 128], F32, "dsk")
                load_T(dsk[:], dskb, s5_d[li].rearrange("(k g) h -> k (g h)", g=8), SW // 128)
                NB_ = 4
                YA = [kb.tile(st, [128, 8, NC8], F32, "YA") for _ in range(2)]
                YB = [kb.tile(st, [128, 8, NC8], F32, "YB") for _ in range(2)]
                ut = [kb.tile(st, [128, 512], F32, "ut") for _ in range(NB_)]
                yt_ = [kb.tile(st, [128, 512], F32, "yt") for _ in range(NB_)]
                t5 = [kb.tile(st, [128, 512], F32, "t5") for _ in range(NB_)]
                go = [kb.tile(st, [128, 512], BF16, "go") for _ in range(NB_)]
                it = 0
                for k in range(SW // 128):
                    A_t, a_b = YA[k % 2]
                    B_t, b_b = YB[k % 2]
                    for g8 in range(8):
                        dma("sp" if g8 % 2 == 0 else "act", A_t[g8 * 16:(g8 + 1) * 16, :, :], y8D[0, k * 8 + g8, :, :].rearrange("(j h) c -> h j c", h=16), writes=[a_b])
                        dma("act" if g8 % 2 == 0 else "sp", B_t[g8 * 16:(g8 + 1) * 16, :, :], y8D[1, k * 8 + g8, :, :].rearrange("(j h) c -> h j c", h=16), writes=[b_b])
                    for (t0, tl) in tok_blocks(512):
                        i = it % NB_
                        it += 1
                        cA, nA = t0 // 8, tl // 8
                        a_t = A_t[:, :, cA:cA + nA]
                        b_t = B_t[:, :, cA:cA + nA]
                        u_t, u_b = ut[i]
                        y_t, y_b = yt_[i]
                        w_t, w_b = t5[i]
                        g_t, g_b = go[i]
                        dma("sp", u_t[:, 0:tl], uTD[k, :, t0:t0 + tl], writes=[u_b])
                        yv = y_t[:, 0:tl].rearrange("p (c j) -> p j c", j=8)
                        op("dve", lambda e: e.tensor_tensor(out=yv, in0=a_t, in1=b_t, op=ALU.add), reads=[a_b, b_b], writes=[y_b])
                        op("dve", lambda e: e.scalar_tensor_tensor(out=y_t[:, 0:tl], in0=u_t[:, 0:tl], scalar=dsk[:, k:k + 1], in1=y_t[:, 0:tl],
                                                                   op0=ALU.mult, op1=ALU.add), reads=[u_b, y_b, dskb], writes=[y_b])
                        dma("act", ytotT[k, :, t0:t0 + tl], y_t[:, 0:tl], reads=[y_b])
                        op("pool", lambda e: e.tensor_tensor(out=w_t[:, 0:tl], in0=y_t[:, 0:tl], in1=y_t[:, 0:tl], op=ALU.mult), reads=[y_b], writes=[w_b])
                        op("dve", lambda e: e.tensor_scalar(out=w_t[:, 0:tl], in0=w_t[:, 0:tl], scalar1=0.044715, scalar2=1.0, op0=ALU.mult, op1=ALU.add),
                           reads=[w_b], writes=[w_b])
                        op("dve", lambda e: e.tensor_tensor(out=w_t[:, 0:tl], in0=w_t[:, 0:tl], in1=y_t[:, 0:tl], op=ALU.mult), reads=[w_b, y_b], writes=[w_b])
                        op("act", lambda e: e.activation(out=w_t[:, 0:tl], in_=w_t[:, 0:tl], func=AF.Sigmoid, scale=1.5957691216057308), reads=[w_b], writes=[w_b])
                        op("dve", lambda e: e.tensor_tensor(out=g_t[:, 0:tl], in0=w_t[:, 0:tl], in1=y_t[:, 0:tl], op=ALU.mult), reads=[w_b, y_b], writes=[g_b])
                        dma("act", geluT[k, :, t0:t0 + tl], g_t[:, 0:tl], reads=[g_b])
                barrier()

        s5_combine()
        def setup(st):
            bg, bgb = kb.tile(st, [128, SW // 128], F32, "bglu")
            load_T(bg[:], bgb, b_glu[li:li + 1, :].rearrange("o (k p) -> (o k) p", p=128), SW // 128)
            return dict(bg=(bg, bgb), y=[kb.tile(st, [128, 512], F32, "gy") for _ in range(3)],
                        s=[kb.tile(st, [128, 512], F32, "gs") for _ in range(3)],
                        o=[kb.tile(st, [128, 512], BF16, "gout") for _ in range(3)], ctr=[0])

        def ep(ri, ci, pss, t0, sl, env):
            i = env["ctr"][0] % 3
            env["ctr"][0] += 1
            (pt, pb), = pss
            y_t, y_b = env["y"][i]
            s_t, s_b = env["s"][i]
            o_t, o_b = env["o"][i]
            bg, bgb = env["bg"]
            dma("sp", y_t[:, 0:sl], ytotT[ci, :, t0:t0 + sl], writes=[y_b])
            op("act", lambda e: e.activation(out=s_t[:, 0:sl], in_=pt[:, 0:sl], func=AF.Sigmoid, bias=bg[:, ci:ci + 1]), reads=[pb, bgb], writes=[s_b])
            op("dve", lambda e: e.tensor_tensor(out=o_t[:, 0:sl], in0=s_t[:, 0:sl], in1=y_t[:, 0:sl], op=ALU.mult), reads=[s_b, y_b], writes=[o_b])
            dma("act", brT[1][ci, :, t0:t0 + sl], o_t[:, 0:sl], reads=[o_b])
        linear([geluT], [[(0, w_glu[li], 0)]], [(i * 128, 128) for i in range(SW // 128)], ep, tb=1024, setup=setup)

    def mixer_stage(li):
        norm_stage(xT, hT, D, gain_fn(1), shift_fn(1))
        Wi = w_in[li]
        def su_u(st):
            return dict(f=[kb.tile(st, [128, 512], F32, "uf") for _ in range(2)],
                        b=[kb.tile(st, [128, 8, 64], BF16, "ub") for _ in range(2)], ctr=[0])

        def ep_u(ri, ci, pss, t0, sl, env):
            i = env["ctr"][0] % 2
            env["ctr"][0] += 1
            (pt, pb), = pss
            f_t, f_b = env["f"][i]
            b_t, b_b = env["b"][i]
            nA = sl // 8
            op("act", lambda e: e.activation(out=f_t[:, 0:sl], in_=pt[:, 0:sl], func=AF.Copy), reads=[pb], writes=[f_b])
            dma("act", uTD[ci, :, t0:t0 + sl], f_t[:, 0:sl], reads=[f_b])
            op("dve", lambda e: e.tensor_copy(out=b_t[:, :, 0:nA], in_=pt[:, 0:sl].rearrange("p (c i) -> p i c", i=8)), reads=[pb], writes=[b_b])
            for g8 in range(8):
                dma("act", u8D[ci * 8 + g8, :, t0 // 8:t0 // 8 + nA].rearrange("(i h) c -> h i c", h=16), b_t[g8 * 16:(g8 + 1) * 16, :, 0:nA], reads=[b_b])
        su_cq, ep_cq = ep_store(cqT, F32)
        su_ckv, ep_ckv = ep_store(ckvT, F32)
        su_r, ep_kr = ep_rope(krT, 64, None)
        _, ep_gq = ep_rope(gqT, 128, 0)
        _, ep_gk = ep_rope(gkT, 128, 1)
        segs = [(OFF_CQ, QL // 128, 128, ep_cq, "cq"), (OFF_CKV, KVL // 128, 128, ep_ckv, "ckv"), (OFF_KR, 1, 64, ep_kr, "r"),
                (OFF_U, SW // 128, 128, ep_u, "u"), (OFF_GQ, HQ, 128, ep_gq, "r"), (OFF_GK, HKV, 128, ep_gk, "r")]
        chunks_all = []
        seg_of = []
        for si_, (o0, nch, m, _e, _k) in enumerate(segs):
            for i in range(nch):
                chunks_all.append((o0 + i * 128, m))
                seg_of.append((si_, i))

        def su_all(st):
            return dict(cq=su_cq(st), ckv=su_ckv(st), r=su_r(st), u=su_u(st))

        def ep_all(ri, ci, pss, t0, sl, env):
            si_, i = seg_of[ci]
            segs[si_][3](ri, i, pss, t0, sl, env[segs[si_][4]])
        linear([hT], [[(0, Wi, 0)]], chunks_all, ep_all, setup=su_all)
        def su_tm(st):
            return ([kb.tile(st, [128, 512], BF16, "tmo") for _ in range(3)], [0])

        def ep_gv(pt, pb, tok0, col0, w, env):
            tiles, ctr = env
            o_t, o_b = tiles[ctr[0] % 3]
            ctr[0] += 1
            op("act", lambda e: e.activation(out=o_t[:, 0:w], in_=pt[:, 0:w], func=AF.Copy), reads=[pb], writes=[o_b])
            dma("act", gvD[tok0:tok0 + 128, col0:col0 + w], o_t[:, 0:w], reads=[o_b])
        linear_tm(hT, Wi, OFF_GV, HKV * 128, ep_gv, setup=su_tm)
        gcq_t = {}

        def ld_gain(src_ap, n, key):
            def f(st_unused=None):
                pass
            return f
        with contextlib.ExitStack() as st:
            gq_t, gq_b = kb.tile(st, [128, QL // 128], F32, "gcq")
            gkv_t, gkv_b = kb.tile(st, [128, KVL // 128], F32, "gckv")
            load_T(gq_t[:], gq_b, g_cq[li:li + 1, :].rearrange("o (k p) -> (o k) p", p=128), QL // 128)
            load_T(gkv_t[:], gkv_b, g_ckv[li:li + 1, :].rearrange("o (k p) -> (o k) p", p=128), KVL // 128)
            barrier()
            norm_stage(cqT, cqnT, QL, lambda t0: (gq_t[:], gq_b), None)
            norm_stage(ckvT, ckvnT, KVL, lambda t0: (gkv_t[:], gkv_b), None)
        Wq = w_uq[li]
        su_n, ep_n = ep_store(qmnT, BF16)
        su_rr, ep_rr = ep_rope(qmrT, 64, None)
        chunks_q = []
        for h in range(HM):
            chunks_q += [(h * 192, 128), (h * 192 + 128, 64)]

        def su_q(st):
            return dict(n=su_n(st), r=su_rr(st))

        def ep_q(ri, ci, pss, t0, sl, env):
            if ci % 2 == 0:
                ep_n(ri, ci // 2, pss, t0, sl, env["n"])
            else:
                ep_rr(ri, ci // 2, pss, t0, sl, env["r"])
        linear([cqnT], [[(0, Wq, 0)]], chunks_q, ep_q, setup=su_q, wgw=384)
        Wkv = w_ukv[li]
        su, ep = ep_store(kmT, BF16)
        linear([ckvnT], [[(0, Wkv, 0)]], [(h * 256, 128) for h in range(HM)], ep, setup=su, wgw=128)

        def ep_vm(pt, pb, tok0, col0, w, env):
            tiles, ctr = env
            o_t, o_b = tiles[ctr[0] % 3]
            ctr[0] += 1
            op("act", lambda e: e.activation(out=o_t[:, 0:w], in_=pt[:, 0:w], func=AF.Copy), reads=[pb], writes=[o_b])
            for hh in range(w // 256):
                h = (col0 + hh * 256) // 256
                dma("act", vmD[tok0:tok0 + 128, h * 128:(h + 1) * 128], o_t[:, hh * 256 + 128:hh * 256 + 256], reads=[o_b])
        linear_tm(ckvnT, Wkv, 0, HM * 256, ep_vm, setup=su_tm)
        s5_phaseA(li)
        with contextlib.ExitStack() as bst:
            gen = s5_recur_gen(li, bst) if not skip() else iter(())
            done = [False]

            def bg(nsteps):
                if done[0]:
                    return
                for _ in range(nsteps):
                    try:
                        next(gen)
                    except StopIteration:
                        done[0] = True
                        return
            heads = [dict(q=[(qmnT[h], 128), (qmrT[h], 64)], k=[(kmT[h], 128), (krT[0], 64)], vcol=h * 128, out=brT[0][h]) for h in range(HM)]
            attention_stage(heads, vmD, 192 ** -0.5, bg=bg)
            grp = HQ // HKV
            heads = [dict(q=[(gqT[h], 128)], k=[(gkT[h // grp], 128)], vcol=(h // grp) * 128, out=brT[2][h]) for h in range(HQ)]
            attention_stage(heads, gvD, 128 ** -0.5, bg=bg)
            while not done[0]:
                bg(64)
            barrier()
        s5_phaseC(li)
        s5_stage(li)
        def su_m(st):
            bgt, bgb = kb.tile(st, [128, 3, KC], F32, "bgate")
            load_T(bgt[:].rearrange("p n k -> p (n k)"), bgb, b_gate[li].rearrange("n (k p) -> (n k) p", p=128), 3 * KC)
            return dict(bg=(bgt, bgb), s=[kb.tile(st, [128, 512], F32, "ms") for _ in range(2)],
                        acc=[kb.tile(st, [128, 512], F32, "macc") for _ in range(2)],
                        o=[kb.tile(st, [128, 512], BF16, "mo") for _ in range(2)], ctr=[0])

        def ep_m(ri, ci, pss, t0, sl, env):
            if ri == 0:
                env["ctr"][0] += 1
            i = env["ctr"][0] % 2
            (pg, pgb), (pbr, pbrb) = pss
            s_t, s_b = env["s"][i]
            a_t, a_b = env["acc"][i]
            o_t, o_b = env["o"][i]
            bgt, bgb = env["bg"]
            op("act", lambda e: e.activation(out=s_t[:, 0:sl], in_=pg[:, 0:sl], func=AF.Sigmoid, bias=bgt[:, ri, ci:ci + 1]), reads=[pgb, bgb], writes=[s_b])
            if ri == 0:
                op("dve", lambda e: e.tensor_tensor(out=a_t[:, 0:sl], in0=s_t[:, 0:sl], in1=pbr[:, 0:sl], op=ALU.mult), reads=[s_b, pbrb], writes=[a_b])
            else:
                op("dve", lambda e: e.tensor_tensor(out=s_t[:, 0:sl], in0=s_t[:, 0:sl], in1=pbr[:, 0:sl], op=ALU.mult), reads=[s_b, pbrb], writes=[s_b])
                if ri == 1:
                    op("dve", lambda e: e.tensor_tensor(out=a_t[:, 0:sl], in0=a_t[:, 0:sl], in1=s_t[:, 0:sl], op=ALU.add), reads=[s_b, a_b], writes=[a_b])
                else:
                    op("dve", lambda e: e.tensor_tensor(out=o_t[:, 0:sl], in0=a_t[:, 0:sl], in1=s_t[:, 0:sl], op=ALU.add), reads=[s_b, a_b], writes=[o_b])
                    dma("act", ygT[ci, :, t0:t0 + sl], o_t[:, 0:sl], reads=[o_b])
        rounds = [[(0, w_gate[li, n], 0), (1 + n, w_branch[li, n], 0)] for n in range(3)]
        linear([hT, brT[0], brT[1], brT[2]], rounds, [(i * 128, 128) for i in range(KC)], ep_m, tb=1024, wgw=256, setup=su_m)
        su, epr = ep_residual(1)
        linear([ygT], [[(0, w_o[li], 0)]], [(i * 128, 128) for i in range(KC)], epr, tb=1024, setup=su)

    def final_stage():
        if skip():
            return
        with contextlib.ExitStack() as st:
            fg, fgb = kb.tile(st, [128, KC], F32, "fg")
            load_T(fg[:], fgb, final_g.rearrange("o (k p) -> (o k) p", p=128), KC)
            xs = [kb.tile(st, [128, KC, 512], F32, "fx") for _ in range(2)]
            sq = [kb.tile(st, [128, 512], F32, "fsq") for _ in range(3)]
            rs = [kb.tile(st, [128, 512], F32, "frs") for _ in range(2)]
            yn = [kb.tile(st, [128, 512], F32, "fyn") for _ in range(3)]
            ob = [kb.tile(st, [128, D], F32, "fob") for _ in range(2)]
            oi = 0
            for bi, t0 in enumerate(range(LC, T, 512)):
                tl = 512
                xt, xb = xs[bi % 2]
                dma("sp", xt[:], xT[:, :, t0:t0 + tl].rearrange("k p t -> p k t"), writes=[xb])
                pt, pb = next_ps()
                for k in range(KC):
                    sqt, sqb = sq[k % 3]
                    op("act", lambda e: e.activation(out=sqt[:], in_=xt[:, k, :], func=AF.Square), reads=[xb], writes=[sqb])
                    mmg(pt[:, 0:tl], [(mats[:, ONESF, :], sqt[:])], reads=[sqb, mats_b], writes=[pb], start=(k == 0), stop=(k == KC - 1))
                rt, rb = rs[bi % 2]
                op("act", lambda e: e.activation(out=rt[:], in_=pt[:, 0:tl], func=AF.Sqrt, bias=epsT[:, 0:1], scale=1.0 / D), reads=[pb, epsT_b], writes=[rb])
                op("dve", lambda e: e.reciprocal(out=rt[:], in_=rt[:]), reads=[rb], writes=[rb])
                for q in range(4):
                    ot, obf = ob[oi % 2]
                    oi += 1
                    for k in range(KC):
                        yt, yb = yn[k % 3]
                        op("dve", lambda e: e.scalar_tensor_tensor(out=yt[:, 0:128], in0=xt[:, k, q * 128:(q + 1) * 128], scalar=fg[:, k:k + 1],
                                                                   in1=rt[:, q * 128:(q + 1) * 128], op0=ALU.mult, op1=ALU.mult),
                           reads=[xb, rb, fgb], writes=[yb])
                        p2, p2b = next_ps()
                        op("pe", lambda e: e.transpose(p2[:, 0:128], yt[:, 0:128], mats[:, IDENT, :]), reads=[yb, mats_b], writes=[p2b])
                        op("act", lambda e: e.activation(out=ot[:, k * 128:(k + 1) * 128], in_=p2[:, 0:128], func=AF.Copy), reads=[p2b], writes=[obf])
                    r0 = t0 - LC + q * 128
                    dma("act", y_out[r0:r0 + 128, :], ot[:], reads=[obf])
            barrier()

    input_stage()
    for li in range(DEPTH):
        mod_stage(li)
        ffn_stage(li, 0, 0)
        mixer_stage(li)
        ffn_stage(li, 1, 2)
    final_stage()
    barrier()
    es.close()
    return nc


def rope_tables(L, GW, d_rot, LC):
    half = d_rot // 2
    freqs = (10000.0 ** (-np.arange(0, half, 2, dtype=np.float32) / np.float32(half))).astype(np.float32)
    t = np.arange(L)
    rows = (t // GW).astype(np.float32)
    cols = (t % GW).astype(np.float32)

    def tab(pos):
        ang = pos[:, None] * freqs[None, :]
        ang = np.concatenate([ang, ang], axis=-1)
        return np.cos(ang), np.sin(ang)
    cr, sr = tab(rows)
    cc, sc = tab(cols)
    cos = np.concatenate([cr, cc], axis=-1).astype(np.float32)
    sin = np.concatenate([sr, sc], axis=-1).astype(np.float32)
    cosT = np.concatenate([np.ones((d_rot, LC), np.float32), cos.T], axis=1)
    sinT = np.concatenate([np.zeros((d_rot, LC), np.float32), sin.T], axis=1)
    return np.ascontiguousarray(cosT), np.ascontiguousarray(sinT)


def rot_matrix(d_rot):
    half = d_rot // 2
    q = half // 2
    R = np.zeros((128, 128), np.float32)
    for b in (0, half):
        for j in range(q):
            R[b + q + j, b + j] = -1.0
            R[b + j, b + q + j] = 1.0
    return R


def host_constants(cfg):
    L, LC, GW = cfg["L"], cfg["LC"], cfg["GW"]
    c64, s64 = rope_tables(L, GW, 64, LC)
    c128, s128 = rope_tables(L, GW, 128, LC)
    mats = np.zeros((7, 128, 128), np.float32)
    mats[0] = np.eye(128, dtype=np.float32)
    mats[1] = rot_matrix(128)
    mats[2] = rot_matrix(64)
    sw = np.zeros((128, 128), np.float32)
    for p in range(64):
        sw[64 + p, p] = 1.0
        sw[p, 64 + p] = 1.0
    mats[3] = sw
    mf = np.zeros((128, 128), np.float32)
    mr = np.zeros((128, 128), np.float32)
    for i in range(8):
        for j in range(8):
            if j >= i:
                mf[i * 16:(i + 1) * 16, j * 16:(j + 1) * 16] = 1.0
            if j <= i:
                mr[i * 16:(i + 1) * 16, j * 16:(j + 1) * 16] = 1.0
    mats[4] = mf
    mats[5] = mr
    mats[6] = 1.0
    jf = np.tile(np.arange(-7, 9, dtype=np.float32)[None, :], (128, 1))
    return dict(k_cos64=c64, k_sin64=s64, k_cos128=c128, k_sin128=s128, k_mats=mats, k_jf=np.ascontiguousarray(jf))


_PROG_CACHE = {}


def run(cfg, inputs, ncores=NCORES, raw=False):
    key = tuple(sorted((k, str(v)) for k, v in cfg.items()))
    if key not in _PROG_CACHE:
        _PROG_CACHE[key] = build_program(cfg)
    nc = _PROG_CACHE[key]
    consts = host_constants(cfg)
    f32 = lambda a: np.ascontiguousarray(np.asarray(a, dtype=np.float32))
    shared = {k: f32(v) for k, v in inputs.items() if k not in ("x", "c", "ctx", "c_ctx", "final_g")}
    shared["c_ctx"] = f32(inputs["c_ctx"]).reshape(1, -1)
    shared["final_g"] = f32(inputs["final_g"]).reshape(1, -1)
    shared.update(consts)
    x = f32(inputs["x"])
    c = f32(inputs["c"])
    ctx = f32(inputs["ctx"])
    in_maps = []
    for b in range(ncores):
        m = dict(shared)
        m["x"] = x[b]
        m["c"] = c[b:b + 1]
        m["ctx"] = ctx[b]
        in_maps.append(m)
    res = run_bass_kernel_spmd(nc, in_maps, core_ids=list(range(ncores)))
    if raw:
        return res.results
    return np.stack([np.asarray(r["y"]) for r in res.results], axis=0)


def kernel(**inputs):
    return run(FULL, inputs)
```
